# Optimizing a Trainium2 kernel written in Bass

```python
import math
import jax
import jax.numpy as jnp
from jax import lax
import numpy as np

D_MODEL = 1024
BATCH = 8
SEQ = 4096
DEPTH = 2
DEC_BATCH = 32
DEC_SEQ = 64
PAST_LEN = 4096

CHUNK = 64
Q_BLOCK = 128
ROPE_THETA = 10000.0
EPS = 1e-6
MACARON_W = 0.5

MLA_HEADS = 16
MLA_NOPE = 64
MLA_ROPE = 32
MLA_V = 64
MLA_KV_RANK = 256
MLA_Q_RANK = 512
MLA_SCALE = (MLA_NOPE + MLA_ROPE) ** -0.5

DIFF_HEADS = 8
DIFF_HEAD_DIM = 64
DIFF_V_DIM = 2 * DIFF_HEAD_DIM
DIFF_SCALE = DIFF_HEAD_DIM ** -0.5

CONV_CH = D_MODEL
CONV_WIDTH = 31
CONV_STATE = CONV_WIDTH - 1

D_FF = 2816
N_BRANCH = 3
N_ADA = 9

IN_COLS = (MLA_Q_RANK, MLA_KV_RANK, MLA_ROPE,
           DIFF_HEADS * 2 * DIFF_HEAD_DIM, DIFF_HEADS * 2 * DIFF_HEAD_DIM, DIFF_HEADS * DIFF_V_DIM,
           2 * CONV_CH)
IN_SPLITS = tuple(int(s) for s in np.cumsum(IN_COLS)[:-1])
D_IN = int(sum(IN_COLS))

kernel_name = 'hybrid_chunk_causal_encoder_step'


def rmsnorm(x, g):
    xf = x.astype(jnp.float32)
    y = xf * lax.rsqrt(jnp.mean(xf * xf, axis=-1, keepdims=True) + EPS)
    return (y * g.astype(jnp.float32)).astype(x.dtype)


def layernorm(x, g, b):
    xf = x.astype(jnp.float32)
    mu = jnp.mean(xf, axis=-1, keepdims=True)
    var = jnp.mean(jnp.square(xf - mu), axis=-1, keepdims=True)
    y = (xf - mu) * lax.rsqrt(var + EPS)
    return (y * g.astype(jnp.float32) + b.astype(jnp.float32)).astype(x.dtype)


def rope(x, pos):
    half = x.shape[-1] // 2
    inv_freq = ROPE_THETA ** (-jnp.arange(half, dtype=jnp.float32) / half)
    ang = pos.astype(jnp.float32)[:, None] * inv_freq[None, :]
    ang = ang.reshape((pos.shape[0],) + (1,) * (x.ndim - 3) + (half,))
    cos, sin = jnp.cos(ang), jnp.sin(ang)
    xf = x.astype(jnp.float32)
    x1, x2 = xf[..., :half], xf[..., half:]
    return jnp.concatenate([x1 * cos - x2 * sin, x2 * cos + x1 * sin], axis=-1).astype(x.dtype)


def chunk_mask(q_pos, k_pos):
    return (k_pos[None, :] // CHUNK) <= (q_pos[:, None] // CHUNK)


def over_query_blocks(fn, q_args, q_pos):
    T = q_pos.shape[0]
    if T > Q_BLOCK and T % Q_BLOCK == 0:
        nb = T // Q_BLOCK
        qb = tuple(jnp.moveaxis(a.reshape((a.shape[0], nb, Q_BLOCK) + a.shape[2:]), 1, 0) for a in q_args)
        out = lax.map(lambda t: fn(t[0], t[1]), (qb, q_pos.reshape(nb, Q_BLOCK)))
        out = jnp.moveaxis(out, 0, 1)
        return out.reshape((out.shape[0], T) + out.shape[3:])
    return fn(q_args, q_pos)


def swiglu(h, wg, wu, wd):
    return (jax.nn.silu(h @ wg) * (h @ wu)) @ wd


def token_mixer(u, past, p, layer):
    B, T, _ = u.shape
    past_len = 0 if past is None else past[0].shape[1]
    q_pos = past_len + jnp.arange(T, dtype=jnp.int32)
    k_pos = jnp.arange(past_len + T, dtype=jnp.int32)
    neg = jnp.finfo(jnp.float32).min
    cq, ckv, kpe, dq, dk, dv, cv = jnp.split(u @ p['w_in'], IN_SPLITS, axis=-1)

    cq = rmsnorm(cq, p['mla_q_norm_g'])
    q = (cq @ p['mla_w_uq']).reshape(B, T, MLA_HEADS, MLA_NOPE + MLA_ROPE)
    q_nope = q[..., :MLA_NOPE]
    q_pe = rope(q[..., MLA_NOPE:], q_pos)
    ckv = rmsnorm(ckv, p['mla_kv_norm_g'])
    kpe = rope(kpe, q_pos)
    ckv_all = ckv if past is None else jnp.concatenate([past[0], ckv], axis=1)
    kpe_all = kpe if past is None else jnp.concatenate([past[1], kpe], axis=1)
    Tk = ckv_all.shape[1]
    k_nope = (ckv_all @ p['mla_w_uk']).reshape(B, Tk, MLA_HEADS, MLA_NOPE)
    v_mla = (ckv_all @ p['mla_w_uv']).reshape(B, Tk, MLA_HEADS, MLA_V)

    def mla_block(qb, pb):
        qn, qr = qb
        s = (jnp.einsum('bqhd,bkhd->bhqk', qn, k_nope)
             + jnp.einsum('bqhr,bkr->bhqk', qr, kpe_all)).astype(jnp.float32) * MLA_SCALE
        s = jnp.where(chunk_mask(pb, k_pos), s, neg)
        pr = jax.nn.softmax(s, axis=-1).astype(v_mla.dtype)
        return jnp.einsum('bhqk,bkhd->bqhd', pr, v_mla)

    o_mla = over_query_blocks(mla_block, (q_nope, q_pe), q_pos)
    o_mla = o_mla.reshape(B, T, MLA_HEADS * MLA_V) @ p['mla_w_o']

    dq = rope(dq.reshape(B, T, DIFF_HEADS, 2, DIFF_HEAD_DIM), q_pos)
    dk = rope(dk.reshape(B, T, DIFF_HEADS, 2, DIFF_HEAD_DIM), q_pos)
    dv = dv.reshape(B, T, DIFF_HEADS, DIFF_V_DIM)
    dk_all = dk if past is None else jnp.concatenate([past[2], dk], axis=1)
    dv_all = dv if past is None else jnp.concatenate([past[3], dv], axis=1)
    lam_init = 0.8 - 0.6 * math.exp(-0.3 * layer)
    lam = (jnp.exp(jnp.sum((p['diff_lq1'] * p['diff_lk1']).astype(jnp.float32)))
           - jnp.exp(jnp.sum((p['diff_lq2'] * p['diff_lk2']).astype(jnp.float32))) + lam_init)

    def diff_block(qb, pb):
        qd = qb[0]
        s = jnp.einsum('bqhmd,bkhmd->bmhqk', qd, dk_all).astype(jnp.float32) * DIFF_SCALE
        s = jnp.where(chunk_mask(pb, k_pos), s, neg)
        pr = jax.nn.softmax(s, axis=-1)
        a = (pr[:, 0] - lam * pr[:, 1]).astype(dv_all.dtype)
        return jnp.einsum('bhqk,bkhe->bqhe', a, dv_all)

    o_diff = over_query_blocks(diff_block, (dq,), q_pos)
    o_diff = rmsnorm(o_diff, p['diff_subln_g']) * (1.0 - lam_init)
    o_diff = o_diff.reshape(B, T, DIFF_HEADS * DIFF_V_DIM) @ p['diff_w_o']

    ga, gb = jnp.split(cv, 2, axis=-1)
    glu = ga * jax.nn.sigmoid(gb)
    prev = jnp.zeros((B, CONV_STATE, CONV_CH), glu.dtype) if past is None else past[4]
    xin = jnp.concatenate([prev, glu], axis=1)
    y = lax.conv_general_dilated(xin, p['conv_w_dw'][:, None, :], window_strides=(1,), padding='VALID',
                                 dimension_numbers=('NWC', 'WIO', 'NWC'), feature_group_count=CONV_CH)
    y = jax.nn.silu(layernorm(y + p['conv_b_dw'], p['conv_ln_g'], p['conv_ln_b']))
    o_conv = y @ p['conv_w_pw2'] + p['conv_b_pw2']
    new_conv = xin[:, -CONV_STATE:]

    gates = jax.nn.sigmoid(u @ p['w_branch_gate'] + p['b_branch_gate']).reshape(B, T, N_BRANCH, D_MODEL)
    merged = gates[:, :, 0] * o_mla + gates[:, :, 1] * o_diff + gates[:, :, 2] * o_conv
    return merged @ p['w_out'], (ckv, kpe, dk, dv, new_conv)


def layer_forward(x, c, past, p, layer):
    mods = (jax.nn.silu(c) @ p['ada_w'] + p['ada_b'])[:, None, :]
    sh1, sc1, g1, sh2, sc2, g2, sh3, sc3, g3 = jnp.split(mods, N_ADA, axis=-1)
    h = rmsnorm(x, p['ffn1_pre_g']) * (1.0 + sc1) + sh1
    x = x + MACARON_W * g1 * rmsnorm(swiglu(h, p['ffn1_w_gate'], p['ffn1_w_up'], p['ffn1_w_down']), p['ffn1_post_g'])
    h = rmsnorm(x, p['mix_pre_g']) * (1.0 + sc2) + sh2
    o, state = token_mixer(h, past, p, layer)
    x = x + g2 * rmsnorm(o, p['mix_post_g'])
    h = rmsnorm(x, p['ffn2_pre_g']) * (1.0 + sc3) + sh3
    x = x + MACARON_W * g3 * rmsnorm(swiglu(h, p['ffn2_w_gate'], p['ffn2_w_up'], p['ffn2_w_down']), p['ffn2_post_g'])
    return x, state


def setup_inputs(seed: int = 0) -> dict:
    key = jax.random.key(seed)
    ks = iter(jax.random.split(key, 64))
    L, D = DEPTH, D_MODEL

    def nrm(shape, scale):
        return jax.random.normal(next(ks), shape, jnp.float32) * scale

    def gain(shape):
        return 1.0 + nrm(shape, 0.02)

    return {
        'x_prompt': nrm((BATCH, SEQ, D), 1.0),
        'x_sample': nrm((DEC_BATCH, DEC_SEQ, D), 1.0),
        'c_prompt': nrm((BATCH, D), 1.0),
        'c_sample': nrm((DEC_BATCH, D), 1.0),
        'cache_mla_ckv': nrm((L, DEC_BATCH, PAST_LEN, MLA_KV_RANK), 1.0),
        'cache_mla_kpe': nrm((L, DEC_BATCH, PAST_LEN, MLA_ROPE), 1.0),
        'cache_diff_k': nrm((L, DEC_BATCH, PAST_LEN, DIFF_HEADS, 2, DIFF_HEAD_DIM), 1.0),
        'cache_diff_v': nrm((L, DEC_BATCH, PAST_LEN, DIFF_HEADS, DIFF_V_DIM), 1.0),
        'state_conv': nrm((L, DEC_BATCH, CONV_STATE, CONV_CH), 0.5),
        'ada_w': nrm((L, D, N_ADA * D), D ** -0.5),
        'ada_b': nrm((L, N_ADA * D), 0.02),
        'ffn1_pre_g': gain((L, D)),
        'ffn1_post_g': gain((L, D)),
        'ffn1_w_gate': nrm((L, D, D_FF), D ** -0.5),
        'ffn1_w_up': nrm((L, D, D_FF), D ** -0.5),
        'ffn1_w_down': nrm((L, D_FF, D), D_FF ** -0.5),
        'mix_pre_g': gain((L, D)),
        'mix_post_g': gain((L, D)),
        'w_in': nrm((L, D, D_IN), D ** -0.5),
        'mla_q_norm_g': gain((L, MLA_Q_RANK)),
        'mla_w_uq': nrm((L, MLA_Q_RANK, MLA_HEADS * (MLA_NOPE + MLA_ROPE)), MLA_Q_RANK ** -0.5),
        'mla_kv_norm_g': gain((L, MLA_KV_RANK)),
        'mla_w_uk': nrm((L, MLA_KV_RANK, MLA_HEADS * MLA_NOPE), MLA_KV_RANK ** -0.5),
        'mla_w_uv': nrm((L, MLA_KV_RANK, MLA_HEADS * MLA_V), MLA_KV_RANK ** -0.5),
        'mla_w_o': nrm((L, MLA_HEADS * MLA_V, D), (MLA_HEADS * MLA_V) ** -0.5),
        'diff_lq1': nrm((L, DIFF_HEAD_DIM), 0.1),
        'diff_lk1': nrm((L, DIFF_HEAD_DIM), 0.1),
        'diff_lq2': nrm((L, DIFF_HEAD_DIM), 0.1),
        'diff_lk2': nrm((L, DIFF_HEAD_DIM), 0.1),
        'diff_subln_g': gain((L, DIFF_V_DIM)),
        'diff_w_o': nrm((L, DIFF_HEADS * DIFF_V_DIM, D), (DIFF_HEADS * DIFF_V_DIM) ** -0.5),
        'conv_w_dw': nrm((L, CONV_WIDTH, CONV_CH), CONV_WIDTH ** -0.5),
        'conv_b_dw': nrm((L, CONV_CH), 0.02),
        'conv_ln_g': gain((L, CONV_CH)),
        'conv_ln_b': nrm((L, CONV_CH), 0.02),
        'conv_w_pw2': nrm((L, CONV_CH, D), CONV_CH ** -0.5),
        'conv_b_pw2': nrm((L, D), 0.02),
        'w_branch_gate': nrm((L, D, N_BRANCH * D), D ** -0.5),
        'b_branch_gate': nrm((L, N_BRANCH * D), 0.02),
        'w_out': nrm((L, D, D), D ** -0.5),
        'ffn2_pre_g': gain((L, D)),
        'ffn2_post_g': gain((L, D)),
        'ffn2_w_gate': nrm((L, D, D_FF), D ** -0.5),
        'ffn2_w_up': nrm((L, D, D_FF), D ** -0.5),
        'ffn2_w_down': nrm((L, D_FF, D), D_FF ** -0.5),
    }


def reference(x_prompt, x_sample, c_prompt, c_sample, cache_mla_ckv, cache_mla_kpe, cache_diff_k, cache_diff_v,
              state_conv, ada_w, ada_b, ffn1_pre_g, ffn1_post_g, ffn1_w_gate, ffn1_w_up, ffn1_w_down,
              mix_pre_g, mix_post_g, w_in, mla_q_norm_g, mla_w_uq, mla_kv_norm_g, mla_w_uk, mla_w_uv, mla_w_o,
              diff_lq1, diff_lk1, diff_lq2, diff_lk2, diff_subln_g, diff_w_o,
              conv_w_dw, conv_b_dw, conv_ln_g, conv_ln_b, conv_w_pw2, conv_b_pw2,
              w_branch_gate, b_branch_gate, w_out,
              ffn2_pre_g, ffn2_post_g, ffn2_w_gate, ffn2_w_up, ffn2_w_down):
    y_prompt, y_sample = x_prompt, x_sample
    st_p, st_s = [], []
    for l in range(DEPTH):
        p = {
            'ada_w': ada_w[l], 'ada_b': ada_b[l],
            'ffn1_pre_g': ffn1_pre_g[l], 'ffn1_post_g': ffn1_post_g[l],
            'ffn1_w_gate': ffn1_w_gate[l], 'ffn1_w_up': ffn1_w_up[l], 'ffn1_w_down': ffn1_w_down[l],
            'mix_pre_g': mix_pre_g[l], 'mix_post_g': mix_post_g[l], 'w_in': w_in[l],
            'mla_q_norm_g': mla_q_norm_g[l], 'mla_w_uq': mla_w_uq[l], 'mla_kv_norm_g': mla_kv_norm_g[l],
            'mla_w_uk': mla_w_uk[l], 'mla_w_uv': mla_w_uv[l], 'mla_w_o': mla_w_o[l],
            'diff_lq1': diff_lq1[l], 'diff_lk1': diff_lk1[l], 'diff_lq2': diff_lq2[l], 'diff_lk2': diff_lk2[l],
            'diff_subln_g': diff_subln_g[l], 'diff_w_o': diff_w_o[l],
            'conv_w_dw': conv_w_dw[l], 'conv_b_dw': conv_b_dw[l], 'conv_ln_g': conv_ln_g[l],
            'conv_ln_b': conv_ln_b[l], 'conv_w_pw2': conv_w_pw2[l], 'conv_b_pw2': conv_b_pw2[l],
            'w_branch_gate': w_branch_gate[l], 'b_branch_gate': b_branch_gate[l], 'w_out': w_out[l],
            'ffn2_pre_g': ffn2_pre_g[l], 'ffn2_post_g': ffn2_post_g[l],
            'ffn2_w_gate': ffn2_w_gate[l], 'ffn2_w_up': ffn2_w_up[l], 'ffn2_w_down': ffn2_w_down[l],
        }
        y_prompt, sp = layer_forward(y_prompt, c_prompt, None, p, l)
        past = (cache_mla_ckv[l], cache_mla_kpe[l], cache_diff_k[l], cache_diff_v[l], state_conv[l])
        y_sample, ss = layer_forward(y_sample, c_sample, past, p, l)
        st_p.append(sp)
        st_s.append(ss)
    new_mla_ckv_prompt = jnp.stack([s[0] for s in st_p])
    new_mla_kpe_prompt = jnp.stack([s[1] for s in st_p])
    new_diff_k_prompt = jnp.stack([s[2] for s in st_p])
    new_diff_v_prompt = jnp.stack([s[3] for s in st_p])
    new_conv_prompt = jnp.stack([s[4] for s in st_p])
    new_mla_ckv_sample = jnp.stack([s[0] for s in st_s])
    new_mla_kpe_sample = jnp.stack([s[1] for s in st_s])
    new_diff_k_sample = jnp.stack([s[2] for s in st_s])
    new_diff_v_sample = jnp.stack([s[3] for s in st_s])
    new_conv_sample = jnp.stack([s[4] for s in st_s])
    return (y_prompt, y_sample,
            new_mla_ckv_prompt, new_mla_kpe_prompt, new_diff_k_prompt, new_diff_v_prompt, new_conv_prompt,
            new_mla_ckv_sample, new_mla_kpe_sample, new_diff_k_sample, new_diff_v_sample, new_conv_sample)
```

```python
import math
from contextlib import ExitStack

import numpy as np
import concourse.bass as bass
import concourse.mybir as mybir
from concourse.bass_utils import run_bass_kernel_spmd

F32 = mybir.dt.float32
BF16 = mybir.dt.bfloat16
I32 = mybir.dt.int32
AF = mybir.ActivationFunctionType
ALU = mybir.AluOpType
AX = mybir.AxisListType

D = 1024
DFF = 2816
NADA = 9
CHUNK = 64
EPS = 1e-6
THETA = 10000.0
MLA_H, MLA_NOPE, MLA_ROPE, MLA_V, MLA_KV, MLA_Q = 16, 64, 32, 64, 256, 512
MLA_SCALE = (MLA_NOPE + MLA_ROPE) ** -0.5
DIFF_H, DIFF_HD = 8, 64
DIFF_SCALE = DIFF_HD ** -0.5
CONVW = 31
CST = 30
DIN = 5920
C_CQ, C_CKV, C_KPE, C_DQ, C_DK, C_DV, C_CV = 0, 512, 768, 800, 1824, 2848, 3872
TS = 64
DEPTH = 2


class Buf:
    __slots__ = ("name", "w", "r")

    def __init__(self, name=""):
        self.name = name
        self.w = None
        self.r = {}


class Sched:
    ENGS = ("pe", "act", "dve", "pool", "sp")

    def __init__(self, nc, stack, n_dma_sems=14):
        self.nc = nc
        self.sems = {}
        self.cnt = {}
        self.ops = {e: [] for e in self.ENGS}
        self.seen = {e: {} for e in self.ENGS}
        for e in ("pe", "act", "dve", "pool"):
            k = "E_" + e
            self.sems[k] = stack.enter_context(nc.semaphore(k))
            self.cnt[k] = 0
        self.dma_pool = {}
        self.dma_rr = {}
        for q in ("sp", "pool", "act"):
            keys = []
            for i in range(n_dma_sems):
                k = "D_%s_%d" % (q, i)
                self.sems[k] = stack.enter_context(nc.semaphore(k))
                self.cnt[k] = 0
                keys.append(k)
            self.dma_pool[q] = keys
            self.dma_rr[q] = 0
        self.n_wait = 0
        self.n_ops = 0

    def _need(self, eng, k, v, deng, deps):
        if eng == "pe" and deng == "pe":
            return
        if self.seen[eng].get(k, 0) >= v:
            return
        if deps.get(k, 0) < v:
            deps[k] = v

    def _collect(self, eng, reads, writes, war):
        deps = {}
        for b in reads:
            if b.w is not None:
                self._need(eng, b.w[0], b.w[1], b.w[2], deps)
        for b in writes:
            if b.w is not None:
                self._need(eng, b.w[0], b.w[1], b.w[2], deps)
            for k, (v, de) in b.r.items():
                self._need(eng, k, v, de, deps)
        for b in war:
            for k, (v, de) in b.r.items():
                self._need(eng, k, v, de, deps)
        return deps

    def _emit_waits(self, eng, deps):
        for k, v in deps.items():
            sem = self.sems[k]
            self.ops[eng].append(lambda h, sem=sem, v=v: h.wait_ge(sem, v))
            self.seen[eng][k] = v
            self.n_wait += 1

    def _commit(self, ident, reads, writes):
        k, v, e = ident
        for b in reads:
            b.r[k] = (v, e)
        for b in writes:
            b.w = ident
            b.r = {}

    def op(self, eng, fn, reads=(), writes=(), war=()):
        deps = self._collect(eng, reads, writes, war)
        self._emit_waits(eng, deps)
        k = "E_" + eng
        self.cnt[k] += 1
        v = self.cnt[k]
        sem = self.sems[k]
        self.ops[eng].append(lambda h, fn=fn, sem=sem: fn(h).then_inc(sem, 1))
        self._commit((k, v, eng), reads, writes)
        self.n_ops += 1

    def dma(self, q, out, in_, reads=(), writes=(), war=(), **kw):
        deps = self._collect(q, reads, writes, war)
        pool = self.dma_pool[q]
        k = pool[self.dma_rr[q] % len(pool)]
        self.dma_rr[q] += 1
        pv = self.cnt[k]
        if pv > 0 and self.seen[q].get(k, 0) < pv:
            deps[k] = max(deps.get(k, 0), pv)
        self._emit_waits(q, deps)
        self.cnt[k] += 16
        v = self.cnt[k]
        sem = self.sems[k]
        self.ops[q].append(
            lambda h, out=out, in_=in_, sem=sem, kw=kw: h.dma_start(out=out, in_=in_, **kw).then_inc(sem, 16))
        self._commit((k, v, q), reads, writes)
        self.n_ops += 1

    def barrier(self):
        for eng in self.ENGS:
            deps = {}
            for k, c in self.cnt.items():
                if c > 0 and self.seen[eng].get(k, 0) < c and not (eng == "pe" and k == "E_pe"):
                    deps[k] = c
            self._emit_waits(eng, deps)

    def finish(self):
        deps = {}
        for q, keys in self.dma_pool.items():
            for k in keys:
                if self.cnt[k] > 0 and self.seen["sp"].get(k, 0) < self.cnt[k]:
                    deps[k] = self.cnt[k]
        for e in ("pe", "act", "dve", "pool"):
            k = "E_" + e
            if self.cnt[k] > 0 and self.seen["sp"].get(k, 0) < self.cnt[k]:
                deps[k] = self.cnt[k]
        self._emit_waits("sp", deps)

    def emit(self):
        nc = self.nc
        with nc.Block() as block:
            @block.sync
            def _(h):
                for f in self.ops["sp"]:
                    f(h)

            @block.tensor
            def _(h):
                for f in self.ops["pe"]:
                    f(h)

            @block.scalar
            def _(h):
                for f in self.ops["act"]:
                    f(h)

            @block.vector
            def _(h):
                for f in self.ops["dve"]:
                    f(h)

            @block.gpsimd
            def _(h):
                for f in self.ops["pool"]:
                    f(h)


WEIGHT_SPECS = [
    ("ada_w", [D, NADA * D]), ("ada_b", [NADA * D]),
    ("ffn1_pre_g", [D]), ("ffn1_post_g", [D]),
    ("ffn1_w_gate", [D, DFF]), ("ffn1_w_up", [D, DFF]), ("ffn1_w_down", [DFF, D]),
    ("mix_pre_g", [D]), ("mix_post_g", [D]), ("w_in", [D, DIN]),
    ("mla_q_norm_g", [MLA_Q]), ("mla_w_uq", [MLA_Q, MLA_H * 96]), ("mla_kv_norm_g", [MLA_KV]),
    ("mla_w_uk", [MLA_KV, MLA_H * 64]), ("mla_w_uv", [MLA_KV, MLA_H * 64]), ("mla_w_o", [D, D]),
    ("diff_lq1", [64]), ("diff_lk1", [64]), ("diff_lq2", [64]), ("diff_lk2", [64]),
    ("diff_subln_g", [128]), ("diff_w_o", [D, D]),
    ("conv_w_dw", [CONVW, D]), ("conv_b_dw", [D]), ("conv_ln_g", [D]), ("conv_ln_b", [D]),
    ("conv_w_pw2", [D, D]), ("conv_b_pw2", [D]),
    ("w_branch_gate", [D, 3 * D]), ("b_branch_gate", [3 * D]), ("w_out", [D, D]),
    ("ffn2_pre_g", [D]), ("ffn2_post_g", [D]),
    ("ffn2_w_gate", [D, DFF]), ("ffn2_w_up", [D, DFF]), ("ffn2_w_down", [DFF, D]),
]


class Cfg:
    def __init__(self, T=4096, NS=4, PAST=4096, L=2, phases=("ffn1", "mix", "ffn2"), dbg=False):
        self.T, self.NS, self.PAST, self.L = T, NS, PAST, L
        self.phases = phases
        self.dbg = dbg
        self.NSTOK = NS * TS
        self.NTOK = T + self.NSTOK
        self.NR = 1 + NS
        self.KS = PAST + TS


def build_program(cfg):
    T, NS, PAST, L = cfg.T, cfg.NS, cfg.PAST, cfg.L
    NTOK, NSTOK, NR, KS = cfg.NTOK, cfg.NSTOK, cfg.NR, cfg.KS
    nc = bass.Bass("TRN2", target_bir_lowering=False)

    def din(name, shape, dt=F32):
        return nc.dram_tensor(name, list(shape), dt, kind="ExternalInput").ap()

    def dout(name, shape, dt=F32):
        return nc.dram_tensor(name, list(shape), dt, kind="ExternalOutput").ap()

    def dscr(name, shape, dt):
        return nc.dram_tensor(name, list(shape), dt).ap()

    x_p = din("x_prompt", [T, D])
    x_s = din("x_sample", [NSTOK, D])
    c_all = din("c_all", [NR, D])
    c_ckv = din("cache_mla_ckv", [DEPTH, NS, PAST, MLA_KV])
    c_kpe = din("cache_mla_kpe", [DEPTH, NS, PAST, MLA_ROPE])
    c_dk = din("cache_diff_k", [DEPTH, NS, PAST, 1024])
    c_dv = din("cache_diff_v", [DEPTH, NS, PAST, 1024])
    c_conv = din("state_conv", [DEPTH, NS, CST, D])
    Wd = {}
    for name, shp in WEIGHT_SPECS:
        Wd[name] = din(name, [2] + shp)

    y_p = dout("y_prompt", [T, D])
    y_s = dout("y_sample", [NSTOK, D])
    o_ckv_p = dout("o_ckv_p", [DEPTH, T, MLA_KV])
    o_kpe_p = dout("o_kpe_p", [DEPTH, T, MLA_ROPE])
    o_dk_p = dout("o_dk_p", [DEPTH, T, 1024])
    o_dv_p = dout("o_dv_p", [DEPTH, T, 1024])
    o_conv_p = dout("o_conv_p", [DEPTH, CST, D])
    o_ckv_s = dout("o_ckv_s", [DEPTH, NSTOK, MLA_KV])
    o_kpe_s = dout("o_kpe_s", [DEPTH, NSTOK, MLA_ROPE])
    o_dk_s = dout("o_dk_s", [DEPTH, NSTOK, 1024])
    o_dv_s = dout("o_dv_s", [DEPTH, NSTOK, 1024])
    o_conv_s = dout("o_conv_s", [DEPTH, NS, CST, D])

    XT = dscr("XT", [D, NTOK], F32)
    NSP = max(NS, 1)
    QTd = dscr("QTd", [8, 128, NTOK], BF16)
    KTd_p = dscr("KTd_p", [8, 128, T], BF16)
    KTd_s = dscr("KTd_s", [NSP, 8, 128, KS], BF16)
    Vd_p = dscr("Vd_p", [T, 1024], BF16)
    Vd_s = dscr("Vd_s", [NSP, KS, 1024], BF16)
    QTm = dscr("QTm", [16, 96, NTOK], BF16)
    KTm_p = dscr("KTm_p", [1024, T], BF16)
    KTm_s = dscr("KTm_s", [NSP, 1024, KS], BF16)
    KPT_p = dscr("KPT_p", [32, T], BF16)
    KPT_s = dscr("KPT_s", [NSP, 32, KS], BF16)
    Vm_p = dscr("Vm_p", [T, 1024], BF16)
    Vm_s = dscr("Vm_s", [NSP, KS, 1024], BF16)
    GLUT = dscr("GLUT", [D, NTOK], F32)
    AOm = dscr("AOm", [D, NTOK], BF16)
    AOd = dscr("AOd", [D, NTOK], BF16)
    b_XT = Buf("XT")
    b_XT2 = Buf("XT_st")
    dbg_out = {}
    if cfg.dbg:
        dbg_out["dbg_xt"] = dout("dbg_xt", [D, NTOK])

    with ExitStack() as st:
        S = Sched(nc, st)
        dumped = set()

        def dbg_dump(name, ap, shape, dt, reads):
            if not cfg.dbg or name in dumped:
                return
            dumped.add(name)
            o = dout(name, shape, dt)
            S.dma("sp", o, ap, reads=reads)

        def sb(name, shape, dt):
            return st.enter_context(nc.sbuf_tensor(name, list(shape), dt))

        PS = [st.enter_context(nc.psum_tensor("ps%d" % i, [128, 512], F32)) for i in range(8)]
        bPS = [Buf("ps%d" % i) for i in range(8)]

        ones_bf = sb("ones_bf", [128, 128], BF16)
        ones_f = sb("ones_f", [128, 128], F32)
        ident_f = sb("ident_f", [128, 128], F32)
        ident_bf = sb("ident_bf", [128, 128], BF16)
        eps_t = sb("eps_t", [128, 1], F32)
        b_const = Buf("const")
        S.op("dve", lambda h: h.memset(ones_f[:], 1.0), writes=[b_const])
        S.op("dve", lambda h: h.memset(ones_bf[:], 1.0), writes=[b_const])
        S.op("dve", lambda h: h.memset(eps_t[:], EPS), writes=[b_const])
        S.op("pool", lambda h: h.affine_select(ident_f[:], ones_f[:], [[-1, 128]], ALU.is_equal, 0.0,
                                               base=0, channel_multiplier=1),
             reads=[b_const], writes=[b_const])
        S.op("dve", lambda h: h.tensor_copy(ident_bf[:], ident_f[:]), reads=[b_const], writes=[b_const])

        modsT = sb("modsT", [128, L, 72, NR], F32)
        gvec = sb("gvec", [128, L, 6, 8], F32)
        gsc = sb("gsc", [128, L, 3, 8, NR], F32)
        gpo = sb("gpo", [128, L, 3, 8, NR], F32)
        b_mods = Buf("mods")
        b_gvec = Buf("gvec")
        WREG = sb("WREG", [128, 67584], BF16)
        b_W = Buf("Wregion")
        AREG = sb("AREG", [128, 30720], BF16)
        b_A = Buf("Aregion")

        def aview(off_bytes, shape, dt):
            esz = 2 if dt == BF16 else 4
            n = int(np.prod(shape))
            a = AREG[:, off_bytes // 2: off_bytes // 2 + n * esz // 2]
            if dt != BF16:
                a = a.bitcast(dt)
            if len(shape) == 2:
                a = a.rearrange("p (a b) -> p a b", a=shape[0])
            elif len(shape) == 3:
                a = a.rearrange("p (a b c) -> p a b c", a=shape[0], b=shape[1])
            return a

        def wview(off_el, shape):
            n = int(np.prod(shape))
            a = WREG[:, off_el: off_el + n]
            if len(shape) == 2:
                a = a.rearrange("p (a b) -> p a b", a=shape[0])
            return a

        vstage = [sb("vstage0", [128, 128], F32), sb("vstage1", [128, 128], F32)]
        b_vstage = [Buf(), Buf()]
        vcnt = [0]

        def load_fm(dst, src2d, n, wbuf):
            i = vcnt[0] % 2
            vcnt[0] += 1
            stg, bs = vstage[i], b_vstage[i]
            S.dma("sp", stg[0:n, :], src2d, writes=[bs])
            S.op("pe", lambda h, stg=stg, n=n, i=i: h.transpose(PS[6 + i][:, 0:n], stg[0:n, :], ident_f[0:n, 0:n]),
                 reads=[bs, b_const], writes=[bPS[6 + i]])
            S.op("dve", lambda h, dst=dst, n=n, i=i: h.tensor_copy(dst, PS[6 + i][:, 0:n]), reads=[bPS[6 + i]], writes=[wbuf])

        def ada_phase():
            c_sb = aview(0, [D], F32)
            cT = aview(4096, [8, NR], F32)
            cTb = aview(4096 + 8 * NR * 4 + 64, [8, NR], BF16)
            adab = aview(8192, [L, 72], F32)
            b_c, b_cT, b_adab = Buf(), Buf(), Buf()
            S.dma("sp", c_sb[0:NR, :], c_all, writes=[b_c], war=[b_A])
            for l in range(L):
                load_fm(adab[:, l, :], Wd["ada_b"][l].rearrange("(m p) -> m p", p=128), 72, b_adab)
                for gi, nm in enumerate(("ffn1_pre_g", "ffn1_post_g", "mix_pre_g", "mix_post_g",
                                         "ffn2_pre_g", "ffn2_post_g")):
                    load_fm(gvec[:, l, gi, :], Wd[nm][l].rearrange("(c p) -> c p", p=128), 8, b_gvec)
            for c in range(8):
                S.op("pe", lambda h, c=c: h.transpose(PS[0][:, c * 8: c * 8 + NR], c_sb[0:NR, c * 128:(c + 1) * 128],
                                                      ident_f[0:NR, 0:NR]),
                     reads=[b_c, b_const], writes=[bPS[0]])
            S.op("act", lambda h: h.activation(cT[:, :, :], PS[0][:, 0:64].rearrange("p (c r) -> p c r", r=8)[:, :, 0:NR],
                                               AF.Silu),
                 reads=[bPS[0]], writes=[b_cT])
            S.op("dve", lambda h: h.tensor_copy(cTb[:, :, :], cT[:, :, :]), reads=[b_cT], writes=[b_cT])
            wbuf = [wview(0, [8, 1024]), wview(8192, [8, 1024])]
            b_wb = [Buf(), Buf()]
            gi = 0
            for l in range(L):
                for g in range(9):
                    wb, bw = wbuf[gi % 2], b_wb[gi % 2]
                    S.dma("pool", wb, Wd["ada_w"][l][:, g * 1024:(g + 1) * 1024].rearrange("(c p) m -> p c m", p=128),
                          writes=[bw], war=[b_W])
                    pb = 1 + gi % 2

                    def mm(h, wb=wb, pb=pb):
                        ins = None
                        for m in range(8):
                            for c in range(8):
                                ins = h.matmul(PS[pb][:, m * 8: m * 8 + NR], wb[:, c, m * 128:(m + 1) * 128],
                                               cTb[:, c, :], start=(c == 0), stop=(c == 7))
                        return ins
                    S.op("pe", mm, reads=[bw, b_cT], writes=[bPS[pb]])
                    S.op("dve", lambda h, l=l, g=g, pb=pb: h.tensor_tensor(
                        modsT[:, l, g * 8:(g + 1) * 8, :],
                        PS[pb][:, 0:64].rearrange("p (c r) -> p c r", r=8)[:, :, 0:NR],
                        adab[:, l, g * 8:(g + 1) * 8].unsqueeze(2).to_broadcast([128, 8, NR]), ALU.add),
                        reads=[bPS[pb], b_adab], writes=[b_mods])
                    gi += 1
            for l in range(L):
                for k in range(3):
                    sc = modsT[:, l, (3 * k + 1) * 8:(3 * k + 2) * 8, :]
                    gt = modsT[:, l, (3 * k + 2) * 8:(3 * k + 3) * 8, :]
                    pre = gvec[:, l, 2 * k, :].unsqueeze(2).to_broadcast([128, 8, NR])
                    post = gvec[:, l, 2 * k + 1, :].unsqueeze(2).to_broadcast([128, 8, NR])
                    wgt = 1.0 if k == 1 else 0.5
                    S.op("dve", lambda h, l=l, k=k, sc=sc, pre=pre: h.scalar_tensor_tensor(
                        gsc[:, l, k, :, :], sc, 1.0, pre, ALU.add, ALU.mult),
                        reads=[b_mods, b_gvec], writes=[b_mods])
                    S.op("dve", lambda h, l=l, k=k, gt=gt, post=post, wgt=wgt: h.scalar_tensor_tensor(
                        gpo[:, l, k, :, :], gt, wgt, post, ALU.mult, ALU.mult),
                        reads=[b_mods, b_gvec], writes=[b_mods])

        tiles = []
        import os
        TN = int(os.environ.get("TILE_N", "512"))
        for i in range(T // TN):
            tiles.append(dict(n=TN, col0=i * TN, segs=[(0, 0, TN)], prompt=True, idx=i))
        tiles.append(dict(n=NSTOK, col0=T, segs=[(1 + s, s * TS, TS) for s in range(NS)], prompt=False, idx=0))

        XTv = XT.rearrange("(c p) n -> p c n", p=128)

        def load_x_phase():
            xin = [aview(0, [D], F32), aview(4096, [D], F32)]
            b_xin = [Buf(), Buf()]
            xo = [aview(8192, [8, 128], F32), aview(8192 + 4096, [8, 128], F32)]
            b_xo = [Buf(), Buf()]
            blocks = [(x_p, i * 128, i * 128) for i in range(T // 128)] + \
                     [(x_s, i * 128, T + i * 128) for i in range(NSTOK // 128)]
            for bi, (src, r0, col) in enumerate(blocks):
                xi, bx = xin[bi % 2], b_xin[bi % 2]
                S.dma("sp", xi, src[r0:r0 + 128, :], writes=[bx], war=[b_A])
                for half in range(2):
                    pb = 2 * (bi % 2) + half
                    S.op("pe", lambda h, xi=xi, half=half, pb=pb: [
                        h.transpose(PS[pb][:, c * 128:(c + 1) * 128],
                                    xi[:, (half * 4 + c) * 128:(half * 4 + c + 1) * 128], ident_f[:])
                        for c in range(4)][-1], reads=[bx, b_const], writes=[bPS[pb]])
                    o, bo = xo[bi % 2], b_xo[bi % 2]
                    eng = "act" if half == 0 else "dve"
                    if half == 0:
                        S.op("act", lambda h, o=o, pb=pb: h.copy(o[:, 0:4, :], PS[pb][:, :].rearrange("p (c n) -> p c n", c=4)),
                             reads=[bPS[pb]], writes=[bo])
                    else:
                        S.op("dve", lambda h, o=o, pb=pb: h.tensor_copy(o[:, 4:8, :], PS[pb][:, :].rearrange("p (c n) -> p c n", c=4)),
                             reads=[bPS[pb]], writes=[bo])
                S.dma("sp", XTv[:, :, col:col + 128], xo[bi % 2], reads=[b_xo[bi % 2]], writes=[b_XT])

        A_XT = 0
        A_HT = 16384
        A_SQ = 24576
        A_RS = 26624
        A_TMP = 28672
        A_FREE = 32768

        def norm_mod(tile, l, k, xT, hT, b_xT, b_hT, sq, b_sq, rs, b_rs, tmp, b_tmp, psb):
            N = tile["n"]
            for c in range(8):
                S.op("act", lambda h, c=c: h.activation(sq[c % 2][:, :N], xT[:, c, :N], AF.Square),
                     reads=[b_xT], writes=[b_sq[c % 2]])
                S.op("pe", lambda h, c=c: h.matmul(PS[psb][:, :N], ones_bf[:], sq[c % 2][:, :N], start=(c == 0), stop=(c == 7)),
                     reads=[b_sq[c % 2], b_const], writes=[bPS[psb]])
            S.op("act", lambda h: h.activation(rs[:, :N], PS[psb][:, :N], AF.Sqrt, bias=eps_t[:, 0:1], scale=1.0 / D),
                 reads=[bPS[psb], b_const], writes=[b_rs])
            S.op("dve", lambda h: h.reciprocal(rs[:, :N], rs[:, :N]), reads=[b_rs], writes=[b_rs])
            for c in range(8):
                t, bt = tmp[c % 2], b_tmp[c % 2]
                S.op("dve", lambda h, c=c, t=t: h.tensor_tensor(t[:, :N], xT[:, c, :N], rs[:, :N], ALU.mult),
                     reads=[b_xT, b_rs], writes=[bt])
                for (r, c0, n) in tile["segs"]:
                    S.op("act", lambda h, c=c, t=t, r=r, c0=c0, n=n: h.activation(
                        hT[:, c, c0:c0 + n], t[:, c0:c0 + n], AF.Identity,
                        bias=modsT[:, l, (3 * k) * 8 + c, r:r + 1], scale=gsc[:, l, k, c, r:r + 1]),
                        reads=[bt, b_mods], writes=[b_hT])

        def post_norm_residual(tile, l, k, yT, b_yT, xr, b_xr, rs, b_rs, tmp, b_tmp, psb):
            N, col0 = tile["n"], tile["col0"]
            S.op("act", lambda h: h.activation(rs[:, :N], PS[psb][:, :N], AF.Sqrt, bias=eps_t[:, 0:1], scale=1.0 / D),
                 reads=[bPS[psb], b_const], writes=[b_rs])
            S.op("dve", lambda h: h.reciprocal(rs[:, :N], rs[:, :N]), reads=[b_rs], writes=[b_rs])
            for c in range(8):
                t, bt = tmp[c % 2], b_tmp[c % 2]
                x_, bx = xr[c % 2], b_xr[c % 2]
                S.dma("sp", x_[:, :N], XT[c * 128:(c + 1) * 128, col0:col0 + N], reads=[b_XT], writes=[bx])
                S.op("dve", lambda h, c=c, t=t: h.tensor_tensor(t[:, :N], yT[:, c, :N], rs[:, :N], ALU.mult),
                     reads=[b_yT, b_rs], writes=[bt])
                for (r, c0, n) in tile["segs"]:
                    S.op("dve", lambda h, c=c, t=t, x_=x_, r=r, c0=c0, n=n: h.scalar_tensor_tensor(
                        x_[:, c0:c0 + n], t[:, c0:c0 + n], gpo[:, l, k, c, r:r + 1], x_[:, c0:c0 + n],
                        ALU.mult, ALU.add), reads=[bt, b_mods, bx], writes=[bx])
                S.dma("sp", XT[c * 128:(c + 1) * 128, col0:col0 + N], x_[:, :N], reads=[bx])

        def ffn_phase(l, k):
            pre = "ffn1" if k == 0 else "ffn2"
            Wg = wview(0, [8, DFF])
            Wu = wview(8 * DFF, [8, DFF])
            Wdn = wview(16 * DFF, [22, D])
            b_wg = [Buf() for _ in range(8)]
            b_wu = [Buf() for _ in range(8)]
            b_wd = [Buf() for _ in range(22)]
            for c in range(8):
                S.dma("pool", Wg[:, c, :], Wd[pre + "_w_gate"][l][c * 128:(c + 1) * 128, :], writes=[b_wg[c]], war=[b_W])
                S.dma("pool", Wu[:, c, :], Wd[pre + "_w_up"][l][c * 128:(c + 1) * 128, :], writes=[b_wu[c]], war=[b_W])
            for j in range(22):
                S.dma("pool", Wdn[:, j, :], Wd[pre + "_w_down"][l][j * 128:(j + 1) * 128, :], writes=[b_wd[j]], war=[b_W])
            xT = aview(A_XT, [8, 512], F32)
            hT = aview(A_HT, [8, 512], BF16)
            sq = [aview(A_SQ, [512], BF16), aview(A_SQ + 1024, [512], BF16)]
            rs = aview(A_RS, [512], F32)
            tmp = [aview(A_TMP, [512], F32), aview(A_TMP + 2048, [512], F32)]
            aT = aview(A_FREE, [22, 512], BF16)
            sg = [aview(A_FREE + 22528, [512], BF16), aview(A_FREE + 22528 + 1024, [512], BF16)]
            yT = xT
            xr = [aview(A_FREE + 22528 + 2048, [512], F32), aview(A_FREE + 22528 + 4096, [512], F32)]
            b_xT, b_hT, b_rs, b_aT = Buf(), Buf(), Buf(), Buf()
            b_yT = b_xT
            b_sq, b_tmp, b_sg, b_xr = [Buf(), Buf()], [Buf(), Buf()], [Buf(), Buf()], [Buf(), Buf()]
            kk = 0 if k == 0 else 2
            def ffn_tile(tile):
                N, col0 = tile["n"], tile["col0"]
                S.dma("sp", xT[:, :, :N], XTv[:, :, col0:col0 + N], reads=[b_XT], writes=[b_xT], war=[b_A])
                norm_mod(tile, l, kk, xT, hT, b_xT, b_hT, sq, b_sq, rs, b_rs, tmp, b_tmp, 0)
                if os.environ.get("DEBUG_BARRIER2"):
                    S.barrier()
                dbg_dump("dbg_rs", rs[:, :N], [128, N], F32, [b_rs])
                dbg_dump("dbg_hT", hT[:, :, :N], [128, 8, N], BF16, [b_hT])
                for j in range(22):
                    pg, pu = 1 + 2 * (j % 3), 2 + 2 * (j % 3)

                    def mmg(h, j=j, pg=pg):
                        ins = None
                        for c in range(8):
                            ins = h.matmul(PS[pg][:, :N], Wg[:, c, j * 128:(j + 1) * 128], hT[:, c, :N],
                                           start=(c == 0), stop=(c == 7))
                        return ins

                    def mmu(h, j=j, pu=pu):
                        ins = None
                        for c in range(8):
                            ins = h.matmul(PS[pu][:, :N], Wu[:, c, j * 128:(j + 1) * 128], hT[:, c, :N],
                                           start=(c == 0), stop=(c == 7))
                        return ins
                    S.op("pe", mmg, reads=b_wg + [b_hT, b_W], writes=[bPS[pg]])
                    S.op("pe", mmu, reads=b_wu + [b_hT, b_W], writes=[bPS[pu]])
                    if os.environ.get("DEBUG_BARRIER"):
                        S.barrier()
                    S.op("act", lambda h, j=j, pg=pg: h.activation(sg[j % 2][:, :N], PS[pg][:, :N], AF.Silu),
                         reads=[bPS[pg]], writes=[b_sg[j % 2]])
                    S.op("dve", lambda h, j=j, pu=pu: h.tensor_tensor(aT[:, j, :N], sg[j % 2][:, :N], PS[pu][:, :N], ALU.mult),
                         reads=[b_sg[j % 2], bPS[pu]], writes=[b_aT])
                for c in range(8):
                    py = 1 + (c % 6)

                    def mmd(h, c=c, py=py):
                        ins = None
                        for j in range(22):
                            ins = h.matmul(PS[py][:, :N], Wdn[:, j, c * 128:(c + 1) * 128], aT[:, j, :N],
                                           start=(j == 0), stop=(j == 21))
                        return ins
                    S.op("pe", mmd, reads=b_wd + [b_aT, b_W], writes=[bPS[py]])
                    S.op("act", lambda h, c=c, py=py: h.copy(yT[:, c, :N], PS[py][:, :N]), reads=[bPS[py]], writes=[b_yT])
                    S.op("act", lambda h, c=c, py=py: h.activation(sq[c % 2][:, :N], PS[py][:, :N], AF.Square),
                         reads=[bPS[py]], writes=[b_sq[c % 2]])
                    S.op("pe", lambda h, c=c: h.matmul(PS[7][:, :N], ones_bf[:], sq[c % 2][:, :N], start=(c == 0), stop=(c == 7)),
                         reads=[b_sq[c % 2], b_const], writes=[bPS[7]])
                dbg_dump("dbg_aT", aT[:, :, :N], [128, 22, N], BF16, [b_aT])
                dbg_dump("dbg_sg", sg[1][:, :N], [128, N], BF16, [b_sg[1]])
                post_norm_residual(tile, l, kk, yT, b_yT, xr, b_xr, rs, b_rs, tmp, b_tmp, 7)

            for tile in tiles:
                ffn_tile(tile)
            S.barrier()


        NBP = T // 128
        NBLK = NBP + 1
        TWO_PI = 2.0 * math.pi

        def mixB_phase(l):
            Win = wview(0, [8, DIN])
            Wuq = wview(8 * DIN, [4, 1536])
            Wuk = wview(8 * DIN + 6144, [2, 1024])
            Wuv = wview(8 * DIN + 8192, [2, 1024])
            TB0 = (8 * DIN + 10240) * 2

            def wtab(off, shape, dt):
                esz = 2 if dt == BF16 else 4
                n = int(np.prod(shape))
                a = WREG[:, (TB0 + off) // 2:(TB0 + off) // 2 + n * esz // 2]
                if dt != BF16:
                    a = a.bitcast(dt)
                if len(shape) == 2:
                    a = a.rearrange("p (a b) -> p a b", a=shape[0])
                return a
            b_win = [Buf() for _ in range(8)]
            b_wu = Buf()
            for c in range(8):
                S.dma("pool", Win[:, c, :], Wd["w_in"][l][c * 128:(c + 1) * 128, :], writes=[b_win[c]], war=[b_W])
            for c in range(4):
                S.dma("pool", Wuq[:, c, :], Wd["mla_w_uq"][l][c * 128:(c + 1) * 128, :], writes=[b_wu], war=[b_W])
            for c in range(2):
                S.dma("pool", Wuk[:, c, :], Wd["mla_w_uk"][l][c * 128:(c + 1) * 128, :], writes=[b_wu], war=[b_W])
                S.dma("pool", Wuv[:, c, :], Wd["mla_w_uv"][l][c * 128:(c + 1) * 128, :], writes=[b_wu], war=[b_W])
            cosD = wtab(0, [NBLK, 32], F32)
            sinD = wtab(NBLK * 128, [NBLK, 32], F32)
            cosM = wtab(NBLK * 256, [NBLK, 16], F32)
            sinM = wtab(NBLK * 320, [NBLK, 16], F32)
            gkv = wtab(NBLK * 384, [256], F32)
            gq = wtab(NBLK * 384 + 1024, [4], F32)
            b_tab = Buf()
            posi = aview(0, [NBLK], I32)
            posf = aview(1024, [NBLK], F32)
            ji = aview(2048, [32], I32)
            invf = aview(2304, [32], F32)
            ang = aview(4096, [NBLK, 32], F32)
            kf = aview(4096 + NBLK * 128, [NBLK, 32], F32)
            ki = aview(4096 + NBLK * 256, [NBLK, 32], I32)
            b_t = Buf()
            S.op("pool", lambda h: h.iota(posi[:, 0:NBP], [[128, NBP]], base=0, channel_multiplier=1), writes=[b_t], war=[b_A])
            S.op("pool", lambda h: h.iota(posi[0:64, NBP:NBLK], [[0, 1]], base=PAST, channel_multiplier=1), writes=[b_t])
            S.op("pool", lambda h: h.iota(posi[64:128, NBP:NBLK], [[0, 1]], base=PAST, channel_multiplier=1), writes=[b_t])
            S.op("dve", lambda h: h.tensor_copy(posf[:, :], posi[:, :]), reads=[b_t], writes=[b_t])
            S.op("pool", lambda h: h.iota(ji[:, :], [[1, 32]], base=0, channel_multiplier=0), writes=[b_t])
            S.dma("sp", gkv, Wd["mla_kv_norm_g"][l].partition_broadcast(128), writes=[b_tab], war=[b_W])
            load_fm(gq, Wd["mla_q_norm_g"][l].rearrange("(c p) -> c p", p=128), 4, b_tab)

            def build_tab(dst, half, is_cos):
                hw = half
                S.op("dve", lambda h: h.tensor_copy(invf[:, 0:hw], ji[:, 0:hw]), reads=[b_t], writes=[b_t])
                S.op("act", lambda h: h.activation(invf[:, 0:hw], invf[:, 0:hw], AF.Exp, scale=-math.log(THETA) / hw),
                     reads=[b_t], writes=[b_t])
                a3, k3, i3 = ang[:, :, 0:hw], kf[:, :, 0:hw], ki[:, :, 0:hw]
                S.op("dve", lambda h: h.tensor_tensor(a3, posf.unsqueeze(2).to_broadcast([128, NBLK, hw]),
                                                      invf[:, 0:hw].unsqueeze(1).to_broadcast([128, NBLK, hw]), ALU.mult),
                     reads=[b_t], writes=[b_t])
                if is_cos:
                    S.op("dve", lambda h: h.tensor_scalar_add(a3, a3, math.pi / 2), reads=[b_t], writes=[b_t])
                S.op("dve", lambda h: h.tensor_scalar_mul(k3, a3, 1.0 / TWO_PI), reads=[b_t], writes=[b_t])
                S.op("dve", lambda h: h.tensor_copy(i3, k3), reads=[b_t], writes=[b_t])
                S.op("dve", lambda h: h.tensor_copy(k3, i3), reads=[b_t], writes=[b_t])
                S.op("dve", lambda h: h.scalar_tensor_tensor(a3, k3, -TWO_PI, a3, ALU.mult, ALU.add), reads=[b_t], writes=[b_t])
                S.op("dve", lambda h: h.tensor_scalar(k3, a3, math.pi, -TWO_PI, ALU.is_gt, ALU.mult), reads=[b_t], writes=[b_t])
                S.op("dve", lambda h: h.tensor_tensor(a3, a3, k3, ALU.add), reads=[b_t], writes=[b_t])
                S.op("dve", lambda h: h.tensor_scalar(k3, a3, -math.pi, TWO_PI, ALU.is_lt, ALU.mult), reads=[b_t], writes=[b_t])
                S.op("dve", lambda h: h.tensor_tensor(a3, a3, k3, ALU.add), reads=[b_t], writes=[b_t])
                S.op("dve", lambda h: h.tensor_scalar(a3, a3, -math.pi, math.pi, ALU.max, ALU.min), reads=[b_t], writes=[b_t])
                S.op("act", lambda h: h.activation(dst, a3, AF.Sin), reads=[b_t], writes=[b_tab])
            build_tab(cosD, 32, True)
            build_tab(sinD, 32, False)
            build_tab(cosM, 16, True)
            build_tab(sinM, 16, False)
            S.barrier()

            xT = aview(0, [8, 512], F32)
            uT = aview(16384, [8, 512], BF16)
            sq = [aview(24576, [512], BF16), aview(25600, [512], BF16)]
            rs = aview(26624, [512], F32)
            tmp = [aview(28672, [512], F32), aview(30720, [512], F32)]
            tmA = aview(32768, [1536], F32)
            tmB = aview(38912, [1024], F32)
            tmC = aview(43008, [1024], F32)
            tbf = aview(47104, [1536], BF16)
            stg = [aview(50176, [8, 128], BF16), aview(52224, [8, 128], BF16)]
            cqn = aview(54272, [4, 512], BF16)
            ckvT = aview(58368, [2, 512], BF16)
            small = aview(60416, [64], F32)
            b_xT, b_uT, b_rs, b_tmA, b_tmB, b_tmC, b_tbf, b_cqn, b_ckvT, b_small = [Buf() for _ in range(10)]
            b_sq, b_tmp, b_stg = [Buf(), Buf()], [Buf(), Buf()], [Buf(), Buf()]
            stgc = [0]

            def evac_T(pb, nrows, ncol_groups, dsts, bf_src_buf):
                i = stgc[0] % 2
                stgc[0] += 1
                sg_, bs_ = stg[i], b_stg[i]
                pv = PS[pb].bitcast(BF16)[0:nrows, 0:ncol_groups * 128].rearrange("p (g n) -> p g n", g=ncol_groups)
                S.op("act", lambda h: h.copy(sg_[0:nrows, 0:ncol_groups, :], pv), reads=[bPS[pb]], writes=[bs_])
                for (dst, lo, hi) in dsts:
                    S.dma("sp", dst, sg_[0:nrows, 0:ncol_groups, lo:hi], reads=[bs_])

            def rope_tm(src, dst, G, half, cosb, sinb, blk, rd, wr, scale=None):
                cb = cosb[:, blk, :].unsqueeze(1).unsqueeze(1).to_broadcast([128, G, 2, half])
                sb_ = sinb[:, blk, :].unsqueeze(1).unsqueeze(1).to_broadcast([128, G, 2, half])
                s4 = src.rearrange("p g (t j) -> p g t j", t=2)
                d4 = dst.rearrange("p g (t j) -> p g t j", t=2)
                xc = tmC[:, 0:G * 2 * half].rearrange("p (g t j) -> p g t j", g=G, t=2)
                xs = tmB[:, 0:G * 2 * half].rearrange("p (g t j) -> p g t j", g=G, t=2)
                S.op("dve", lambda h: h.tensor_tensor(xc, s4, cb, ALU.mult), reads=rd + [b_tab], writes=[b_tmC])
                S.op("dve", lambda h: h.tensor_tensor(xs, s4, sb_, ALU.mult), reads=rd + [b_tab], writes=[b_tmB])
                if scale is None:
                    S.op("dve", lambda h: h.tensor_tensor(d4[:, :, 0, :], xc[:, :, 0, :], xs[:, :, 1, :], ALU.subtract),
                         reads=[b_tmC, b_tmB], writes=wr)
                    S.op("dve", lambda h: h.tensor_tensor(d4[:, :, 1, :], xc[:, :, 1, :], xs[:, :, 0, :], ALU.add),
                         reads=[b_tmC, b_tmB], writes=wr)
                else:
                    S.op("dve", lambda h: h.tensor_tensor(xc[:, :, 0, :], xc[:, :, 0, :], xs[:, :, 1, :], ALU.subtract),
                         reads=[b_tmC, b_tmB], writes=[b_tmC])
                    S.op("dve", lambda h: h.tensor_tensor(xc[:, :, 1, :], xc[:, :, 1, :], xs[:, :, 0, :], ALU.add),
                         reads=[b_tmC, b_tmB], writes=[b_tmC])
                    S.op("act", lambda h: h.activation(d4, xc, AF.Copy, scale=scale), reads=[b_tmC], writes=wr)

            def tm_matmul(pb, tb, col_lo, ncols, lhs, nk, W, wbufs):
                def mm(h):
                    ins = None
                    for c in range(nk):
                        ins = h.matmul(PS[pb][:, 0:ncols], lhs[:, c, tb * 128:(tb + 1) * 128], W[:, c, col_lo:col_lo + ncols],
                                       start=(c == 0), stop=(c == nk - 1))
                    return ins
                return mm

            def tile_B(tile):
                N, col0 = tile["n"], tile["col0"]
                NTB = N // 128
                prompt = tile["prompt"]
                S.dma("sp", xT[:, :, :N], XTv[:, :, col0:col0 + N], reads=[b_XT], writes=[b_xT], war=[b_A])
                norm_mod(tile, l, 1, xT, uT, b_xT, b_uT, sq, b_sq, rs, b_rs, tmp, b_tmp, 0)

                def out_rows(o_p, o_s, tb, width):
                    r0 = (col0 if prompt else 0) + tb * 128
                    return (o_p if prompt else o_s)[l, r0:r0 + 128, :]

                for tb in range(NTB):
                    blk = (col0 // 128 + tb) if prompt else NBP
                    tokc = col0 + tb * 128
                    S.op("pe", tm_matmul(1, tb, C_CKV, 288, uT, 8, Win, b_win), reads=b_win + [b_uT, b_W], writes=[bPS[1]])
                    S.op("act", lambda h: h.activation(tmB[:, 0:256], PS[1][:, 0:256], AF.Square, accum_out=small[:, 0:1]),
                         reads=[bPS[1]], writes=[b_tmB, b_small])
                    S.op("act", lambda h: h.activation(small[:, 1:2], small[:, 0:1], AF.Sqrt, bias=eps_t[:, 0:1], scale=1.0 / MLA_KV),
                         reads=[b_small, b_const], writes=[b_small])
                    S.op("dve", lambda h: h.reciprocal(small[:, 2:3], small[:, 1:2]), reads=[b_small], writes=[b_small])
                    S.op("dve", lambda h: h.scalar_tensor_tensor(tmA[:, 0:256], PS[1][:, 0:256], small[:, 2:3], gkv[:, :],
                                                                 ALU.mult, ALU.mult),
                         reads=[bPS[1], b_small, b_tab], writes=[b_tmA])
                    S.dma("sp", out_rows(o_ckv_p, o_ckv_s, tb, 256), tmA[:, 0:256], reads=[b_tmA])
                    S.op("act", lambda h: h.copy(tbf[:, 0:256], tmA[:, 0:256]), reads=[b_tmA], writes=[b_tbf])
                    rope_tm(PS[1][:, 256:288].rearrange("p (g d) -> p g d", g=1), tmA[:, 256:288].rearrange("p (g d) -> p g d", g=1),
                            1, 16, cosM, sinM, blk, [bPS[1]], [b_tmA])
                    S.dma("sp", out_rows(o_kpe_p, o_kpe_s, tb, 32), tmA[:, 256:288], reads=[b_tmA])
                    S.op("act", lambda h: h.copy(tbf[:, 256:288], tmA[:, 256:288]), reads=[b_tmA], writes=[b_tbf])
                    S.op("pe", lambda h, tb=tb: [h.transpose(PS[2].bitcast(BF16)[:, c * 128:(c + 1) * 128], tbf[:, c * 128:(c + 1) * 128], ident_bf[:])
                                                 for c in range(2)][-1], reads=[b_tbf, b_const], writes=[bPS[2]])
                    S.op("dve", lambda h, tb=tb: h.tensor_copy(ckvT[:, :, tb * 128:(tb + 1) * 128],
                                                                PS[2].bitcast(BF16)[:, 0:256].rearrange("p (c n) -> p c n", c=2)),
                         reads=[bPS[2]], writes=[b_ckvT])
                    S.op("pe", lambda h: h.transpose(PS[3].bitcast(BF16)[0:32, 0:128], tbf[:, 256:288], ident_bf[:]),
                         reads=[b_tbf, b_const], writes=[bPS[3]])
                    if prompt:
                        kp_d = [(KPT_p[:, tokc:tokc + 128].rearrange("r (g n) -> r g n", g=1), 0, 128)]
                    else:
                        kp_d = [(KPT_s[2 * tb + e, :, PAST:PAST + 64].rearrange("r (g n) -> r g n", g=1), e * 64, e * 64 + 64) for e in range(2)]
                    evac_T(3, 32, 1, kp_d, None)
                    for which, cbase, scale in (("q", C_DQ, DIFF_SCALE), ("k", C_DK, None)):
                        for half_ in range(2):
                            pb = 4 + half_
                            S.op("pe", tm_matmul(pb, tb, cbase + half_ * 512, 512, uT, 8, Win, b_win),
                                 reads=b_win + [b_uT, b_W], writes=[bPS[pb]])
                            rope_tm(PS[pb][:, :].rearrange("p (g d) -> p g d", g=8),
                                    tmA[:, half_ * 512:(half_ + 1) * 512].rearrange("p (g d) -> p g d", g=8),
                                    8, 32, cosD, sinD, blk, [bPS[pb]], [b_tmA], scale=scale)
                        if which == "k":
                            S.dma("sp", out_rows(o_dk_p, o_dk_s, tb, 1024), tmA[:, 0:1024], reads=[b_tmA])
                        S.op("act", lambda h: h.copy(tbf[:, 0:1024], tmA[:, 0:1024]), reads=[b_tmA], writes=[b_tbf])
                        S.op("pe", lambda h: [h.transpose(PS[6].bitcast(BF16)[:, c * 128:(c + 1) * 128], tbf[:, c * 128:(c + 1) * 128], ident_bf[:])
                                              for c in range(8)][-1], reads=[b_tbf, b_const], writes=[bPS[6]])
                        if which == "q":
                            dd = [(QTd[:, :, tokc:tokc + 128].rearrange("h p n -> p h n"), 0, 128)]
                        elif prompt:
                            dd = [(KTd_p[:, :, tokc:tokc + 128].rearrange("h p n -> p h n"), 0, 128)]
                        else:
                            dd = [(KTd_s[2 * tb + e, :, :, PAST:PAST + 64].rearrange("h p n -> p h n"), e * 64, e * 64 + 64) for e in range(2)]
                        evac_T(6, 128, 8, dd, None)
                    for half_ in range(2):
                        pb = 4 + half_
                        S.op("pe", tm_matmul(pb, tb, C_DV + half_ * 512, 512, uT, 8, Win, b_win),
                             reads=b_win + [b_uT, b_W], writes=[bPS[pb]])
                        S.op("act", lambda h, half_=half_, pb=pb: h.copy(tmA[:, half_ * 512:(half_ + 1) * 512], PS[pb][:, :]),
                             reads=[bPS[pb]], writes=[b_tmA])
                    S.dma("sp", out_rows(o_dv_p, o_dv_s, tb, 1024), tmA[:, 0:1024], reads=[b_tmA])
                    S.op("dve", lambda h: h.tensor_copy(tbf[:, 0:1024], tmA[:, 0:1024]), reads=[b_tmA], writes=[b_tbf])
                    if prompt:
                        S.dma("sp", Vd_p[tokc:tokc + 128, :], tbf[:, 0:1024], reads=[b_tbf])
                    else:
                        for e in range(2):
                            S.dma("sp", Vd_s[2 * tb + e, PAST:PAST + 64, :], tbf[e * 64:(e + 1) * 64, 0:1024], reads=[b_tbf])
                    for half_ in range(2):
                        pb = 4 + half_
                        S.op("pe", tm_matmul(pb, tb, half_ * 512, 512, ckvT, 2, Wuv, None),
                             reads=[b_wu, b_ckvT, b_W], writes=[bPS[pb]])
                        S.op("act", lambda h, half_=half_, pb=pb: h.copy(tbf[:, half_ * 512:(half_ + 1) * 512], PS[pb][:, :]),
                             reads=[bPS[pb]], writes=[b_tbf])
                    if prompt:
                        S.dma("sp", Vm_p[tokc:tokc + 128, :], tbf[:, 0:1024], reads=[b_tbf])
                    else:
                        for e in range(2):
                            S.dma("sp", Vm_s[2 * tb + e, PAST:PAST + 64, :], tbf[e * 64:(e + 1) * 64, 0:1024], reads=[b_tbf])

                for oc in range(8):
                    pb = 1 + oc % 2

                    def mmk(h, oc=oc, pb=pb):
                        ins = None
                        for c in range(2):
                            ins = h.matmul(PS[pb][:, :N], Wuk[:, c, oc * 128:(oc + 1) * 128], ckvT[:, c, :N], start=(c == 0), stop=(c == 1))
                        return ins
                    S.op("pe", mmk, reads=[b_wu, b_ckvT, b_W], writes=[bPS[pb]])
                    t_ = tmp[oc % 2].bitcast(BF16)
                    S.op("act", lambda h, pb=pb, t_=t_: h.copy(t_[:, :N], PS[pb][:, :N]), reads=[bPS[pb]], writes=[b_tmp[oc % 2]])
                    if prompt:
                        S.dma("sp", KTm_p[oc * 128:(oc + 1) * 128, col0:col0 + N], t_[:, :N], reads=[b_tmp[oc % 2]])
                    else:
                        for s_ in range(NS):
                            S.dma("sp", KTm_s[s_, oc * 128:(oc + 1) * 128, PAST:PAST + 64], t_[:, s_ * 64:(s_ + 1) * 64], reads=[b_tmp[oc % 2]])
                cqf = xT
                for c in range(4):
                    pb = 1 + c % 2

                    def mmq(h, c=c, pb=pb):
                        ins = None
                        for k_ in range(8):
                            ins = h.matmul(PS[pb][:, :N], Win[:, k_, C_CQ + c * 128:C_CQ + (c + 1) * 128], uT[:, k_, :N],
                                           start=(k_ == 0), stop=(k_ == 7))
                        return ins
                    S.op("pe", mmq, reads=b_win + [b_uT, b_W], writes=[bPS[pb]])
                    S.op("act", lambda h, c=c, pb=pb: h.copy(cqf[:, c, :N], PS[pb][:, :N]), reads=[bPS[pb]], writes=[b_xT])
                    S.op("act", lambda h, c=c, pb=pb: h.activation(sq[c % 2][:, :N], PS[pb][:, :N], AF.Square),
                         reads=[bPS[pb]], writes=[b_sq[c % 2]])
                    S.op("pe", lambda h, c=c: h.matmul(PS[0][:, :N], ones_bf[:], sq[c % 2][:, :N], start=(c == 0), stop=(c == 3)),
                         reads=[b_sq[c % 2], b_const], writes=[bPS[0]])
                S.op("act", lambda h: h.activation(rs[:, :N], PS[0][:, :N], AF.Sqrt, bias=eps_t[:, 0:1], scale=1.0 / MLA_Q),
                     reads=[bPS[0], b_const], writes=[b_rs])
                S.op("dve", lambda h: h.reciprocal(rs[:, :N], rs[:, :N]), reads=[b_rs], writes=[b_rs])
                for c in range(4):
                    S.op("dve", lambda h, c=c: h.scalar_tensor_tensor(cqn[:, c, :N], cqf[:, c, :N], gq[:, c:c + 1], rs[:, :N],
                                                                      ALU.mult, ALU.mult),
                         reads=[b_xT, b_rs, b_tab], writes=[b_cqn])
                for tb in range(NTB):
                    blk = (col0 // 128 + tb) if prompt else NBP
                    tokc = col0 + tb * 128
                    for j in range(3):
                        pb = 4 + j
                        S.op("pe", tm_matmul(pb, tb, j * 512, 512, cqn, 4, Wuq, None), reads=[b_wu, b_cqn, b_W], writes=[bPS[pb]])
                        S.op("act", lambda h, j=j, pb=pb: h.activation(tmA[:, j * 512:(j + 1) * 512], PS[pb][:, :], AF.Copy, scale=MLA_SCALE),
                             reads=[bPS[pb]], writes=[b_tmA])
                    q3 = tmA[:, 0:1536].rearrange("p (h d) -> p h d", h=16)
                    qb3 = tbf[:, 0:1536].rearrange("p (h d) -> p h d", h=16)
                    S.op("act", lambda h: h.copy(qb3[:, :, 0:64], q3[:, :, 0:64]), reads=[b_tmA], writes=[b_tbf])
                    rope_tm(q3[:, :, 64:96], qb3[:, :, 64:96], 16, 16, cosM, sinM, blk, [b_tmA], [b_tbf])
                    for g in range(2):
                        S.op("pe", lambda h, g=g: [h.transpose(PS[6 + g].bitcast(BF16)[0:96, hh * 128:(hh + 1) * 128],
                                                               tbf[:, (g * 8 + hh) * 96:(g * 8 + hh + 1) * 96], ident_bf[:])
                                                   for hh in range(8)][-1], reads=[b_tbf, b_const], writes=[bPS[6 + g]])
                        evac_T(6 + g, 96, 8, [(QTm[g * 8:(g + 1) * 8, :, tokc:tokc + 128].rearrange("h p n -> p h n"), 0, 128)], None)
                for c in range(8):
                    def mmg(h, c=c):
                        ins = None
                        for k_ in range(8):
                            ins = h.matmul(PS[1][:, :N], Win[:, k_, C_CV + c * 128:C_CV + (c + 1) * 128], uT[:, k_, :N],
                                           start=(k_ == 0), stop=(k_ == 7))
                        return ins

                    def mmb(h, c=c):
                        ins = None
                        for k_ in range(8):
                            ins = h.matmul(PS[2][:, :N], Win[:, k_, C_CV + 1024 + c * 128:C_CV + 1024 + (c + 1) * 128], uT[:, k_, :N],
                                           start=(k_ == 0), stop=(k_ == 7))
                        return ins
                    S.op("pe", mmg, reads=b_win + [b_uT, b_W], writes=[bPS[1]])
                    S.op("pe", mmb, reads=b_win + [b_uT, b_W], writes=[bPS[2]])
                    t_, bt_ = tmp[c % 2], b_tmp[c % 2]
                    S.op("act", lambda h, t_=t_: h.activation(t_[:, :N], PS[2][:, :N], AF.Sigmoid), reads=[bPS[2]], writes=[bt_])
                    S.op("dve", lambda h, t_=t_: h.tensor_tensor(t_[:, :N], t_[:, :N], PS[1][:, :N], ALU.mult), reads=[bt_, bPS[1]], writes=[bt_])
                    S.dma("sp", GLUT[c * 128:(c + 1) * 128, col0:col0 + N], t_[:, :N], reads=[bt_])


            KG = min(512, PAST)
            NKB = KG // 128
            cdk = [aview(0, [1024], BF16), aview(2048, [1024], BF16)]
            cdv = [aview(4096, [1024], BF16), aview(6144, [1024], BF16)]
            cck = [aview(8192, [288], BF16), aview(8192 + 576, [288], BF16)]
            ckvTc = aview(16384, [2, 512], BF16)
            b_cdk, b_cdv, b_cck = [Buf(), Buf()], [Buf(), Buf()], [Buf(), Buf()]
            b_ckc = Buf()
            cnt = [0]

            def prep_group(s_, g0):
                for kb in range(NKB):
                    k0 = g0 + kb * 128
                    i = cnt[0] % 2
                    cnt[0] += 1
                    S.dma("pool", cdk[i], c_dk[l, s_, k0:k0 + 128, :], writes=[b_cdk[i]])
                    S.dma("pool", cdv[i], c_dv[l, s_, k0:k0 + 128, :], writes=[b_cdv[i]])
                    S.dma("pool", cck[i][:, 0:256], c_ckv[l, s_, k0:k0 + 128, :], writes=[b_cck[i]])
                    S.dma("pool", cck[i][:, 256:288], c_kpe[l, s_, k0:k0 + 128, :], writes=[b_cck[i]])
                    S.dma("sp", Vd_s[s_, k0:k0 + 128, :], cdv[i], reads=[b_cdv[i]])
                    S.op("pe", lambda h, i=i: [h.transpose(PS[6].bitcast(BF16)[:, c * 128:(c + 1) * 128], cdk[i][:, c * 128:(c + 1) * 128], ident_bf[:])
                                               for c in range(8)][-1], reads=[b_cdk[i], b_const], writes=[bPS[6]])
                    evac_T(6, 128, 8, [(KTd_s[s_, :, :, k0:k0 + 128].rearrange("h p n -> p h n"), 0, 128)], None)
                    S.op("pe", lambda h, i=i: [h.transpose(PS[2].bitcast(BF16)[:, c * 128:(c + 1) * 128], cck[i][:, c * 128:(c + 1) * 128], ident_bf[:])
                                               for c in range(2)][-1], reads=[b_cck[i], b_const], writes=[bPS[2]])
                    S.op("dve", lambda h, kb=kb: h.tensor_copy(ckvTc[:, :, kb * 128:(kb + 1) * 128],
                                                                PS[2].bitcast(BF16)[:, 0:256].rearrange("p (c n) -> p c n", c=2)),
                         reads=[bPS[2]], writes=[b_ckc])
                    S.op("pe", lambda h, i=i: h.transpose(PS[3].bitcast(BF16)[0:32, 0:128], cck[i][:, 256:288], ident_bf[:]),
                         reads=[b_cck[i], b_const], writes=[bPS[3]])
                    evac_T(3, 32, 1, [(KPT_s[s_, :, k0:k0 + 128].rearrange("r (g n) -> r g n", g=1), 0, 128)], None)
                    for half_ in range(2):
                        pb = 4 + half_
                        S.op("pe", tm_matmul(pb, kb, half_ * 512, 512, ckvTc, 2, Wuv, None), reads=[b_wu, b_ckc, b_W], writes=[bPS[pb]])
                        S.op("act", lambda h, half_=half_, pb=pb: h.copy(tbf[:, half_ * 512:(half_ + 1) * 512], PS[pb][:, :]),
                             reads=[bPS[pb]], writes=[b_tbf])
                    S.dma("sp", Vm_s[s_, k0:k0 + 128, :], tbf[:, 0:1024], reads=[b_tbf])
                for oc in range(8):
                    pb = 1 if oc % 2 == 0 else 7

                    def mmk(h, oc=oc, pb=pb):
                        ins = None
                        for c in range(2):
                            ins = h.matmul(PS[pb][:, :KG], Wuk[:, c, oc * 128:(oc + 1) * 128], ckvTc[:, c, :KG], start=(c == 0), stop=(c == 1))
                        return ins
                    S.op("pe", mmk, reads=[b_wu, b_ckc, b_W], writes=[bPS[pb]])
                    t_ = tmp[oc % 2].bitcast(BF16)
                    S.op("act", lambda h, pb=pb, t_=t_: h.copy(t_[:, :KG], PS[pb][:, :KG]), reads=[bPS[pb]], writes=[b_tmp[oc % 2]])
                    S.dma("sp", KTm_s[s_, oc * 128:(oc + 1) * 128, g0:g0 + KG], t_[:, :KG], reads=[b_tmp[oc % 2]])

            for s_ in range(NS):
                for g0 in range(0, PAST, KG):
                    prep_group(s_, g0)
            S.barrier()

            for tile in tiles:
                tile_B(tile)
            S.barrier()


        def attn_phase(l):
            lam_init = 0.8 - 0.6 * math.exp(-0.3 * l)
            prm = aview(0, [4, 64], F32)
            pr2 = aview(1024, [16], F32)
            gsub = aview(1100, [1], F32)
            b_prm = Buf()
            for i, nm in enumerate(("diff_lq1", "diff_lk1", "diff_lq2", "diff_lk2")):
                S.dma("sp", prm[:, i, :], Wd[nm][l].partition_broadcast(128), writes=[b_prm], war=[b_A])
            load_fm(gsub[:, 0:1], Wd["diff_subln_g"][l].rearrange("(o p) -> o p", o=1), 1, b_prm)
            S.op("dve", lambda h: h.tensor_tensor(prm[:, 0, :], prm[:, 0, :], prm[:, 1, :], ALU.mult), reads=[b_prm], writes=[b_prm])
            S.op("dve", lambda h: h.tensor_tensor(prm[:, 2, :], prm[:, 2, :], prm[:, 3, :], ALU.mult), reads=[b_prm], writes=[b_prm])
            S.op("dve", lambda h: h.reduce_sum(pr2[:, 0:1], prm[:, 0, :], AX.X), reads=[b_prm], writes=[b_prm])
            S.op("dve", lambda h: h.reduce_sum(pr2[:, 1:2], prm[:, 2, :], AX.X), reads=[b_prm], writes=[b_prm])
            S.op("act", lambda h: h.activation(pr2[:, 2:4], pr2[:, 0:2], AF.Exp), reads=[b_prm], writes=[b_prm])
            S.op("dve", lambda h: h.scalar_tensor_tensor(pr2[:, 4:5], pr2[:, 3:4], -lam_init, pr2[:, 2:3], ALU.add, ALU.subtract),
                 reads=[b_prm], writes=[b_prm])
            S.op("dve", lambda h: h.tensor_scalar_mul(gsub[:, 0:1], gsub[:, 0:1], 1.0 - lam_init), reads=[b_prm], writes=[b_prm])
            neg_lam = pr2[:, 4:5]

            KMAX = max(T, KS)
            NKT = (KMAX + 127) // 128
            SET = 8448 + 8192 + 8448
            def hset(i):
                base = i * SET // 2
                kt_ = WREG[:, base: base + 4224]
                qt_ = WREG[:, base + 4224: base + 4224 + 4096]
                v_ = WREG[:, base + 8320: base + 8320 + 4224]
                return kt_, qt_, v_
            hs = [hset(0), hset(1)]
            b_hs = [[Buf() for _ in range(5)], [Buf() for _ in range(5)]]
            PT = [aview(2048 + i * 1024, [512], BF16) for i in range(4)]
            b_PT = [Buf() for _ in range(4)]
            acc = [aview(6144, [512], F32), aview(8192, [512], F32)]
            b_acc = [Buf(), Buf()]
            accB = [aview(21504, [512], F32), aview(23552, [512], F32)]
            b_accB = [Buf(), Buf()]
            tt = [aview(10240 + i * 2048, [512], F32) for i in range(4)]
            b_tt = [Buf() for _ in range(4)]
            sqd = aview(18432, [512], BF16)
            b_sqd = Buf()
            ost = [aview(19456, [512], BF16), aview(20480, [512], BF16)]
            b_ost = [Buf(), Buf()]
            ctr = dict(pt=0, sb=0, os=0, hs=0)

            seqs = [dict(q0=0, nq=T, QB=min(512, T), K=T, causal=True,
                         KTd=KTd_p, Vd=Vd_p, KTm=KTm_p, KPT=KPT_p, Vm=Vm_p)]
            for s_ in range(NS):
                seqs.append(dict(q0=T + s_ * TS, nq=TS, QB=TS, K=KS, causal=False,
                                 KTd=KTd_s[s_], Vd=Vd_s[s_], KTm=KTm_s[s_], KPT=KPT_s[s_], Vm=Vm_s[s_]))

            def load_head(kind, sq_, hd, i):
                kt_, qt_, v_ = hs[i]
                K, nq, q0 = sq_["K"], sq_["nq"], sq_["q0"]
                nfull = K // 128
                rem = K - nfull * 128
                if kind == "mla":
                    S.dma("sp", kt_[0:64, 0:K], sq_["KTm"][hd * 64:(hd + 1) * 64, :], writes=[b_hs[i][0]], war=[b_W])
                    S.dma("sp", kt_[64:96, 0:K], sq_["KPT"][:, :], writes=[b_hs[i][1]], war=[b_W])
                    S.dma("sp", qt_[0:96, 0:nq], QTm[hd, :, q0:q0 + nq], writes=[b_hs[i][2]], war=[b_W])
                    dv = 64
                    Vsrc = sq_["Vm"]
                else:
                    S.dma("sp", kt_[:, 0:K], sq_["KTd"][hd, :, :], writes=[b_hs[i][0], b_hs[i][1]], war=[b_W])
                    S.dma("sp", qt_[:, 0:nq], QTd[hd, :, q0:q0 + nq], writes=[b_hs[i][2]], war=[b_W])
                    dv = 128
                    Vsrc = sq_["Vd"]
                v3 = v_[:, 0:NKT * dv].rearrange("p (t d) -> p t d", d=dv)
                S.dma("sp", v3[:, 0:nfull, :], Vsrc[0:nfull * 128, hd * dv:(hd + 1) * dv].rearrange("(t p) d -> p t d", p=128),
                      writes=[b_hs[i][3]], war=[b_W])
                if rem:
                    S.dma("sp", v3[0:rem, nfull, :], Vsrc[nfull * 128:K, hd * dv:(hd + 1) * dv], writes=[b_hs[i][4]], war=[b_W])

            def attend_block(kind, sq_, hd, i, qb, maps):
                kt_, qt_, v_ = hs[i]
                K, QB = sq_["K"], sq_["QB"]
                dv = 64 if kind == "mla" else 128
                v3 = v_[:, 0:NKT * dv].rearrange("p (t d) -> p t d", d=dv)
                qc0 = qb * QB
                nq = QB
                if sq_["causal"]:
                    nkt = (qc0 + QB) // 128
                    diag0 = qc0 // 128
                else:
                    nkt = (K + 127) // 128
                    diag0 = nkt + 1
                for m in maps:
                    if kind == "mla":
                        r0, r1 = 0, 96
                    else:
                        r0, r1 = m * 64, (m + 1) * 64
                    ob, sb_, ac, bac = 2 + m, 4 + m, acc[m], b_acc[m]
                    ac2, bac2 = accB[m], b_accB[m]
                    SK = 2
                    stageA, stageB, stageC = [], [], []
                    for kt in range(nkt):
                        nk = min(128, K - kt * 128)
                        c_lo = (kt - diag0) * 128 if kt >= diag0 else 0
                        spb = (0, 1, 7)[ctr["sb"] % 3]
                        ctr["sb"] += 1
                        pi = ctr["pt"] % 4
                        ctr["pt"] += 1
                        pt_, bpt = PT[pi], b_PT[pi]

                        def fA(kt=kt, nk=nk, c_lo=c_lo, spb=spb, r0=r0, r1=r1):
                            S.op("pe", lambda h: h.matmul(
                                PS[spb][0:nk, c_lo:nq], kt_[r0:r1, kt * 128:kt * 128 + nk], qt_[r0:r1, qc0 + c_lo:qc0 + nq],
                                start=True, stop=True), reads=b_hs[i], writes=[bPS[spb]])

                        def fB(kt=kt, nk=nk, c_lo=c_lo, spb=spb, pt_=pt_, bpt=bpt, ac=ac, bac=bac, ac2=ac2, bac2=bac2):
                            S.op("act", lambda h: h.activation(pt_[0:nk, c_lo:nq], PS[spb][0:nk, c_lo:nq], AF.Exp),
                                 reads=[bPS[spb]], writes=[bpt])
                            if kt >= diag0 and nk == 128:
                                S.op("dve", lambda h: h.memset(pt_[64:128, c_lo:c_lo + 64], 0.0), reads=[bpt], writes=[bpt])
                            if kt == 0:
                                S.op("dve", lambda h: h.tensor_copy(ac[0:nk, 0:nq], pt_[0:nk, 0:nq]), reads=[bpt], writes=[bac])
                            elif kt == 1:
                                if c_lo > 0:
                                    S.op("pool", lambda h: h.memset(ac2[:, 0:c_lo], 0.0), writes=[bac2])
                                if nk < 128:
                                    S.op("pool", lambda h: h.memset(ac2[:, 0:nq], 0.0), writes=[bac2])
                                S.op("pool", lambda h: h.tensor_copy(ac2[0:nk, c_lo:nq], pt_[0:nk, c_lo:nq]), reads=[bpt], writes=[bac2])
                            elif kt % 2 == 0:
                                S.op("dve", lambda h: h.tensor_tensor(ac[0:nk, c_lo:nq], ac[0:nk, c_lo:nq], pt_[0:nk, c_lo:nq], ALU.add),
                                     reads=[bpt, bac], writes=[bac])
                            else:
                                S.op("pool", lambda h: h.tensor_tensor(ac2[0:nk, c_lo:nq], ac2[0:nk, c_lo:nq], pt_[0:nk, c_lo:nq], ALU.add),
                                     reads=[bpt, bac2], writes=[bac2])

                        def fC(kt=kt, nk=nk, c_lo=c_lo, pt_=pt_, bpt=bpt, ob=ob):
                            S.op("pe", lambda h: h.matmul(
                                PS[ob][0:dv, c_lo:nq], v3[0:nk, kt, :], pt_[0:nk, c_lo:nq], start=(kt == 0), stop=(kt == nkt - 1)),
                                reads=b_hs[i] + [bpt], writes=[bPS[ob]])
                        stageA.append(fA)
                        stageB.append(fB)
                        stageC.append(fC)
                    for step in range(nkt + SK):
                        if step < nkt:
                            stageA[step]()
                            stageB[step]()
                        if step - SK >= 0:
                            stageC[step - SK]()
                    if nkt > 1:
                        S.op("pe", lambda h, ac=ac, ac2=ac2, sb_=sb_: [
                            h.matmul(PS[sb_][0:dv, 0:nq], ones_f[:, 0:dv], ac[:, 0:nq], start=True, stop=False),
                            h.matmul(PS[sb_][0:dv, 0:nq], ones_f[:, 0:dv], ac2[:, 0:nq], start=False, stop=True)][-1],
                            reads=[bac, bac2, b_const], writes=[bPS[sb_]])
                    else:
                        S.op("pe", lambda h, ac=ac, sb_=sb_: h.matmul(PS[sb_][0:dv, 0:nq], ones_f[:, 0:dv], ac[:, 0:nq], start=True, stop=True),
                             reads=[bac, b_const], writes=[bPS[sb_]])
                    S.op("dve", lambda h, m=m, sb_=sb_: h.reciprocal(tt[m][0:dv, 0:nq], PS[sb_][0:dv, 0:nq]), reads=[bPS[sb_]], writes=[b_tt[m]])
                    S.op("dve", lambda h, m=m, ob=ob: h.tensor_tensor(tt[m][0:dv, 0:nq], tt[m][0:dv, 0:nq], PS[ob][0:dv, 0:nq], ALU.mult),
                         reads=[bPS[ob], b_tt[m]], writes=[b_tt[m]])
                oi = ctr["os"] % 2
                ctr["os"] += 1
                o_, bo_ = ost[oi], b_ost[oi]
                gq0 = sq_["q0"] + qc0
                if kind == "mla":
                    S.op("act", lambda h, o_=o_: h.copy(o_[0:64, 0:nq], tt[0][0:64, 0:nq]), reads=[b_tt[0]], writes=[bo_])
                    S.dma("sp", AOm[hd * 64:(hd + 1) * 64, gq0:gq0 + nq], o_[0:64, 0:nq], reads=[bo_])
                else:
                    S.op("dve", lambda h: h.scalar_tensor_tensor(tt[2][:, 0:nq], tt[1][:, 0:nq], neg_lam, tt[0][:, 0:nq], ALU.mult, ALU.add),
                         reads=[b_tt[0], b_tt[1], b_prm], writes=[b_tt[2]])
                    S.op("act", lambda h: h.activation(sqd[:, 0:nq], tt[2][:, 0:nq], AF.Square), reads=[b_tt[2]], writes=[b_sqd])
                    S.op("pe", lambda h: h.matmul(PS[6][:, 0:nq], ones_bf[:], sqd[:, 0:nq], start=True, stop=True),
                         reads=[b_sqd, b_const], writes=[bPS[6]])
                    S.op("act", lambda h: h.activation(tt[3][:, 0:nq], PS[6][:, 0:nq], AF.Sqrt, bias=eps_t[:, 0:1], scale=1.0 / 128),
                         reads=[bPS[6], b_const], writes=[b_tt[3]])
                    S.op("dve", lambda h: h.reciprocal(tt[3][:, 0:nq], tt[3][:, 0:nq]), reads=[b_tt[3]], writes=[b_tt[3]])
                    S.op("dve", lambda h, o_=o_: h.scalar_tensor_tensor(o_[:, 0:nq], tt[2][:, 0:nq], gsub[:, 0:1], tt[3][:, 0:nq], ALU.mult, ALU.mult),
                         reads=[b_tt[2], b_tt[3], b_prm], writes=[bo_])
                    S.dma("sp", AOd[hd * 128:(hd + 1) * 128, gq0:gq0 + nq], o_[:, 0:nq], reads=[bo_])

            work = []
            for sq_ in seqs:
                for hd in range(16):
                    work.append(("mla", sq_, hd))
                for hd in range(8):
                    work.append(("diff", sq_, hd))
            for wi, (kind, sq_, hd) in enumerate(work):
                if wi == 0:
                    load_head(kind, sq_, hd, 0)
                if wi + 1 < len(work):
                    k2, s2, h2 = work[wi + 1]
                    load_head(k2, s2, h2, (wi + 1) % 2)
                for qb in range(sq_["nq"] // sq_["QB"]):
                    attend_block(kind, sq_, hd, wi % 2, qb, [0] if kind == "mla" else [0, 1])
            S.barrier()


        def mixC_phase(l):
            Wg_ = wview(0, [8, 3072])
            Wmo = wview(24576, [8, 1024])
            Wdo = wview(32768, [8, 1024])
            Wpw = wview(40960, [8, 1024])
            Wou = wview(49152, [8, 1024])
            yb = wview(57344, [8, 512])
            mrg = wview(61440, [8, 512])
            PB0 = 65536 * 2

            def wpar(off, shape):
                n = int(np.prod(shape))
                a = WREG[:, (PB0 + off) // 2:(PB0 + off) // 2 + n * 2].bitcast(F32)
                if len(shape) == 2:
                    a = a.rearrange("p (a b) -> p a b", a=shape[0])
                return a
            wdw = wpar(0, [8, 32])
            bdw = wpar(1024, [8])
            lng = wpar(1056, [8])
            lnb = wpar(1088, [8])
            bpw = wpar(1120, [8])
            bgt = wpar(1152, [24])
            b_wm = [Buf() for _ in range(5)]
            b_par = Buf()
            for c in range(8):
                S.dma("pool", Wg_[:, c, :], Wd["w_branch_gate"][l][c * 128:(c + 1) * 128, :], writes=[b_wm[0]], war=[b_W])
            for wi_, (wv_, nm) in enumerate(((Wmo, "mla_w_o"), (Wdo, "diff_w_o"), (Wpw, "conv_w_pw2"), (Wou, "w_out"))):
                S.dma("pool", wv_, Wd[nm][l].rearrange("(c p) m -> p c m", p=128), writes=[b_wm[1 + wi_]], war=[b_W])
            for c in range(8):
                load_fm(wdw[:, c, 0:31], Wd["conv_w_dw"][l][:, c * 128:(c + 1) * 128], 31, b_par)
            load_fm(bdw, Wd["conv_b_dw"][l].rearrange("(c p) -> c p", p=128), 8, b_par)
            load_fm(lng, Wd["conv_ln_g"][l].rearrange("(c p) -> c p", p=128), 8, b_par)
            load_fm(lnb, Wd["conv_ln_b"][l].rearrange("(c p) -> c p", p=128), 8, b_par)
            load_fm(bpw, Wd["conv_b_pw2"][l].rearrange("(c p) -> c p", p=128), 8, b_par)
            load_fm(bgt, Wd["b_branch_gate"][l].rearrange("(c p) -> c p", p=128), 24, b_par)
            S.barrier()

            xT = aview(0, [8, 512], F32)
            uT = aview(16384, [8, 512], BF16)
            sq = [aview(24576, [512], BF16), aview(25600, [512], BF16)]
            rs = aview(26624, [512], F32)
            tmp = [aview(28672, [512], F32), aview(30720, [512], F32)]
            aom = aview(32768, [8, 512], BF16)
            aod = aview(40960, [8, 512], BF16)
            xin = [aview(49152, [544], F32), aview(49152 + 2176, [544], F32)]
            gt3 = aview(53504, [3, 512], BF16)
            xr = [aview(56576, [512], F32), aview(58624, [512], F32)]
            cst = aview(32768, [D], F32)
            b_xT, b_uT, b_rs, b_aom, b_aod, b_gt3, b_yb, b_mrg, b_cst = [Buf() for _ in range(9)]
            b_sq, b_tmp, b_xin, b_xr = [Buf(), Buf()], [Buf(), Buf()], [Buf(), Buf()], [Buf(), Buf()]
            b_cv = [Buf() for _ in range(8)]

            def tile_C(tile):
                N, col0, prompt = tile["n"], tile["col0"], tile["prompt"]
                segs = tile["segs"]
                nseg = len(segs)
                sl = segs[0][2]
                last_prompt = prompt and (col0 + N == T)
                S.dma("sp", xT[:, :, :N], XTv[:, :, col0:col0 + N], reads=[b_XT], writes=[b_xT], war=[b_A])
                norm_mod(tile, l, 1, xT, uT, b_xT, b_uT, sq, b_sq, rs, b_rs, tmp, b_tmp, 0)
                for c in range(8):
                    xi, bxi = xin[c % 2], b_xin[c % 2]
                    x3 = xi[:, 0:nseg * (CST + sl)].rearrange("p (s n) -> p s n", s=nseg)
                    if prompt:
                        if col0 == 0:
                            S.op("dve", lambda h, x3=x3: h.memset(x3[:, 0, 0:CST], 0.0), writes=[bxi])
                            S.dma("sp", x3[:, 0, CST:CST + N], GLUT[c * 128:(c + 1) * 128, 0:N], writes=[bxi])
                        else:
                            S.dma("sp", x3[:, 0, 0:CST + N], GLUT[c * 128:(c + 1) * 128, col0 - CST:col0 + N], writes=[bxi])
                    else:
                        S.dma("sp", x3[:, :, CST:CST + sl], GLUT[c * 128:(c + 1) * 128, col0:col0 + N].rearrange("p (s n) -> p s n", s=nseg),
                              writes=[bxi])
                        for si in range(nseg):
                            S.dma("sp", cst[0:CST, c * 128:(c + 1) * 128], c_conv[l, si, :, c * 128:(c + 1) * 128], writes=[b_cst])
                            S.op("pe", lambda h, c=c: h.transpose(PS[1][:, 0:CST], cst[0:CST, c * 128:(c + 1) * 128], ident_f[0:CST, 0:CST]),
                                 reads=[b_cst, b_const], writes=[bPS[1]])
                            S.op("act", lambda h, x3=x3, si=si: h.copy(x3[:, si, 0:CST], PS[1][:, 0:CST]), reads=[bPS[1]], writes=[bxi])
                    cv = xT[:, c, :N].rearrange("p (s n) -> p s n", s=nseg)
                    ceng = "dve"
                    S.op(ceng, lambda h, c=c, x3=x3, cv=cv: h.tensor_scalar(cv, x3[:, :, 0:sl], wdw[:, c, 0:1], bdw[:, c:c + 1], ALU.mult, ALU.add),
                         reads=[bxi, b_par], writes=[b_cv[c]], war=[b_xT])
                    for k_ in range(1, CONVW):
                        S.op(ceng, lambda h, c=c, k_=k_, x3=x3, cv=cv: h.scalar_tensor_tensor(
                            cv, x3[:, :, k_:k_ + sl], wdw[:, c, k_:k_ + 1], cv, ALU.mult, ALU.add),
                            reads=[bxi, b_par, b_cv[c]], writes=[b_cv[c]])
                    if last_prompt or not prompt:
                        for si in range(nseg):
                            S.op("pe", lambda h, x3=x3, si=si: h.transpose(PS[2][0:32, 0:128], x3[:, si, CST + sl - 32:CST + sl], ident_f[:]),
                                 reads=[bxi, b_const], writes=[bPS[2]])
                            t_, bt_ = tmp[si % 2], b_tmp[si % 2]
                            S.op("act", lambda h, t_=t_: h.copy(t_[0:32, 0:128], PS[2][0:32, 0:128]), reads=[bPS[2]], writes=[bt_])
                            dst = o_conv_p[l, :, c * 128:(c + 1) * 128] if prompt else o_conv_s[l, si, :, c * 128:(c + 1) * 128]
                            S.dma("sp", dst, t_[2:32, 0:128], reads=[bt_])
                    S.op("act", lambda h, c=c: h.copy(sq[0][:, :N], xT[:, c, :N]), reads=[b_cv[c]], writes=[b_sq[0]])
                    S.op("act", lambda h, c=c: h.activation(sq[1][:, :N], xT[:, c, :N], AF.Square), reads=[b_cv[c]], writes=[b_sq[1]])
                    S.op("pe", lambda h, c=c: h.matmul(PS[3][:, :N], ones_bf[:], sq[0][:, :N], start=(c == 0), stop=(c == 7)),
                         reads=[b_sq[0], b_const], writes=[bPS[3]])
                    S.op("pe", lambda h, c=c: h.matmul(PS[4][:, :N], ones_bf[:], sq[1][:, :N], start=(c == 0), stop=(c == 7)),
                         reads=[b_sq[1], b_const], writes=[bPS[4]])
                nmu, var_ = tmp[0], tmp[1]
                S.op("act", lambda h: h.activation(nmu[:, :N], PS[3][:, :N], AF.Copy, scale=-1.0 / D), reads=[bPS[3]], writes=[b_tmp[0]])
                S.op("dve", lambda h: h.tensor_tensor(var_[:, :N], nmu[:, :N], nmu[:, :N], ALU.mult), reads=[b_tmp[0]], writes=[b_tmp[1]])
                S.op("dve", lambda h: h.scalar_tensor_tensor(var_[:, :N], PS[4][:, :N], 1.0 / D, var_[:, :N], ALU.mult, ALU.subtract),
                     reads=[bPS[4], b_tmp[1]], writes=[b_tmp[1]])
                S.op("dve", lambda h: h.tensor_scalar_max(var_[:, :N], var_[:, :N], 0.0), reads=[b_tmp[1]], writes=[b_tmp[1]])
                S.op("act", lambda h: h.activation(rs[:, :N], var_[:, :N], AF.Sqrt, bias=eps_t[:, 0:1], scale=1.0),
                     reads=[b_tmp[1], b_const], writes=[b_rs])
                S.op("dve", lambda h: h.reciprocal(rs[:, :N], rs[:, :N]), reads=[b_rs], writes=[b_rs])
                for c in range(8):
                    S.op("dve", lambda h, c=c: h.tensor_tensor(xT[:, c, :N], xT[:, c, :N], nmu[:, :N], ALU.add), reads=[b_cv[c], b_tmp[0]], writes=[b_cv[c]])
                    S.op("dve", lambda h, c=c: h.tensor_tensor(xT[:, c, :N], xT[:, c, :N], rs[:, :N], ALU.mult), reads=[b_cv[c], b_rs], writes=[b_cv[c]])
                    S.op("act", lambda h, c=c: h.activation(yb[:, c, :N], xT[:, c, :N], AF.Silu, bias=lnb[:, c:c + 1], scale=lng[:, c:c + 1]),
                         reads=[b_cv[c], b_par], writes=[b_yb])
                S.dma("sp", aom[:, :, :N], AOm.rearrange("(c p) n -> p c n", p=128)[:, :, col0:col0 + N], writes=[b_aom])
                S.dma("sp", aod[:, :, :N], AOd.rearrange("(c p) n -> p c n", p=128)[:, :, col0:col0 + N], writes=[b_aod])
                for oc in range(8):
                    for gi in range(3):
                        def mmgate(h, gi=gi, oc=oc):
                            ins = None
                            for k_ in range(8):
                                ins = h.matmul(PS[1 + gi][:, :N], Wg_[:, k_, (gi * 8 + oc) * 128:(gi * 8 + oc + 1) * 128], uT[:, k_, :N],
                                               start=(k_ == 0), stop=(k_ == 7))
                            return ins
                        S.op("pe", mmgate, reads=[b_wm[0], b_uT, b_W], writes=[bPS[1 + gi]])
                        S.op("act", lambda h, gi=gi, oc=oc: h.activation(gt3[:, gi, :N], PS[1 + gi][:, :N], AF.Sigmoid,
                                                                         bias=bgt[:, gi * 8 + oc:gi * 8 + oc + 1], scale=1.0),
                             reads=[bPS[1 + gi], b_par], writes=[b_gt3])
                    for bi_, (wv_, src, bsrc) in enumerate(((Wmo, aom, b_aom), (Wdo, aod, b_aod), (Wpw, yb, b_yb))):
                        def mmbr(h, bi_=bi_, wv_=wv_, src=src, oc=oc):
                            ins = None
                            for k_ in range(8):
                                ins = h.matmul(PS[4 + bi_][:, :N], wv_[:, k_, oc * 128:(oc + 1) * 128], src[:, k_, :N],
                                               start=(k_ == 0), stop=(k_ == 7))
                            return ins
                        S.op("pe", mmbr, reads=[b_wm[1 + bi_], bsrc, b_W], writes=[bPS[4 + bi_]])
                    t0, t1 = tmp[0], tmp[1]
                    S.op("dve", lambda h: h.tensor_tensor(t0[:, :N], gt3[:, 0, :N], PS[4][:, :N], ALU.mult), reads=[b_gt3, bPS[4]], writes=[b_tmp[0]])
                    S.op("dve", lambda h: h.tensor_tensor(t1[:, :N], gt3[:, 1, :N], PS[5][:, :N], ALU.mult), reads=[b_gt3, bPS[5]], writes=[b_tmp[1]])
                    S.op("dve", lambda h: h.tensor_tensor(t0[:, :N], t0[:, :N], t1[:, :N], ALU.add), reads=[b_tmp[0], b_tmp[1]], writes=[b_tmp[0]])
                    S.op("dve", lambda h, oc=oc: h.scalar_tensor_tensor(t1[:, :N], PS[6][:, :N], bpw[:, oc:oc + 1], gt3[:, 2, :N], ALU.add, ALU.mult),
                         reads=[bPS[6], b_par, b_gt3], writes=[b_tmp[1]])
                    S.op("dve", lambda h, oc=oc: h.tensor_tensor(mrg[:, oc, :N], t0[:, :N], t1[:, :N], ALU.add), reads=[b_tmp[0], b_tmp[1]], writes=[b_mrg])
                for oc in range(8):
                    py = 1 + (oc % 6)

                    def mmo(h, oc=oc, py=py):
                        ins = None
                        for k_ in range(8):
                            ins = h.matmul(PS[py][:, :N], Wou[:, k_, oc * 128:(oc + 1) * 128], mrg[:, k_, :N], start=(k_ == 0), stop=(k_ == 7))
                        return ins
                    S.op("pe", mmo, reads=[b_wm[4], b_mrg, b_W], writes=[bPS[py]])
                    S.op("act", lambda h, oc=oc, py=py: h.copy(xT[:, oc, :N], PS[py][:, :N]), reads=[bPS[py]], writes=[b_xT], war=b_cv)
                    S.op("act", lambda h, oc=oc, py=py: h.activation(sq[oc % 2][:, :N], PS[py][:, :N], AF.Square), reads=[bPS[py]], writes=[b_sq[oc % 2]])
                    S.op("pe", lambda h, oc=oc: h.matmul(PS[7][:, :N], ones_bf[:], sq[oc % 2][:, :N], start=(oc == 0), stop=(oc == 7)),
                         reads=[b_sq[oc % 2], b_const], writes=[bPS[7]])
                post_norm_residual(tile, l, 1, xT, b_xT, xr, b_xr, rs, b_rs, tmp, b_tmp, 7)

            for tile in tiles:
                tile_C(tile)
            S.barrier()

        def store_y_phase():
            xi = [aview(0, [8, 128], F32), aview(4096, [8, 128], F32)]
            b_xi = [Buf(), Buf()]
            xo = [aview(8192, [D], F32), aview(8192 + 4096, [D], F32)]
            b_xo = [Buf(), Buf()]
            blocks = [(y_p, i * 128, i * 128) for i in range(T // 128)] + \
                     [(y_s, i * 128, T + i * 128) for i in range(NSTOK // 128)]
            for bi, (dst, r0, col) in enumerate(blocks):
                xi_, bx = xi[bi % 2], b_xi[bi % 2]
                S.dma("sp", xi_, XTv[:, :, col:col + 128], reads=[b_XT], writes=[bx], war=[b_A])
                o, bo = xo[bi % 2], b_xo[bi % 2]
                for half in range(2):
                    pb = 2 * (bi % 2) + half
                    S.op("pe", lambda h, xi_=xi_, half=half, pb=pb: [
                        h.transpose(PS[pb][:, c * 128:(c + 1) * 128], xi_[:, half * 4 + c, :], ident_f[:])
                        for c in range(4)][-1], reads=[bx, b_const], writes=[bPS[pb]])
                    if half == 0:
                        S.op("act", lambda h, o=o, pb=pb: h.copy(o[:, 0:512], PS[pb][:, :]), reads=[bPS[pb]], writes=[bo])
                    else:
                        S.op("dve", lambda h, o=o, pb=pb: h.tensor_copy(o[:, 512:1024], PS[pb][:, :]), reads=[bPS[pb]], writes=[bo])
                S.dma("sp", dst[r0:r0 + 128, :], o, reads=[bo])

        ada_phase()
        S.barrier()
        load_x_phase()
        S.barrier()
        for l in range(L):
            if "ffn1" in cfg.phases:
                ffn_phase(l, 0)
            if "mixB" in cfg.phases or "mix" in cfg.phases:
                mixB_phase(l)
            if "attn" in cfg.phases or "mix" in cfg.phases:
                attn_phase(l)
            if "mixC" in cfg.phases or "mix" in cfg.phases:
                mixC_phase(l)
            if "ffn2" in cfg.phases:
                ffn_phase(l, 1)
        if cfg.dbg:
            S.dma("sp", dbg_out["dbg_xt"], XT, reads=[b_XT])
        S.barrier()
        store_y_phase()
        S.finish()
        S.emit()
        print("sched: ops=%d waits=%d" % (S.n_ops, S.n_wait))
    return nc


def make_in_maps(inputs, cfg, ncores):
    NS, T = cfg.NS, cfg.T
    maps = []
    for b in range(ncores):
        m = {}
        m["x_prompt"] = np.ascontiguousarray(inputs["x_prompt"][b])
        m["x_sample"] = np.ascontiguousarray(inputs["x_sample"][b * NS:(b + 1) * NS]).reshape(NS * TS, D)
        m["c_all"] = np.ascontiguousarray(
            np.concatenate([inputs["c_prompt"][b:b + 1], inputs["c_sample"][b * NS:(b + 1) * NS]], axis=0))
        m["cache_mla_ckv"] = np.ascontiguousarray(inputs["cache_mla_ckv"][:, b * NS:(b + 1) * NS])
        m["cache_mla_kpe"] = np.ascontiguousarray(inputs["cache_mla_kpe"][:, b * NS:(b + 1) * NS])
        m["cache_diff_k"] = np.ascontiguousarray(inputs["cache_diff_k"][:, b * NS:(b + 1) * NS]).reshape(
            DEPTH, NS, cfg.PAST, 1024)
        m["cache_diff_v"] = np.ascontiguousarray(inputs["cache_diff_v"][:, b * NS:(b + 1) * NS]).reshape(
            DEPTH, NS, cfg.PAST, 1024)
        m["state_conv"] = np.ascontiguousarray(inputs["state_conv"][:, b * NS:(b + 1) * NS])
        for name, _ in WEIGHT_SPECS:
            m[name] = np.ascontiguousarray(inputs[name])
        maps.append(m)
    return maps


def gather_outputs(results, cfg, ncores):
    NS, T, L = cfg.NS, cfg.T, DEPTH

    def stk(key, shape_fn=None, axis=0):
        return [r[key] for r in results]
    y_p = np.stack([r["y_prompt"] for r in results], 0)
    y_s = np.concatenate([r["y_sample"].reshape(NS, TS, D) for r in results], 0)
    ckv_p = np.stack([r["o_ckv_p"] for r in results], 1)
    kpe_p = np.stack([r["o_kpe_p"] for r in results], 1)
    dk_p = np.stack([r["o_dk_p"].reshape(L, T, DIFF_H, 2, DIFF_HD) for r in results], 1)
    dv_p = np.stack([r["o_dv_p"].reshape(L, T, DIFF_H, 128) for r in results], 1)
    conv_p = np.stack([r["o_conv_p"] for r in results], 1)
    ckv_s = np.concatenate([r["o_ckv_s"].reshape(L, NS, TS, MLA_KV) for r in results], 1)
    kpe_s = np.concatenate([r["o_kpe_s"].reshape(L, NS, TS, MLA_ROPE) for r in results], 1)
    dk_s = np.concatenate([r["o_dk_s"].reshape(L, NS, TS, DIFF_H, 2, DIFF_HD) for r in results], 1)
    dv_s = np.concatenate([r["o_dv_s"].reshape(L, NS, TS, DIFF_H, 128) for r in results], 1)
    conv_s = np.concatenate([r["o_conv_s"] for r in results], 1)
    return (y_p, y_s, ckv_p, kpe_p, dk_p, dv_p, conv_p, ckv_s, kpe_s, dk_s, dv_s, conv_s)


def kernel(**inputs):
    cfg = Cfg()
    ncores = 8
    nc = build_program(cfg)
    in_maps = make_in_maps(inputs, cfg, ncores)
    res = run_bass_kernel_spmd(nc, in_maps, core_ids=list(range(ncores)))
    outs = gather_outputs(res.results, cfg, ncores)
    return tuple(np.ascontiguousarray(o, dtype=np.float32) for o in outs)
```

```python
import math
from contextlib import ExitStack

import numpy as np
import concourse.bass as bass
import concourse.mybir as mybir
from concourse.bass_utils import run_bass_kernel_spmd

F32 = mybir.dt.float32
BF16 = mybir.dt.bfloat16
I32 = mybir.dt.int32
AF = mybir.ActivationFunctionType
ALU = mybir.AluOpType
AX = mybir.AxisListType

D = 1024
DFF = 2816
NADA = 9
CHUNK = 64
EPS = 1e-6
THETA = 10000.0
MLA_H, MLA_NOPE, MLA_ROPE, MLA_V, MLA_KV, MLA_Q = 16, 64, 32, 64, 256, 512
MLA_SCALE = (MLA_NOPE + MLA_ROPE) ** -0.5
DIFF_H, DIFF_HD = 8, 64
DIFF_SCALE = DIFF_HD ** -0.5
CONVW = 31
CST = 30
DIN = 5920
C_CQ, C_CKV, C_KPE, C_DQ, C_DK, C_DV, C_CV = 0, 512, 768, 800, 1824, 2848, 3872
TS = 64
DEPTH = 2


class Buf:
    __slots__ = ("name", "w", "r")

    def __init__(self, name=""):
        self.name = name
        self.w = None
        self.r = {}


class Sched:
    ENGS = ("pe", "act", "dve", "pool", "sp")

    def __init__(self, nc, stack, n_dma_sems=14):
        self.nc = nc
        self.sems = {}
        self.cnt = {}
        self.ops = {e: [] for e in self.ENGS}
        self.seen = {e: {} for e in self.ENGS}
        for e in ("pe", "act", "dve", "pool"):
            k = "E_" + e
            self.sems[k] = stack.enter_context(nc.semaphore(k))
            self.cnt[k] = 0
        self.dma_pool = {}
        self.dma_rr = {}
        for q in ("sp", "pool", "act"):
            keys = []
            for i in range(n_dma_sems):
                k = "D_%s_%d" % (q, i)
                self.sems[k] = stack.enter_context(nc.semaphore(k))
                self.cnt[k] = 0
                keys.append(k)
            self.dma_pool[q] = keys
            self.dma_rr[q] = 0
        self.n_wait = 0
        self.n_ops = 0

    def _need(self, eng, k, v, deng, deps):
        if eng == "pe" and deng == "pe":
            return
        if self.seen[eng].get(k, 0) >= v:
            return
        if deps.get(k, 0) < v:
            deps[k] = v

    def _collect(self, eng, reads, writes, war):
        deps = {}
        for b in reads:
            if b.w is not None:
                self._need(eng, b.w[0], b.w[1], b.w[2], deps)
        for b in writes:
            if b.w is not None:
                self._need(eng, b.w[0], b.w[1], b.w[2], deps)
            for k, (v, de) in b.r.items():
                self._need(eng, k, v, de, deps)
        for b in war:
            for k, (v, de) in b.r.items():
                self._need(eng, k, v, de, deps)
        return deps

    def _emit_waits(self, eng, deps):
        for k, v in deps.items():
            sem = self.sems[k]
            self.ops[eng].append(lambda h, sem=sem, v=v: h.wait_ge(sem, v))
            self.seen[eng][k] = v
            self.n_wait += 1

    def _commit(self, ident, reads, writes):
        k, v, e = ident
        for b in reads:
            b.r[k] = (v, e)
        for b in writes:
            b.w = ident
            b.r = {}

    def op(self, eng, fn, reads=(), writes=(), war=()):
        deps = self._collect(eng, reads, writes, war)
        self._emit_waits(eng, deps)
        k = "E_" + eng
        self.cnt[k] += 1
        v = self.cnt[k]
        sem = self.sems[k]
        self.ops[eng].append(lambda h, fn=fn, sem=sem: fn(h).then_inc(sem, 1))
        self._commit((k, v, eng), reads, writes)
        self.n_ops += 1

    def dma(self, q, out, in_, reads=(), writes=(), war=(), **kw):
        deps = self._collect(q, reads, writes, war)
        pool = self.dma_pool[q]
        k = pool[self.dma_rr[q] % len(pool)]
        self.dma_rr[q] += 1
        pv = self.cnt[k]
        if pv > 0 and self.seen[q].get(k, 0) < pv:
            deps[k] = max(deps.get(k, 0), pv)
        self._emit_waits(q, deps)
        self.cnt[k] += 16
        v = self.cnt[k]
        sem = self.sems[k]
        self.ops[q].append(
            lambda h, out=out, in_=in_, sem=sem, kw=kw: h.dma_start(out=out, in_=in_, **kw).then_inc(sem, 16))
        self._commit((k, v, q), reads, writes)
        self.n_ops += 1

    def barrier(self):
        for eng in self.ENGS:
            deps = {}
            for k, c in self.cnt.items():
                if c > 0 and self.seen[eng].get(k, 0) < c and not (eng == "pe" and k == "E_pe"):
                    deps[k] = c
            self._emit_waits(eng, deps)

    def finish(self):
        deps = {}
        for q, keys in self.dma_pool.items():
            for k in keys:
                if self.cnt[k] > 0 and self.seen["sp"].get(k, 0) < self.cnt[k]:
                    deps[k] = self.cnt[k]
        for e in ("pe", "act", "dve", "pool"):
            k = "E_" + e
            if self.cnt[k] > 0 and self.seen["sp"].get(k, 0) < self.cnt[k]:
                deps[k] = self.cnt[k]
        self._emit_waits("sp", deps)

    def emit(self):
        nc = self.nc
        with nc.Block() as block:
            @block.sync
            def _(h):
                for f in self.ops["sp"]:
                    f(h)

            @block.tensor
            def _(h):
                for f in self.ops["pe"]:
                    f(h)

            @block.scalar
            def _(h):
                for f in self.ops["act"]:
                    f(h)

            @block.vector
            def _(h):
                for f in self.ops["dve"]:
                    f(h)

            @block.gpsimd
            def _(h):
                for f in self.ops["pool"]:
                    f(h)


WEIGHT_SPECS = [
    ("ada_w", [D, NADA * D]), ("ada_b", [NADA * D]),
    ("ffn1_pre_g", [D]), ("ffn1_post_g", [D]),
    ("ffn1_w_gate", [D, DFF]), ("ffn1_w_up", [D, DFF]), ("ffn1_w_down", [DFF, D]),
    ("mix_pre_g", [D]), ("mix_post_g", [D]), ("w_in", [D, DIN]),
    ("mla_q_norm_g", [MLA_Q]), ("mla_w_uq", [MLA_Q, MLA_H * 96]), ("mla_kv_norm_g", [MLA_KV]),
    ("mla_w_uk", [MLA_KV, MLA_H * 64]), ("mla_w_uv", [MLA_KV, MLA_H * 64]), ("mla_w_o", [D, D]),
    ("diff_lq1", [64]), ("diff_lk1", [64]), ("diff_lq2", [64]), ("diff_lk2", [64]),
    ("diff_subln_g", [128]), ("diff_w_o", [D, D]),
    ("conv_w_dw", [CONVW, D]), ("conv_b_dw", [D]), ("conv_ln_g", [D]), ("conv_ln_b", [D]),
    ("conv_w_pw2", [D, D]), ("conv_b_pw2", [D]),
    ("w_branch_gate", [D, 3 * D]), ("b_branch_gate", [3 * D]), ("w_out", [D, D]),
    ("ffn2_pre_g", [D]), ("ffn2_post_g", [D]),
    ("ffn2_w_gate", [D, DFF]), ("ffn2_w_up", [D, DFF]), ("ffn2_w_down", [DFF, D]),
]


class Cfg:
    def __init__(self, T=4096, NS=4, PAST=4096, L=2, phases=("ffn1", "mix", "ffn2"), dbg=False):
        self.T, self.NS, self.PAST, self.L = T, NS, PAST, L
        self.phases = phases
        self.dbg = dbg
        self.NSTOK = NS * TS
        self.NTOK = T + self.NSTOK
        self.NR = 1 + NS
        self.KS = PAST + TS


def build_program(cfg):
    T, NS, PAST, L = cfg.T, cfg.NS, cfg.PAST, cfg.L
    NTOK, NSTOK, NR, KS = cfg.NTOK, cfg.NSTOK, cfg.NR, cfg.KS
    nc = bass.Bass("TRN2", target_bir_lowering=False)

    def din(name, shape, dt=F32):
        return nc.dram_tensor(name, list(shape), dt, kind="ExternalInput").ap()

    def dout(name, shape, dt=F32):
        return nc.dram_tensor(name, list(shape), dt, kind="ExternalOutput").ap()

    def dscr(name, shape, dt):
        return nc.dram_tensor(name, list(shape), dt).ap()

    x_p = din("x_prompt", [T, D])
    x_s = din("x_sample", [NSTOK, D])
    c_all = din("c_all", [NR, D])
    c_ckv = din("cache_mla_ckv", [DEPTH, NS, PAST, MLA_KV])
    c_kpe = din("cache_mla_kpe", [DEPTH, NS, PAST, MLA_ROPE])
    c_dk = din("cache_diff_k", [DEPTH, NS, PAST, 1024])
    c_dv = din("cache_diff_v", [DEPTH, NS, PAST, 1024])
    c_conv = din("state_conv", [DEPTH, NS, CST, D])
    Wd = {}
    for name, shp in WEIGHT_SPECS:
        Wd[name] = din(name, [2] + shp)

    y_p = dout("y_prompt", [T, D])
    y_s = dout("y_sample", [NSTOK, D])
    o_ckv_p = dout("o_ckv_p", [DEPTH, T, MLA_KV])
    o_kpe_p = dout("o_kpe_p", [DEPTH, T, MLA_ROPE])
    o_dk_p = dout("o_dk_p", [DEPTH, T, 1024])
    o_dv_p = dout("o_dv_p", [DEPTH, T, 1024])
    o_conv_p = dout("o_conv_p", [DEPTH, CST, D])
    o_ckv_s = dout("o_ckv_s", [DEPTH, NSTOK, MLA_KV])
    o_kpe_s = dout("o_kpe_s", [DEPTH, NSTOK, MLA_ROPE])
    o_dk_s = dout("o_dk_s", [DEPTH, NSTOK, 1024])
    o_dv_s = dout("o_dv_s", [DEPTH, NSTOK, 1024])
    o_conv_s = dout("o_conv_s", [DEPTH, NS, CST, D])

    XT = dscr("XT", [D, NTOK], F32)
    NSP = max(NS, 1)
    QTd = dscr("QTd", [8, 128, NTOK], BF16)
    KTd_p = dscr("KTd_p", [8, 128, T], BF16)
    KTd_s = dscr("KTd_s", [NSP, 8, 128, KS], BF16)
    Vd_p = dscr("Vd_p", [T, 1024], BF16)
    Vd_s = dscr("Vd_s", [NSP, KS, 1024], BF16)
    QTm = dscr("QTm", [16, 96, NTOK], BF16)
    KTm_p = dscr("KTm_p", [1024, T], BF16)
    KTm_s = dscr("KTm_s", [NSP, 1024, KS], BF16)
    KPT_p = dscr("KPT_p", [32, T], BF16)
    KPT_s = dscr("KPT_s", [NSP, 32, KS], BF16)
    Vm_p = dscr("Vm_p", [T, 1024], BF16)
    Vm_s = dscr("Vm_s", [NSP, KS, 1024], BF16)
    GLUT = dscr("GLUT", [D, NTOK], F32)
    AOm = dscr("AOm", [D, NTOK], BF16)
    AOd = dscr("AOd", [D, NTOK], BF16)
    b_XT = Buf("XT")
    b_XT2 = Buf("XT_st")
    dbg_out = {}
    if cfg.dbg:
        dbg_out["dbg_xt"] = dout("dbg_xt", [D, NTOK])

    with ExitStack() as st:
        S = Sched(nc, st)
        dumped = set()

        def dbg_dump(name, ap, shape, dt, reads):
            if not cfg.dbg or name in dumped:
                return
            dumped.add(name)
            o = dout(name, shape, dt)
            S.dma("sp", o, ap, reads=reads)

        def sb(name, shape, dt):
            return st.enter_context(nc.sbuf_tensor(name, list(shape), dt))

        PS = [st.enter_context(nc.psum_tensor("ps%d" % i, [128, 512], F32)) for i in range(8)]
        bPS = [Buf("ps%d" % i) for i in range(8)]

        ones_bf = sb("ones_bf", [128, 128], BF16)
        ones_f = sb("ones_f", [128, 128], F32)
        ident_f = sb("ident_f", [128, 128], F32)
        ident_bf = sb("ident_bf", [128, 128], BF16)
        eps_t = sb("eps_t", [128, 1], F32)
        b_const = Buf("const")
        S.op("dve", lambda h: h.memset(ones_f[:], 1.0), writes=[b_const])
        S.op("dve", lambda h: h.memset(ones_bf[:], 1.0), writes=[b_const])
        S.op("dve", lambda h: h.memset(eps_t[:], EPS), writes=[b_const])
        S.op("pool", lambda h: h.affine_select(ident_f[:], ones_f[:], [[-1, 128]], ALU.is_equal, 0.0,
                                               base=0, channel_multiplier=1),
             reads=[b_const], writes=[b_const])
        S.op("dve", lambda h: h.tensor_copy(ident_bf[:], ident_f[:]), reads=[b_const], writes=[b_const])

        modsT = sb("modsT", [128, L, 72, NR], F32)
        gvec = sb("gvec", [128, L, 6, 8], F32)
        gsc = sb("gsc", [128, L, 3, 8, NR], F32)
        gpo = sb("gpo", [128, L, 3, 8, NR], F32)
        b_mods = Buf("mods")
        b_gvec = Buf("gvec")
        WREG = sb("WREG", [128, 67584], BF16)
        b_W = Buf("Wregion")
        AREG = sb("AREG", [128, 30720], BF16)
        b_A = Buf("Aregion")

        def aview(off_bytes, shape, dt):
            esz = 2 if dt == BF16 else 4
            n = int(np.prod(shape))
            a = AREG[:, off_bytes // 2: off_bytes // 2 + n * esz // 2]
            if dt != BF16:
                a = a.bitcast(dt)
            if len(shape) == 2:
                a = a.rearrange("p (a b) -> p a b", a=shape[0])
            elif len(shape) == 3:
                a = a.rearrange("p (a b c) -> p a b c", a=shape[0], b=shape[1])
            return a

        def wview(off_el, shape):
            n = int(np.prod(shape))
            a = WREG[:, off_el: off_el + n]
            if len(shape) == 2:
                a = a.rearrange("p (a b) -> p a b", a=shape[0])
            return a

        vstage = [sb("vstage0", [128, 128], F32), sb("vstage1", [128, 128], F32)]
        b_vstage = [Buf(), Buf()]
        vcnt = [0]

        def load_fm(dst, src2d, n, wbuf):
            i = vcnt[0] % 2
            vcnt[0] += 1
            stg, bs = vstage[i], b_vstage[i]
            S.dma("sp", stg[0:n, :], src2d, writes=[bs])
            S.op("pe", lambda h, stg=stg, n=n, i=i: h.transpose(PS[6 + i][:, 0:n], stg[0:n, :], ident_f[0:n, 0:n]),
                 reads=[bs, b_const], writes=[bPS[6 + i]])
            S.op("dve", lambda h, dst=dst, n=n, i=i: h.tensor_copy(dst, PS[6 + i][:, 0:n]), reads=[bPS[6 + i]], writes=[wbuf])

        def ada_phase():
            c_sb = aview(0, [D], F32)
            cT = aview(4096, [8, NR], F32)
            cTb = aview(4096 + 8 * NR * 4 + 64, [8, NR], BF16)
            adab = aview(8192, [L, 72], F32)
            b_c, b_cT, b_adab = Buf(), Buf(), Buf()
            S.dma("sp", c_sb[0:NR, :], c_all, writes=[b_c], war=[b_A])
            for l in range(L):
                load_fm(adab[:, l, :], Wd["ada_b"][l].rearrange("(m p) -> m p", p=128), 72, b_adab)
                for gi, nm in enumerate(("ffn1_pre_g", "ffn1_post_g", "mix_pre_g", "mix_post_g",
                                         "ffn2_pre_g", "ffn2_post_g")):
                    load_fm(gvec[:, l, gi, :], Wd[nm][l].rearrange("(c p) -> c p", p=128), 8, b_gvec)
            for c in range(8):
                S.op("pe", lambda h, c=c: h.transpose(PS[0][:, c * 8: c * 8 + NR], c_sb[0:NR, c * 128:(c + 1) * 128],
                                                      ident_f[0:NR, 0:NR]),
                     reads=[b_c, b_const], writes=[bPS[0]])
            S.op("act", lambda h: h.activation(cT[:, :, :], PS[0][:, 0:64].rearrange("p (c r) -> p c r", r=8)[:, :, 0:NR],
                                               AF.Silu),
                 reads=[bPS[0]], writes=[b_cT])
            S.op("dve", lambda h: h.tensor_copy(cTb[:, :, :], cT[:, :, :]), reads=[b_cT], writes=[b_cT])
            wbuf = [wview(0, [8, 1024]), wview(8192, [8, 1024])]
            b_wb = [Buf(), Buf()]
            gi = 0
            for l in range(L):
                for g in range(9):
                    wb, bw = wbuf[gi % 2], b_wb[gi % 2]
                    S.dma("pool", wb, Wd["ada_w"][l][:, g * 1024:(g + 1) * 1024].rearrange("(c p) m -> p c m", p=128),
                          writes=[bw], war=[b_W])
                    pb = 1 + gi % 2

                    def mm(h, wb=wb, pb=pb):
                        ins = None
                        for m in range(8):
                            for c in range(8):
                                ins = h.matmul(PS[pb][:, m * 8: m * 8 + NR], wb[:, c, m * 128:(m + 1) * 128],
                                               cTb[:, c, :], start=(c == 0), stop=(c == 7))
                        return ins
                    S.op("pe", mm, reads=[bw, b_cT], writes=[bPS[pb]])
                    S.op("dve", lambda h, l=l, g=g, pb=pb: h.tensor_tensor(
                        modsT[:, l, g * 8:(g + 1) * 8, :],
                        PS[pb][:, 0:64].rearrange("p (c r) -> p c r", r=8)[:, :, 0:NR],
                        adab[:, l, g * 8:(g + 1) * 8].unsqueeze(2).to_broadcast([128, 8, NR]), ALU.add),
                        reads=[bPS[pb], b_adab], writes=[b_mods])
                    gi += 1
            for l in range(L):
                for k in range(3):
                    sc = modsT[:, l, (3 * k + 1) * 8:(3 * k + 2) * 8, :]
                    gt = modsT[:, l, (3 * k + 2) * 8:(3 * k + 3) * 8, :]
                    pre = gvec[:, l, 2 * k, :].unsqueeze(2).to_broadcast([128, 8, NR])
                    post = gvec[:, l, 2 * k + 1, :].unsqueeze(2).to_broadcast([128, 8, NR])
                    wgt = 1.0 if k == 1 else 0.5
                    S.op("dve", lambda h, l=l, k=k, sc=sc, pre=pre: h.scalar_tensor_tensor(
                        gsc[:, l, k, :, :], sc, 1.0, pre, ALU.add, ALU.mult),
                        reads=[b_mods, b_gvec], writes=[b_mods])
                    S.op("dve", lambda h, l=l, k=k, gt=gt, post=post, wgt=wgt: h.scalar_tensor_tensor(
                        gpo[:, l, k, :, :], gt, wgt, post, ALU.mult, ALU.mult),
                        reads=[b_mods, b_gvec], writes=[b_mods])

        tiles = []
        import os
        TN = int(os.environ.get("TILE_N", "512"))
        for i in range(T // TN):
            tiles.append(dict(n=TN, col0=i * TN, segs=[(0, 0, TN)], prompt=True, idx=i))
        tiles.append(dict(n=NSTOK, col0=T, segs=[(1 + s, s * TS, TS) for s in range(NS)], prompt=False, idx=0))

        XTv = XT.rearrange("(c p) n -> p c n", p=128)

        def load_x_phase():
            xin = [aview(0, [D], F32), aview(4096, [D], F32)]
            b_xin = [Buf(), Buf()]
            xo = [aview(8192, [8, 128], F32), aview(8192 + 4096, [8, 128], F32)]
            b_xo = [Buf(), Buf()]
            blocks = [(x_p, i * 128, i * 128) for i in range(T // 128)] + \
                     [(x_s, i * 128, T + i * 128) for i in range(NSTOK // 128)]
            for bi, (src, r0, col) in enumerate(blocks):
                xi, bx = xin[bi % 2], b_xin[bi % 2]
                S.dma("sp", xi, src[r0:r0 + 128, :], writes=[bx], war=[b_A])
                for half in range(2):
                    pb = 2 * (bi % 2) + half
                    S.op("pe", lambda h, xi=xi, half=half, pb=pb: [
                        h.transpose(PS[pb][:, c * 128:(c + 1) * 128],
                                    xi[:, (half * 4 + c) * 128:(half * 4 + c + 1) * 128], ident_f[:])
                        for c in range(4)][-1], reads=[bx, b_const], writes=[bPS[pb]])
                    o, bo = xo[bi % 2], b_xo[bi % 2]
                    eng = "act" if half == 0 else "dve"
                    if half == 0:
                        S.op("act", lambda h, o=o, pb=pb: h.copy(o[:, 0:4, :], PS[pb][:, :].rearrange("p (c n) -> p c n", c=4)),
                             reads=[bPS[pb]], writes=[bo])
                    else:
                        S.op("dve", lambda h, o=o, pb=pb: h.tensor_copy(o[:, 4:8, :], PS[pb][:, :].rearrange("p (c n) -> p c n", c=4)),
                             reads=[bPS[pb]], writes=[bo])
                S.dma("sp", XTv[:, :, col:col + 128], xo[bi % 2], reads=[b_xo[bi % 2]], writes=[b_XT])

        A_XT = 0
        A_HT = 16384
        A_SQ = 24576
        A_RS = 26624
        A_TMP = 28672
        A_FREE = 32768

        def norm_mod(tile, l, k, xT, hT, b_xT, b_hT, sq, b_sq, rs, b_rs, tmp, b_tmp, psb):
            N = tile["n"]
            for c in range(8):
                S.op("act", lambda h, c=c: h.activation(sq[c % 2][:, :N], xT[:, c, :N], AF.Square),
                     reads=[b_xT], writes=[b_sq[c % 2]])
                S.op("pe", lambda h, c=c: h.matmul(PS[psb][:, :N], ones_bf[:], sq[c % 2][:, :N], start=(c == 0), stop=(c == 7)),
                     reads=[b_sq[c % 2], b_const], writes=[bPS[psb]])
            S.op("act", lambda h: h.activation(rs[:, :N], PS[psb][:, :N], AF.Sqrt, bias=eps_t[:, 0:1], scale=1.0 / D),
                 reads=[bPS[psb], b_const], writes=[b_rs])
            S.op("dve", lambda h: h.reciprocal(rs[:, :N], rs[:, :N]), reads=[b_rs], writes=[b_rs])
            for c in range(8):
                t, bt = tmp[c % 2], b_tmp[c % 2]
                S.op("dve", lambda h, c=c, t=t: h.tensor_tensor(t[:, :N], xT[:, c, :N], rs[:, :N], ALU.mult),
                     reads=[b_xT, b_rs], writes=[bt])
                for (r, c0, n) in tile["segs"]:
                    S.op("act", lambda h, c=c, t=t, r=r, c0=c0, n=n: h.activation(
                        hT[:, c, c0:c0 + n], t[:, c0:c0 + n], AF.Identity,
                        bias=modsT[:, l, (3 * k) * 8 + c, r:r + 1], scale=gsc[:, l, k, c, r:r + 1]),
                        reads=[bt, b_mods], writes=[b_hT])

        def post_norm_residual(tile, l, k, yT, b_yT, xr, b_xr, rs, b_rs, tmp, b_tmp, psb):
            N, col0 = tile["n"], tile["col0"]
            S.op("act", lambda h: h.activation(rs[:, :N], PS[psb][:, :N], AF.Sqrt, bias=eps_t[:, 0:1], scale=1.0 / D),
                 reads=[bPS[psb], b_const], writes=[b_rs])
            S.op("dve", lambda h: h.reciprocal(rs[:, :N], rs[:, :N]), reads=[b_rs], writes=[b_rs])
            for c in range(8):
                t, bt = tmp[c % 2], b_tmp[c % 2]
                x_, bx = xr[c % 2], b_xr[c % 2]
                S.dma("sp", x_[:, :N], XT[c * 128:(c + 1) * 128, col0:col0 + N], reads=[b_XT], writes=[bx])
                S.op("dve", lambda h, c=c, t=t: h.tensor_tensor(t[:, :N], yT[:, c, :N], rs[:, :N], ALU.mult),
                     reads=[b_yT, b_rs], writes=[bt])
                for (r, c0, n) in tile["segs"]:
                    S.op("dve", lambda h, c=c, t=t, x_=x_, r=r, c0=c0, n=n: h.scalar_tensor_tensor(
                        x_[:, c0:c0 + n], t[:, c0:c0 + n], gpo[:, l, k, c, r:r + 1], x_[:, c0:c0 + n],
                        ALU.mult, ALU.add), reads=[bt, b_mods, bx], writes=[bx])
                S.dma("sp", XT[c * 128:(c + 1) * 128, col0:col0 + N], x_[:, :N], reads=[bx])

        def ffn_phase(l, k):
            pre = "ffn1" if k == 0 else "ffn2"
            Wg = wview(0, [8, DFF])
            Wu = wview(8 * DFF, [8, DFF])
            Wdn = wview(16 * DFF, [22, D])
            b_wg = [Buf() for _ in range(8)]
            b_wu = [Buf() for _ in range(8)]
            b_wd = [Buf() for _ in range(22)]
            for c in range(8):
                S.dma("pool", Wg[:, c, :], Wd[pre + "_w_gate"][l][c * 128:(c + 1) * 128, :], writes=[b_wg[c]], war=[b_W])
                S.dma("pool", Wu[:, c, :], Wd[pre + "_w_up"][l][c * 128:(c + 1) * 128, :], writes=[b_wu[c]], war=[b_W])
            for j in range(22):
                S.dma("pool", Wdn[:, j, :], Wd[pre + "_w_down"][l][j * 128:(j + 1) * 128, :], writes=[b_wd[j]], war=[b_W])
            xT = aview(A_XT, [8, 512], F32)
            hT = aview(A_HT, [8, 512], BF16)
            sq = [aview(A_SQ, [512], BF16), aview(A_SQ + 1024, [512], BF16)]
            rs = aview(A_RS, [512], F32)
            tmp = [aview(A_TMP, [512], F32), aview(A_TMP + 2048, [512], F32)]
            aT = aview(A_FREE, [22, 512], BF16)
            sg = [aview(A_FREE + 22528, [512], BF16), aview(A_FREE + 22528 + 1024, [512], BF16)]
            yT = xT
            xr = [aview(A_FREE + 22528 + 2048, [512], F32), aview(A_FREE + 22528 + 4096, [512], F32)]
            b_xT, b_hT, b_rs, b_aT = Buf(), Buf(), Buf(), Buf()
            b_yT = b_xT
            b_sq, b_tmp, b_sg, b_xr = [Buf(), Buf()], [Buf(), Buf()], [Buf(), Buf()], [Buf(), Buf()]
            kk = 0 if k == 0 else 2
            def ffn_tile(tile):
                N, col0 = tile["n"], tile["col0"]
                S.dma("sp", xT[:, :, :N], XTv[:, :, col0:col0 + N], reads=[b_XT], writes=[b_xT], war=[b_A])
                norm_mod(tile, l, kk, xT, hT, b_xT, b_hT, sq, b_sq, rs, b_rs, tmp, b_tmp, 0)
                if os.environ.get("DEBUG_BARRIER2"):
                    S.barrier()
                dbg_dump("dbg_rs", rs[:, :N], [128, N], F32, [b_rs])
                dbg_dump("dbg_hT", hT[:, :, :N], [128, 8, N], BF16, [b_hT])
                for j in range(22):
                    pg, pu = 1 + 2 * (j % 3), 2 + 2 * (j % 3)

                    def mmg(h, j=j, pg=pg):
                        ins = None
                        for c in range(8):
                            ins = h.matmul(PS[pg][:, :N], Wg[:, c, j * 128:(j + 1) * 128], hT[:, c, :N],
                                           start=(c == 0), stop=(c == 7))
                        return ins

                    def mmu(h, j=j, pu=pu):
                        ins = None
                        for c in range(8):
                            ins = h.matmul(PS[pu][:, :N], Wu[:, c, j * 128:(j + 1) * 128], hT[:, c, :N],
                                           start=(c == 0), stop=(c == 7))
                        return ins
                    S.op("pe", mmg, reads=b_wg + [b_hT, b_W], writes=[bPS[pg]])
                    S.op("pe", mmu, reads=b_wu + [b_hT, b_W], writes=[bPS[pu]])
                    if os.environ.get("DEBUG_BARRIER"):
                        S.barrier()
                    S.op("act", lambda h, j=j, pg=pg: h.activation(sg[j % 2][:, :N], PS[pg][:, :N], AF.Silu),
                         reads=[bPS[pg]], writes=[b_sg[j % 2]])
                    S.op("dve", lambda h, j=j, pu=pu: h.tensor_tensor(aT[:, j, :N], sg[j % 2][:, :N], PS[pu][:, :N], ALU.mult),
                         reads=[b_sg[j % 2], bPS[pu]], writes=[b_aT])
                for c in range(8):
                    py = 1 + (c % 6)

                    def mmd(h, c=c, py=py):
                        ins = None
                        for j in range(22):
                            ins = h.matmul(PS[py][:, :N], Wdn[:, j, c * 128:(c + 1) * 128], aT[:, j, :N],
                                           start=(j == 0), stop=(j == 21))
                        return ins
                    S.op("pe", mmd, reads=b_wd + [b_aT, b_W], writes=[bPS[py]])
                    S.op("act", lambda h, c=c, py=py: h.copy(yT[:, c, :N], PS[py][:, :N]), reads=[bPS[py]], writes=[b_yT])
                    S.op("act", lambda h, c=c, py=py: h.activation(sq[c % 2][:, :N], PS[py][:, :N], AF.Square),
                         reads=[bPS[py]], writes=[b_sq[c % 2]])
                    S.op("pe", lambda h, c=c: h.matmul(PS[7][:, :N], ones_bf[:], sq[c % 2][:, :N], start=(c == 0), stop=(c == 7)),
                         reads=[b_sq[c % 2], b_const], writes=[bPS[7]])
                dbg_dump("dbg_aT", aT[:, :, :N], [128, 22, N], BF16, [b_aT])
                dbg_dump("dbg_sg", sg[1][:, :N], [128, N], BF16, [b_sg[1]])
                post_norm_residual(tile, l, kk, yT, b_yT, xr, b_xr, rs, b_rs, tmp, b_tmp, 7)

            for tile in tiles:
                ffn_tile(tile)
            S.barrier()


        NBP = T // 128
        NBLK = NBP + 1
        TWO_PI = 2.0 * math.pi

        def mixB_phase(l):
            Win = wview(0, [8, DIN])
            Wuq = wview(8 * DIN, [4, 1536])
            Wuk = wview(8 * DIN + 6144, [2, 1024])
            Wuv = wview(8 * DIN + 8192, [2, 1024])
            TB0 = (8 * DIN + 10240) * 2

            def wtab(off, shape, dt):
                esz = 2 if dt == BF16 else 4
                n = int(np.prod(shape))
                a = WREG[:, (TB0 + off) // 2:(TB0 + off) // 2 + n * esz // 2]
                if dt != BF16:
                    a = a.bitcast(dt)
                if len(shape) == 2:
                    a = a.rearrange("p (a b) -> p a b", a=shape[0])
                return a
            b_win = [Buf() for _ in range(8)]
            b_wu = Buf()
            for c in range(8):
                S.dma("pool", Win[:, c, :], Wd["w_in"][l][c * 128:(c + 1) * 128, :], writes=[b_win[c]], war=[b_W])
            for c in range(4):
                S.dma("pool", Wuq[:, c, :], Wd["mla_w_uq"][l][c * 128:(c + 1) * 128, :], writes=[b_wu], war=[b_W])
            for c in range(2):
                S.dma("pool", Wuk[:, c, :], Wd["mla_w_uk"][l][c * 128:(c + 1) * 128, :], writes=[b_wu], war=[b_W])
                S.dma("pool", Wuv[:, c, :], Wd["mla_w_uv"][l][c * 128:(c + 1) * 128, :], writes=[b_wu], war=[b_W])
            cosD = wtab(0, [NBLK, 32], F32)
            sinD = wtab(NBLK * 128, [NBLK, 32], F32)
            cosM = wtab(NBLK * 256, [NBLK, 16], F32)
            sinM = wtab(NBLK * 320, [NBLK, 16], F32)
            gkv = wtab(NBLK * 384, [256], F32)
            gq = wtab(NBLK * 384 + 1024, [4], F32)
            b_tab = Buf()
            posi = aview(0, [NBLK], I32)
            posf = aview(1024, [NBLK], F32)
            ji = aview(2048, [32], I32)
            invf = aview(2304, [32], F32)
            ang = aview(4096, [NBLK, 32], F32)
            kf = aview(4096 + NBLK * 128, [NBLK, 32], F32)
            ki = aview(4096 + NBLK * 256, [NBLK, 32], I32)
            b_t = Buf()
            S.op("pool", lambda h: h.iota(posi[:, 0:NBP], [[128, NBP]], base=0, channel_multiplier=1), writes=[b_t], war=[b_A])
            S.op("pool", lambda h: h.iota(posi[0:64, NBP:NBLK], [[0, 1]], base=PAST, channel_multiplier=1), writes=[b_t])
            S.op("pool", lambda h: h.iota(posi[64:128, NBP:NBLK], [[0, 1]], base=PAST, channel_multiplier=1), writes=[b_t])
            S.op("dve", lambda h: h.tensor_copy(posf[:, :], posi[:, :]), reads=[b_t], writes=[b_t])
            S.op("pool", lambda h: h.iota(ji[:, :], [[1, 32]], base=0, channel_multiplier=0), writes=[b_t])
            S.dma("sp", gkv, Wd["mla_kv_norm_g"][l].partition_broadcast(128), writes=[b_tab], war=[b_W])
            load_fm(gq, Wd["mla_q_norm_g"][l].rearrange("(c p) -> c p", p=128), 4, b_tab)

            def build_tab(dst, half, is_cos):
                hw = half
                S.op("dve", lambda h: h.tensor_copy(invf[:, 0:hw], ji[:, 0:hw]), reads=[b_t], writes=[b_t])
                S.op("act", lambda h: h.activation(invf[:, 0:hw], invf[:, 0:hw], AF.Exp, scale=-math.log(THETA) / hw),
                     reads=[b_t], writes=[b_t])
                a3, k3, i3 = ang[:, :, 0:hw], kf[:, :, 0:hw], ki[:, :, 0:hw]
                S.op("dve", lambda h: h.tensor_tensor(a3, posf.unsqueeze(2).to_broadcast([128, NBLK, hw]),
                                                      invf[:, 0:hw].unsqueeze(1).to_broadcast([128, NBLK, hw]), ALU.mult),
                     reads=[b_t], writes=[b_t])
                if is_cos:
                    S.op("dve", lambda h: h.tensor_scalar_add(a3, a3, math.pi / 2), reads=[b_t], writes=[b_t])
                S.op("dve", lambda h: h.tensor_scalar_mul(k3, a3, 1.0 / TWO_PI), reads=[b_t], writes=[b_t])
                S.op("dve", lambda h: h.tensor_copy(i3, k3), reads=[b_t], writes=[b_t])
                S.op("dve", lambda h: h.tensor_copy(k3, i3), reads=[b_t], writes=[b_t])
                S.op("dve", lambda h: h.scalar_tensor_tensor(a3, k3, -TWO_PI, a3, ALU.mult, ALU.add), reads=[b_t], writes=[b_t])
                S.op("dve", lambda h: h.tensor_scalar(k3, a3, math.pi, -TWO_PI, ALU.is_gt, ALU.mult), reads=[b_t], writes=[b_t])
                S.op("dve", lambda h: h.tensor_tensor(a3, a3, k3, ALU.add), reads=[b_t], writes=[b_t])
                S.op("dve", lambda h: h.tensor_scalar(k3, a3, -math.pi, TWO_PI, ALU.is_lt, ALU.mult), reads=[b_t], writes=[b_t])
                S.op("dve", lambda h: h.tensor_tensor(a3, a3, k3, ALU.add), reads=[b_t], writes=[b_t])
                S.op("dve", lambda h: h.tensor_scalar(a3, a3, -math.pi, math.pi, ALU.max, ALU.min), reads=[b_t], writes=[b_t])
                S.op("act", lambda h: h.activation(dst, a3, AF.Sin), reads=[b_t], writes=[b_tab])
            build_tab(cosD, 32, True)
            build_tab(sinD, 32, False)
            build_tab(cosM, 16, True)
            build_tab(sinM, 16, False)
            S.barrier()

            xT = aview(0, [8, 512], F32)
            uT = aview(16384, [8, 512], BF16)
            sq = [aview(24576, [512], BF16), aview(25600, [512], BF16)]
            rs = aview(26624, [512], F32)
            tmp = [aview(28672, [512], F32), aview(30720, [512], F32)]
            tmA = aview(32768, [1536], F32)
            tmB = aview(38912, [1024], F32)
            tmC = aview(43008, [1024], F32)
            tbf = aview(47104, [1536], BF16)
            stg = [aview(50176, [8, 128], BF16), aview(52224, [8, 128], BF16)]
            cqn = aview(54272, [4, 512], BF16)
            ckvT = aview(58368, [2, 512], BF16)
            small = aview(60416, [64], F32)
            b_xT, b_uT, b_rs, b_tmA, b_tmB, b_tmC, b_tbf, b_cqn, b_ckvT, b_small = [Buf() for _ in range(10)]
            b_sq, b_tmp, b_stg = [Buf(), Buf()], [Buf(), Buf()], [Buf(), Buf()]
            stgc = [0]

            def evac_T(pb, nrows, ncol_groups, dsts, bf_src_buf):
                i = stgc[0] % 2
                stgc[0] += 1
                sg_, bs_ = stg[i], b_stg[i]
                pv = PS[pb].bitcast(BF16)[0:nrows, 0:ncol_groups * 128].rearrange("p (g n) -> p g n", g=ncol_groups)
                S.op("act", lambda h: h.copy(sg_[0:nrows, 0:ncol_groups, :], pv), reads=[bPS[pb]], writes=[bs_])
                for (dst, lo, hi) in dsts:
                    S.dma("sp", dst, sg_[0:nrows, 0:ncol_groups, lo:hi], reads=[bs_])

            def rope_tm(src, dst, G, half, cosb, sinb, blk, rd, wr, scale=None):
                cb = cosb[:, blk, :].unsqueeze(1).unsqueeze(1).to_broadcast([128, G, 2, half])
                sb_ = sinb[:, blk, :].unsqueeze(1).unsqueeze(1).to_broadcast([128, G, 2, half])
                s4 = src.rearrange("p g (t j) -> p g t j", t=2)
                d4 = dst.rearrange("p g (t j) -> p g t j", t=2)
                xc = tmC[:, 0:G * 2 * half].rearrange("p (g t j) -> p g t j", g=G, t=2)
                xs = tmB[:, 0:G * 2 * half].rearrange("p (g t j) -> p g t j", g=G, t=2)
                S.op("dve", lambda h: h.tensor_tensor(xc, s4, cb, ALU.mult), reads=rd + [b_tab], writes=[b_tmC])
                S.op("dve", lambda h: h.tensor_tensor(xs, s4, sb_, ALU.mult), reads=rd + [b_tab], writes=[b_tmB])
                if scale is None:
                    S.op("dve", lambda h: h.tensor_tensor(d4[:, :, 0, :], xc[:, :, 0, :], xs[:, :, 1, :], ALU.subtract),
                         reads=[b_tmC, b_tmB], writes=wr)
                    S.op("dve", lambda h: h.tensor_tensor(d4[:, :, 1, :], xc[:, :, 1, :], xs[:, :, 0, :], ALU.add),
                         reads=[b_tmC, b_tmB], writes=wr)
                else:
                    S.op("dve", lambda h: h.tensor_tensor(xc[:, :, 0, :], xc[:, :, 0, :], xs[:, :, 1, :], ALU.subtract),
                         reads=[b_tmC, b_tmB], writes=[b_tmC])
                    S.op("dve", lambda h: h.tensor_tensor(xc[:, :, 1, :], xc[:, :, 1, :], xs[:, :, 0, :], ALU.add),
                         reads=[b_tmC, b_tmB], writes=[b_tmC])
                    S.op("act", lambda h: h.activation(d4, xc, AF.Copy, scale=scale), reads=[b_tmC], writes=wr)

            def tm_matmul(pb, tb, col_lo, ncols, lhs, nk, W, wbufs):
                def mm(h):
                    ins = None
                    for c in range(nk):
                        ins = h.matmul(PS[pb][:, 0:ncols], lhs[:, c, tb * 128:(tb + 1) * 128], W[:, c, col_lo:col_lo + ncols],
                                       start=(c == 0), stop=(c == nk - 1))
                    return ins
                return mm

            def tile_B(tile):
                N, col0 = tile["n"], tile["col0"]
                NTB = N // 128
                prompt = tile["prompt"]
                S.dma("sp", xT[:, :, :N], XTv[:, :, col0:col0 + N], reads=[b_XT], writes=[b_xT], war=[b_A])
                norm_mod(tile, l, 1, xT, uT, b_xT, b_uT, sq, b_sq, rs, b_rs, tmp, b_tmp, 0)

                def out_rows(o_p, o_s, tb, width):
                    r0 = (col0 if prompt else 0) + tb * 128
                    return (o_p if prompt else o_s)[l, r0:r0 + 128, :]

                for tb in range(NTB):
                    blk = (col0 // 128 + tb) if prompt else NBP
                    tokc = col0 + tb * 128
                    S.op("pe", tm_matmul(1, tb, C_CKV, 288, uT, 8, Win, b_win), reads=b_win + [b_uT, b_W], writes=[bPS[1]])
                    S.op("act", lambda h: h.activation(tmB[:, 0:256], PS[1][:, 0:256], AF.Square, accum_out=small[:, 0:1]),
                         reads=[bPS[1]], writes=[b_tmB, b_small])
                    S.op("act", lambda h: h.activation(small[:, 1:2], small[:, 0:1], AF.Sqrt, bias=eps_t[:, 0:1], scale=1.0 / MLA_KV),
                         reads=[b_small, b_const], writes=[b_small])
                    S.op("dve", lambda h: h.reciprocal(small[:, 2:3], small[:, 1:2]), reads=[b_small], writes=[b_small])
                    S.op("dve", lambda h: h.scalar_tensor_tensor(tmA[:, 0:256], PS[1][:, 0:256], small[:, 2:3], gkv[:, :],
                                                                 ALU.mult, ALU.mult),
                         reads=[bPS[1], b_small, b_tab], writes=[b_tmA])
                    S.dma("sp", out_rows(o_ckv_p, o_ckv_s, tb, 256), tmA[:, 0:256], reads=[b_tmA])
                    S.op("act", lambda h: h.copy(tbf[:, 0:256], tmA[:, 0:256]), reads=[b_tmA], writes=[b_tbf])
                    rope_tm(PS[1][:, 256:288].rearrange("p (g d) -> p g d", g=1), tmA[:, 256:288].rearrange("p (g d) -> p g d", g=1),
                            1, 16, cosM, sinM, blk, [bPS[1]], [b_tmA])
                    S.dma("sp", out_rows(o_kpe_p, o_kpe_s, tb, 32), tmA[:, 256:288], reads=[b_tmA])
                    S.op("act", lambda h: h.copy(tbf[:, 256:288], tmA[:, 256:288]), reads=[b_tmA], writes=[b_tbf])
                    S.op("pe", lambda h, tb=tb: [h.transpose(PS[2].bitcast(BF16)[:, c * 128:(c + 1) * 128], tbf[:, c * 128:(c + 1) * 128], ident_bf[:])
                                                 for c in range(2)][-1], reads=[b_tbf, b_const], writes=[bPS[2]])
                    S.op("dve", lambda h, tb=tb: h.tensor_copy(ckvT[:, :, tb * 128:(tb + 1) * 128],
                                                                PS[2].bitcast(BF16)[:, 0:256].rearrange("p (c n) -> p c n", c=2)),
                         reads=[bPS[2]], writes=[b_ckvT])
                    S.op("pe", lambda h: h.transpose(PS[3].bitcast(BF16)[0:32, 0:128], tbf[:, 256:288], ident_bf[:]),
                         reads=[b_tbf, b_const], writes=[bPS[3]])
                    if prompt:
                        kp_d = [(KPT_p[:, tokc:tokc + 128].rearrange("r (g n) -> r g n", g=1), 0, 128)]
                    else:
                        kp_d = [(KPT_s[2 * tb + e, :, PAST:PAST + 64].rearrange("r (g n) -> r g n", g=1), e * 64, e * 64 + 64) for e in range(2)]
                    evac_T(3, 32, 1, kp_d, None)
                    for which, cbase, scale in (("q", C_DQ, DIFF_SCALE), ("k", C_DK, None)):
                        for half_ in range(2):
                            pb = 4 + half_
                            S.op("pe", tm_matmul(pb, tb, cbase + half_ * 512, 512, uT, 8, Win, b_win),
                                 reads=b_win + [b_uT, b_W], writes=[bPS[pb]])
                            rope_tm(PS[pb][:, :].rearrange("p (g d) -> p g d", g=8),
                                    tmA[:, half_ * 512:(half_ + 1) * 512].rearrange("p (g d) -> p g d", g=8),
                                    8, 32, cosD, sinD, blk, [bPS[pb]], [b_tmA], scale=scale)
                        if which == "k":
                            S.dma("sp", out_rows(o_dk_p, o_dk_s, tb, 1024), tmA[:, 0:1024], reads=[b_tmA])
                        S.op("act", lambda h: h.copy(tbf[:, 0:1024], tmA[:, 0:1024]), reads=[b_tmA], writes=[b_tbf])
                        S.op("pe", lambda h: [h.transpose(PS[6].bitcast(BF16)[:, c * 128:(c + 1) * 128], tbf[:, c * 128:(c + 1) * 128], ident_bf[:])
                                              for c in range(8)][-1], reads=[b_tbf, b_const], writes=[bPS[6]])
                        if which == "q":
                            dd = [(QTd[:, :, tokc:tokc + 128].rearrange("h p n -> p h n"), 0, 128)]
                        elif prompt:
                            dd = [(KTd_p[:, :, tokc:tokc + 128].rearrange("h p n -> p h n"), 0, 128)]
                        else:
                            dd = [(KTd_s[2 * tb + e, :, :, PAST:PAST + 64].rearrange("h p n -> p h n"), e * 64, e * 64 + 64) for e in range(2)]
                        evac_T(6, 128, 8, dd, None)
                    for half_ in range(2):
                        pb = 4 + half_
                        S.op("pe", tm_matmul(pb, tb, C_DV + half_ * 512, 512, uT, 8, Win, b_win),
                             reads=b_win + [b_uT, b_W], writes=[bPS[pb]])
                        S.op("act", lambda h, half_=half_, pb=pb: h.copy(tmA[:, half_ * 512:(half_ + 1) * 512], PS[pb][:, :]),
                             reads=[bPS[pb]], writes=[b_tmA])
                    S.dma("sp", out_rows(o_dv_p, o_dv_s, tb, 1024), tmA[:, 0:1024], reads=[b_tmA])
                    S.op("dve", lambda h: h.tensor_copy(tbf[:, 0:1024], tmA[:, 0:1024]), reads=[b_tmA], writes=[b_tbf])
                    if prompt:
                        S.dma("sp", Vd_p[tokc:tokc + 128, :], tbf[:, 0:1024], reads=[b_tbf])
                    else:
                        for e in range(2):
                            S.dma("sp", Vd_s[2 * tb + e, PAST:PAST + 64, :], tbf[e * 64:(e + 1) * 64, 0:1024], reads=[b_tbf])
                    for half_ in range(2):
                        pb = 4 + half_
                        S.op("pe", tm_matmul(pb, tb, half_ * 512, 512, ckvT, 2, Wuv, None),
                             reads=[b_wu, b_ckvT, b_W], writes=[bPS[pb]])
                        S.op("act", lambda h, half_=half_, pb=pb: h.copy(tbf[:, half_ * 512:(half_ + 1) * 512], PS[pb][:, :]),
                             reads=[bPS[pb]], writes=[b_tbf])
                    if prompt:
                        S.dma("sp", Vm_p[tokc:tokc + 128, :], tbf[:, 0:1024], reads=[b_tbf])
                    else:
                        for e in range(2):
                            S.dma("sp", Vm_s[2 * tb + e, PAST:PAST + 64, :], tbf[e * 64:(e + 1) * 64, 0:1024], reads=[b_tbf])

                for oc in range(8):
                    pb = 1 + oc % 2

                    def mmk(h, oc=oc, pb=pb):
                        ins = None
                        for c in range(2):
                            ins = h.matmul(PS[pb][:, :N], Wuk[:, c, oc * 128:(oc + 1) * 128], ckvT[:, c, :N], start=(c == 0), stop=(c == 1))
                        return ins
                    S.op("pe", mmk, reads=[b_wu, b_ckvT, b_W], writes=[bPS[pb]])
                    t_ = tmp[oc % 2].bitcast(BF16)
                    S.op("act", lambda h, pb=pb, t_=t_: h.copy(t_[:, :N], PS[pb][:, :N]), reads=[bPS[pb]], writes=[b_tmp[oc % 2]])
                    if prompt:
                        S.dma("sp", KTm_p[oc * 128:(oc + 1) * 128, col0:col0 + N], t_[:, :N], reads=[b_tmp[oc % 2]])
                    else:
                        for s_ in range(NS):
                            S.dma("sp", KTm_s[s_, oc * 128:(oc + 1) * 128, PAST:PAST + 64], t_[:, s_ * 64:(s_ + 1) * 64], reads=[b_tmp[oc % 2]])
                cqf = xT
                for c in range(4):
                    pb = 1 + c % 2

                    def mmq(h, c=c, pb=pb):
                        ins = None
                        for k_ in range(8):
                            ins = h.matmul(PS[pb][:, :N], Win[:, k_, C_CQ + c * 128:C_CQ + (c + 1) * 128], uT[:, k_, :N],
                                           start=(k_ == 0), stop=(k_ == 7))
                        return ins
                    S.op("pe", mmq, reads=b_win + [b_uT, b_W], writes=[bPS[pb]])
                    S.op("act", lambda h, c=c, pb=pb: h.copy(cqf[:, c, :N], PS[pb][:, :N]), reads=[bPS[pb]], writes=[b_xT])
                    S.op("act", lambda h, c=c, pb=pb: h.activation(sq[c % 2][:, :N], PS[pb][:, :N], AF.Square),
                         reads=[bPS[pb]], writes=[b_sq[c % 2]])
                    S.op("pe", lambda h, c=c: h.matmul(PS[0][:, :N], ones_bf[:], sq[c % 2][:, :N], start=(c == 0), stop=(c == 3)),
                         reads=[b_sq[c % 2], b_const], writes=[bPS[0]])
                S.op("act", lambda h: h.activation(rs[:, :N], PS[0][:, :N], AF.Sqrt, bias=eps_t[:, 0:1], scale=1.0 / MLA_Q),
                     reads=[bPS[0], b_const], writes=[b_rs])
                S.op("dve", lambda h: h.reciprocal(rs[:, :N], rs[:, :N]), reads=[b_rs], writes=[b_rs])
                for c in range(4):
                    S.op("dve", lambda h, c=c: h.scalar_tensor_tensor(cqn[:, c, :N], cqf[:, c, :N], gq[:, c:c + 1], rs[:, :N],
                                                                      ALU.mult, ALU.mult),
                         reads=[b_xT, b_rs, b_tab], writes=[b_cqn])
                for tb in range(NTB):
                    blk = (col0 // 128 + tb) if prompt else NBP
                    tokc = col0 + tb * 128
                    for j in range(3):
                        pb = 4 + j
                        S.op("pe", tm_matmul(pb, tb, j * 512, 512, cqn, 4, Wuq, None), reads=[b_wu, b_cqn, b_W], writes=[bPS[pb]])
                        S.op("act", lambda h, j=j, pb=pb: h.activation(tmA[:, j * 512:(j + 1) * 512], PS[pb][:, :], AF.Copy, scale=MLA_SCALE),
                             reads=[bPS[pb]], writes=[b_tmA])
                    q3 = tmA[:, 0:1536].rearrange("p (h d) -> p h d", h=16)
                    qb3 = tbf[:, 0:1536].rearrange("p (h d) -> p h d", h=16)
                    S.op("act", lambda h: h.copy(qb3[:, :, 0:64], q3[:, :, 0:64]), reads=[b_tmA], writes=[b_tbf])
                    rope_tm(q3[:, :, 64:96], qb3[:, :, 64:96], 16, 16, cosM, sinM, blk, [b_tmA], [b_tbf])
                    for g in range(2):
                        S.op("pe", lambda h, g=g: [h.transpose(PS[6 + g].bitcast(BF16)[0:96, hh * 128:(hh + 1) * 128],
                                                               tbf[:, (g * 8 + hh) * 96:(g * 8 + hh + 1) * 96], ident_bf[:])
                                                   for hh in range(8)][-1], reads=[b_tbf, b_const], writes=[bPS[6 + g]])
                        evac_T(6 + g, 96, 8, [(QTm[g * 8:(g + 1) * 8, :, tokc:tokc + 128].rearrange("h p n -> p h n"), 0, 128)], None)
                for c in range(8):
                    def mmg(h, c=c):
                        ins = None
                        for k_ in range(8):
                            ins = h.matmul(PS[1][:, :N], Win[:, k_, C_CV + c * 128:C_CV + (c + 1) * 128], uT[:, k_, :N],
                                           start=(k_ == 0), stop=(k_ == 7))
                        return ins

                    def mmb(h, c=c):
                        ins = None
                        for k_ in range(8):
                            ins = h.matmul(PS[2][:, :N], Win[:, k_, C_CV + 1024 + c * 128:C_CV + 1024 + (c + 1) * 128], uT[:, k_, :N],
                                           start=(k_ == 0), stop=(k_ == 7))
                        return ins
                    S.op("pe", mmg, reads=b_win + [b_uT, b_W], writes=[bPS[1]])
                    S.op("pe", mmb, reads=b_win + [b_uT, b_W], writes=[bPS[2]])
                    t_, bt_ = tmp[c % 2], b_tmp[c % 2]
                    S.op("act", lambda h, t_=t_: h.activation(t_[:, :N], PS[2][:, :N], AF.Sigmoid), reads=[bPS[2]], writes=[bt_])
                    S.op("dve", lambda h, t_=t_: h.tensor_tensor(t_[:, :N], t_[:, :N], PS[1][:, :N], ALU.mult), reads=[bt_, bPS[1]], writes=[bt_])
                    S.dma("sp", GLUT[c * 128:(c + 1) * 128, col0:col0 + N], t_[:, :N], reads=[bt_])


            KG = min(512, PAST)
            NKB = KG // 128
            cdk = [aview(0, [1024], BF16), aview(2048, [1024], BF16)]
            cdv = [aview(4096, [1024], BF16), aview(6144, [1024], BF16)]
            cck = [aview(8192, [288], BF16), aview(8192 + 576, [288], BF16)]
            ckvTc = aview(16384, [2, 512], BF16)
            b_cdk, b_cdv, b_cck = [Buf(), Buf()], [Buf(), Buf()], [Buf(), Buf()]
            b_ckc = Buf()
            cnt = [0]

            def prep_group(s_, g0):
                for kb in range(NKB):
                    k0 = g0 + kb * 128
                    i = cnt[0] % 2
                    cnt[0] += 1
                    S.dma("pool", cdk[i], c_dk[l, s_, k0:k0 + 128, :], writes=[b_cdk[i]])
                    S.dma("pool", cdv[i], c_dv[l, s_, k0:k0 + 128, :], writes=[b_cdv[i]])
                    S.dma("pool", cck[i][:, 0:256], c_ckv[l, s_, k0:k0 + 128, :], writes=[b_cck[i]])
                    S.dma("pool", cck[i][:, 256:288], c_kpe[l, s_, k0:k0 + 128, :], writes=[b_cck[i]])
                    S.dma("sp", Vd_s[s_, k0:k0 + 128, :], cdv[i], reads=[b_cdv[i]])
                    S.op("pe", lambda h, i=i: [h.transpose(PS[6].bitcast(BF16)[:, c * 128:(c + 1) * 128], cdk[i][:, c * 128:(c + 1) * 128], ident_bf[:])
                                               for c in range(8)][-1], reads=[b_cdk[i], b_const], writes=[bPS[6]])
                    evac_T(6, 128, 8, [(KTd_s[s_, :, :, k0:k0 + 128].rearrange("h p n -> p h n"), 0, 128)], None)
                    S.op("pe", lambda h, i=i: [h.transpose(PS[2].bitcast(BF16)[:, c * 128:(c + 1) * 128], cck[i][:, c * 128:(c + 1) * 128], ident_bf[:])
                                               for c in range(2)][-1], reads=[b_cck[i], b_const], writes=[bPS[2]])
                    S.op("dve", lambda h, kb=kb: h.tensor_copy(ckvTc[:, :, kb * 128:(kb + 1) * 128],
                                                                PS[2].bitcast(BF16)[:, 0:256].rearrange("p (c n) -> p c n", c=2)),
                         reads=[bPS[2]], writes=[b_ckc])
                    S.op("pe", lambda h, i=i: h.transpose(PS[3].bitcast(BF16)[0:32, 0:128], cck[i][:, 256:288], ident_bf[:]),
                         reads=[b_cck[i], b_const], writes=[bPS[3]])
                    evac_T(3, 32, 1, [(KPT_s[s_, :, k0:k0 + 128].rearrange("r (g n) -> r g n", g=1), 0, 128)], None)
                    for half_ in range(2):
                        pb = 4 + half_
                        S.op("pe", tm_matmul(pb, kb, half_ * 512, 512, ckvTc, 2, Wuv, None), reads=[b_wu, b_ckc, b_W], writes=[bPS[pb]])
                        S.op("act", lambda h, half_=half_, pb=pb: h.copy(tbf[:, half_ * 512:(half_ + 1) * 512], PS[pb][:, :]),
                             reads=[bPS[pb]], writes=[b_tbf])
                    S.dma("sp", Vm_s[s_, k0:k0 + 128, :], tbf[:, 0:1024], reads=[b_tbf])
                for oc in range(8):
                    pb = 1 if oc % 2 == 0 else 7

                    def mmk(h, oc=oc, pb=pb):
                        ins = None
                        for c in range(2):
                            ins = h.matmul(PS[pb][:, :KG], Wuk[:, c, oc * 128:(oc + 1) * 128], ckvTc[:, c, :KG], start=(c == 0), stop=(c == 1))
                        return ins
                    S.op("pe", mmk, reads=[b_wu, b_ckc, b_W], writes=[bPS[pb]])
                    t_ = tmp[oc % 2].bitcast(BF16)
                    S.op("act", lambda h, pb=pb, t_=t_: h.copy(t_[:, :KG], PS[pb][:, :KG]), reads=[bPS[pb]], writes=[b_tmp[oc % 2]])
                    S.dma("sp", KTm_s[s_, oc * 128:(oc + 1) * 128, g0:g0 + KG], t_[:, :KG], reads=[b_tmp[oc % 2]])

            for s_ in range(NS):
                for g0 in range(0, PAST, KG):
                    prep_group(s_, g0)
            S.barrier()

            for tile in tiles:
                tile_B(tile)
            S.barrier()


        def attn_phase(l):
            lam_init = 0.8 - 0.6 * math.exp(-0.3 * l)
            prm = aview(0, [4, 64], F32)
            pr2 = aview(1024, [16], F32)
            gsub = aview(1100, [1], F32)
            b_prm = Buf()
            for i, nm in enumerate(("diff_lq1", "diff_lk1", "diff_lq2", "diff_lk2")):
                S.dma("sp", prm[:, i, :], Wd[nm][l].partition_broadcast(128), writes=[b_prm], war=[b_A])
            load_fm(gsub[:, 0:1], Wd["diff_subln_g"][l].rearrange("(o p) -> o p", o=1), 1, b_prm)
            S.op("dve", lambda h: h.tensor_tensor(prm[:, 0, :], prm[:, 0, :], prm[:, 1, :], ALU.mult), reads=[b_prm], writes=[b_prm])
            S.op("dve", lambda h: h.tensor_tensor(prm[:, 2, :], prm[:, 2, :], prm[:, 3, :], ALU.mult), reads=[b_prm], writes=[b_prm])
            S.op("dve", lambda h: h.reduce_sum(pr2[:, 0:1], prm[:, 0, :], AX.X), reads=[b_prm], writes=[b_prm])
            S.op("dve", lambda h: h.reduce_sum(pr2[:, 1:2], prm[:, 2, :], AX.X), reads=[b_prm], writes=[b_prm])
            S.op("act", lambda h: h.activation(pr2[:, 2:4], pr2[:, 0:2], AF.Exp), reads=[b_prm], writes=[b_prm])
            S.op("dve", lambda h: h.scalar_tensor_tensor(pr2[:, 4:5], pr2[:, 3:4], -lam_init, pr2[:, 2:3], ALU.add, ALU.subtract),
                 reads=[b_prm], writes=[b_prm])
            S.op("dve", lambda h: h.tensor_scalar_mul(gsub[:, 0:1], gsub[:, 0:1], 1.0 - lam_init), reads=[b_prm], writes=[b_prm])
            neg_lam = pr2[:, 4:5]

            KMAX = max(T, KS)
            NKT = (KMAX + 127) // 128
            def hset(i):
                base = i * 16640
                kt_ = WREG[:, base: base + 4224]
                qt_ = WREG[:, base + 4224: base + 4224 + 4096]
                v_ = WREG[:, base + 8320: base + 8320 + 4224]
                qb_ = WREG[:, base + 12544: base + 12544 + 4096]
                return kt_, qt_, v_, qb_
            hs = [hset(0), hset(1)]
            b_hs = [[Buf() for _ in range(6)], [Buf() for _ in range(6)]]
            PT = [aview(2048 + i * 1024, [512], BF16) for i in range(4)]
            b_PT = [Buf() for _ in range(4)]
            acc = [aview(6144, [512], F32), aview(8192, [512], F32)]
            b_acc = [Buf(), Buf()]
            accB = [aview(21504, [512], F32), aview(23552, [512], F32)]
            b_accB = [Buf(), Buf()]
            tt = [aview(10240 + i * 2048, [512], F32) for i in range(4)]
            b_tt = [Buf() for _ in range(4)]
            sqd = aview(18432, [512], BF16)
            b_sqd = Buf()
            ost = [aview(19456, [512], BF16), aview(20480, [512], BF16)]
            b_ost = [Buf(), Buf()]
            ctr = dict(pt=0, sb=0, os=0, hs=0)

            seqs = [dict(q0=0, nq=T, QB=min(512, T), K=T, causal=True,
                         KTd=KTd_p, Vd=Vd_p, KTm=KTm_p, KPT=KPT_p, Vm=Vm_p)]
            for s_ in range(NS):
                seqs.append(dict(q0=T + s_ * TS, nq=TS, QB=TS, K=KS, causal=False,
                                 KTd=KTd_s[s_], Vd=Vd_s[s_], KTm=KTm_s[s_], KPT=KPT_s[s_], Vm=Vm_s[s_]))

            def load_head(kind, sq_, hd, i):
                kt_, qt_, v_, qb_ = hs[i]
                K, nq, q0 = sq_["K"], sq_["nq"], sq_["q0"]
                nfull = K // 128
                rem = K - nfull * 128
                if kind == "mla":
                    S.dma("sp", kt_[0:64, 0:K], sq_["KTm"][hd * 64:(hd + 1) * 64, :], writes=[b_hs[i][0]], war=[b_W])
                    S.dma("sp", kt_[64:96, 0:K], sq_["KPT"][:, :], writes=[b_hs[i][1]], war=[b_W])
                    S.dma("sp", qt_[0:96, 0:nq], QTm[hd, :, q0:q0 + nq], writes=[b_hs[i][2]], war=[b_W])
                    vc0 = (hd // 2) * 128
                    Vsrc = sq_["Vm"]
                else:
                    S.dma("sp", kt_[:, 0:K], sq_["KTd"][hd, :, :], writes=[b_hs[i][0], b_hs[i][1]], war=[b_W])
                    S.op("pool", lambda h: h.memset(qt_[64:128, 0:nq], 0.0), writes=[b_hs[i][2]], war=[b_W])
                    S.op("pool", lambda h: h.memset(qb_[0:64, 0:nq], 0.0), writes=[b_hs[i][5]], war=[b_W])
                    S.dma("sp", qt_[0:64, 0:nq], QTd[hd, 0:64, q0:q0 + nq], writes=[b_hs[i][2]], war=[b_W])
                    S.dma("sp", qb_[64:128, 0:nq], QTd[hd, 64:128, q0:q0 + nq], writes=[b_hs[i][5]], war=[b_W])
                    vc0 = hd * 128
                    Vsrc = sq_["Vd"]
                v3 = v_[:, 0:NKT * 128].rearrange("p (t d) -> p t d", d=128)
                S.dma("sp", v3[:, 0:nfull, :], Vsrc[0:nfull * 128, vc0:vc0 + 128].rearrange("(t p) d -> p t d", p=128),
                      writes=[b_hs[i][3]], war=[b_W])
                if rem:
                    S.dma("sp", v3[0:rem, nfull, :], Vsrc[nfull * 128:K, vc0:vc0 + 128], writes=[b_hs[i][4]], war=[b_W])

            def attend_block(kind, sq_, hd, i, qb, maps):
                kt_, qt_, v_, qb_ = hs[i]
                K, QB = sq_["K"], sq_["QB"]
                dv = 128
                po = (hd % 2) * 64 if kind == "mla" else 0
                pn = 64 if kind == "mla" else 128
                v3 = v_[:, 0:NKT * 128].rearrange("p (t d) -> p t d", d=128)
                qc0 = qb * QB
                nq = QB
                if sq_["causal"]:
                    nkt = (qc0 + QB) // 128
                    diag0 = qc0 // 128
                else:
                    nkt = (K + 127) // 128
                    diag0 = nkt + 1
                for m in maps:
                    if kind == "mla":
                        r0, r1 = 0, 96
                        qsrc = qt_
                    else:
                        r0, r1 = 0, 128
                        qsrc = qt_ if m == 0 else qb_
                    ob, sb_, ac, bac = 2 + m, 4 + m, acc[m], b_acc[m]
                    ac2, bac2 = accB[m], b_accB[m]
                    SK = 2
                    stageA, stageB, stageC = [], [], []
                    for kt in range(nkt):
                        nk = min(128, K - kt * 128)
                        c_lo = (kt - diag0) * 128 if kt >= diag0 else 0
                        spb = (0, 1, 7)[ctr["sb"] % 3]
                        ctr["sb"] += 1
                        pi = ctr["pt"] % 4
                        ctr["pt"] += 1
                        pt_, bpt = PT[pi], b_PT[pi]

                        def fA(kt=kt, nk=nk, c_lo=c_lo, spb=spb, r0=r0, r1=r1, qsrc=qsrc):
                            S.op("pe", lambda h: h.matmul(
                                PS[spb][0:nk, c_lo:nq], kt_[r0:r1, kt * 128:kt * 128 + nk], qsrc[r0:r1, qc0 + c_lo:qc0 + nq],
                                start=True, stop=True), reads=b_hs[i], writes=[bPS[spb]])

                        def fB(kt=kt, nk=nk, c_lo=c_lo, spb=spb, pt_=pt_, bpt=bpt, ac=ac, bac=bac, ac2=ac2, bac2=bac2):
                            S.op("act", lambda h: h.activation(pt_[0:nk, c_lo:nq], PS[spb][0:nk, c_lo:nq], AF.Exp),
                                 reads=[bPS[spb]], writes=[bpt])
                            if kt >= diag0 and nk == 128:
                                S.op("dve", lambda h: h.memset(pt_[64:128, c_lo:c_lo + 64], 0.0), reads=[bpt], writes=[bpt])
                            if kt == 0:
                                S.op("dve", lambda h: h.tensor_copy(ac[0:nk, 0:nq], pt_[0:nk, 0:nq]), reads=[bpt], writes=[bac])
                            elif kt == 1:
                                if c_lo > 0:
                                    S.op("pool", lambda h: h.memset(ac2[:, 0:c_lo], 0.0), writes=[bac2])
                                if nk < 128:
                                    S.op("pool", lambda h: h.memset(ac2[:, 0:nq], 0.0), writes=[bac2])
                                S.op("pool", lambda h: h.tensor_copy(ac2[0:nk, c_lo:nq], pt_[0:nk, c_lo:nq]), reads=[bpt], writes=[bac2])
                            elif kt % 2 == 0:
                                S.op("dve", lambda h: h.tensor_tensor(ac[0:nk, c_lo:nq], ac[0:nk, c_lo:nq], pt_[0:nk, c_lo:nq], ALU.add),
                                     reads=[bpt, bac], writes=[bac])
                            else:
                                S.op("pool", lambda h: h.tensor_tensor(ac2[0:nk, c_lo:nq], ac2[0:nk, c_lo:nq], pt_[0:nk, c_lo:nq], ALU.add),
                                     reads=[bpt, bac2], writes=[bac2])

                        def fC(kt=kt, nk=nk, c_lo=c_lo, pt_=pt_, bpt=bpt, ob=ob):
                            S.op("pe", lambda h: h.matmul(
                                PS[ob][0:dv, c_lo:nq], v3[0:nk, kt, :], pt_[0:nk, c_lo:nq], start=(kt == 0), stop=(kt == nkt - 1)),
                                reads=b_hs[i] + [bpt], writes=[bPS[ob]])
                        stageA.append(fA)
                        stageB.append(fB)
                        stageC.append(fC)
                    for step in range(nkt + SK):
                        if step < nkt:
                            stageA[step]()
                            stageB[step]()
                        if step - SK >= 0:
                            stageC[step - SK]()
                    if nkt > 1:
                        S.op("pe", lambda h, ac=ac, ac2=ac2, sb_=sb_: [
                            h.matmul(PS[sb_][0:dv, 0:nq], ones_f[:, 0:dv], ac[:, 0:nq], start=True, stop=False),
                            h.matmul(PS[sb_][0:dv, 0:nq], ones_f[:, 0:dv], ac2[:, 0:nq], start=False, stop=True)][-1],
                            reads=[bac, bac2, b_const], writes=[bPS[sb_]])
                    else:
                        S.op("pe", lambda h, ac=ac, sb_=sb_: h.matmul(PS[sb_][0:dv, 0:nq], ones_f[:, 0:dv], ac[:, 0:nq], start=True, stop=True),
                             reads=[bac, b_const], writes=[bPS[sb_]])
                    S.op("act", lambda h, m=m, sb_=sb_: h.activation(tt[m][po:po + pn, 0:nq], PS[sb_][po:po + pn, 0:nq], AF.Ln),
                         reads=[bPS[sb_]], writes=[b_tt[m]])
                    S.op("act", lambda h, m=m: h.activation(tt[m][po:po + pn, 0:nq], tt[m][po:po + pn, 0:nq], AF.Exp, scale=-1.0),
                         reads=[b_tt[m]], writes=[b_tt[m]])
                    S.op("dve", lambda h, m=m, ob=ob: h.tensor_tensor(tt[m][po:po + pn, 0:nq], tt[m][po:po + pn, 0:nq], PS[ob][po:po + pn, 0:nq], ALU.mult),
                         reads=[bPS[ob], b_tt[m]], writes=[b_tt[m]])
                oi = ctr["os"] % 2
                ctr["os"] += 1
                o_, bo_ = ost[oi], b_ost[oi]
                gq0 = sq_["q0"] + qc0
                if kind == "mla":
                    S.op("act", lambda h, o_=o_: h.copy(o_[po:po + 64, 0:nq], tt[0][po:po + 64, 0:nq]), reads=[b_tt[0]], writes=[bo_])
                    S.dma("sp", AOm[hd * 64:(hd + 1) * 64, gq0:gq0 + nq], o_[po:po + 64, 0:nq], reads=[bo_])
                else:
                    S.op("dve", lambda h: h.scalar_tensor_tensor(tt[2][:, 0:nq], tt[1][:, 0:nq], neg_lam, tt[0][:, 0:nq], ALU.mult, ALU.add),
                         reads=[b_tt[0], b_tt[1], b_prm], writes=[b_tt[2]])
                    S.op("act", lambda h: h.activation(sqd[:, 0:nq], tt[2][:, 0:nq], AF.Square), reads=[b_tt[2]], writes=[b_sqd])
                    S.op("pe", lambda h: h.matmul(PS[6][:, 0:nq], ones_bf[:], sqd[:, 0:nq], start=True, stop=True),
                         reads=[b_sqd, b_const], writes=[bPS[6]])
                    S.op("act", lambda h: h.activation(tt[3][:, 0:nq], PS[6][:, 0:nq], AF.Sqrt, bias=eps_t[:, 0:1], scale=1.0 / 128),
                         reads=[bPS[6], b_const], writes=[b_tt[3]])
                    S.op("dve", lambda h: h.reciprocal(tt[3][:, 0:nq], tt[3][:, 0:nq]), reads=[b_tt[3]], writes=[b_tt[3]])
                    S.op("dve", lambda h, o_=o_: h.scalar_tensor_tensor(o_[:, 0:nq], tt[2][:, 0:nq], gsub[:, 0:1], tt[3][:, 0:nq], ALU.mult, ALU.mult),
                         reads=[b_tt[2], b_tt[3], b_prm], writes=[bo_])
                    S.dma("sp", AOd[hd * 128:(hd + 1) * 128, gq0:gq0 + nq], o_[:, 0:nq], reads=[bo_])

            work = []
            for sq_ in seqs:
                for hd in range(16):
                    work.append(("mla", sq_, hd))
                for hd in range(8):
                    work.append(("diff", sq_, hd))
            for wi, (kind, sq_, hd) in enumerate(work):
                if wi == 0:
                    load_head(kind, sq_, hd, 0)
                if wi + 1 < len(work):
                    k2, s2, h2 = work[wi + 1]
                    load_head(k2, s2, h2, (wi + 1) % 2)
                for qb in range(sq_["nq"] // sq_["QB"]):
                    attend_block(kind, sq_, hd, wi % 2, qb, [0] if kind == "mla" else [0, 1])
            S.barrier()


        def mixC_phase(l):
            Wg_ = wview(0, [8, 3072])
            Wmo = wview(24576, [8, 1024])
            Wdo = wview(32768, [8, 1024])
            Wpw = wview(40960, [8, 1024])
            Wou = wview(49152, [8, 1024])
            yb = wview(57344, [8, 512])
            mrg = wview(61440, [8, 512])
            PB0 = 65536 * 2

            def wpar(off, shape):
                n = int(np.prod(shape))
                a = WREG[:, (PB0 + off) // 2:(PB0 + off) // 2 + n * 2].bitcast(F32)
                if len(shape) == 2:
                    a = a.rearrange("p (a b) -> p a b", a=shape[0])
                return a
            wdw = wpar(0, [8, 32])
            bdw = wpar(1024, [8])
            lng = wpar(1056, [8])
            lnb = wpar(1088, [8])
            bpw = wpar(1120, [8])
            bgt = wpar(1152, [24])
            b_wm = [Buf() for _ in range(5)]
            b_par = Buf()
            for c in range(8):
                S.dma("pool", Wg_[:, c, :], Wd["w_branch_gate"][l][c * 128:(c + 1) * 128, :], writes=[b_wm[0]], war=[b_W])
            for wi_, (wv_, nm) in enumerate(((Wmo, "mla_w_o"), (Wdo, "diff_w_o"), (Wpw, "conv_w_pw2"), (Wou, "w_out"))):
                S.dma("pool", wv_, Wd[nm][l].rearrange("(c p) m -> p c m", p=128), writes=[b_wm[1 + wi_]], war=[b_W])
            for c in range(8):
                load_fm(wdw[:, c, 0:31], Wd["conv_w_dw"][l][:, c * 128:(c + 1) * 128], 31, b_par)
            load_fm(bdw, Wd["conv_b_dw"][l].rearrange("(c p) -> c p", p=128), 8, b_par)
            load_fm(lng, Wd["conv_ln_g"][l].rearrange("(c p) -> c p", p=128), 8, b_par)
            load_fm(lnb, Wd["conv_ln_b"][l].rearrange("(c p) -> c p", p=128), 8, b_par)
            load_fm(bpw, Wd["conv_b_pw2"][l].rearrange("(c p) -> c p", p=128), 8, b_par)
            load_fm(bgt, Wd["b_branch_gate"][l].rearrange("(c p) -> c p", p=128), 24, b_par)
            S.barrier()

            xT = aview(0, [8, 512], F32)
            uT = aview(16384, [8, 512], BF16)
            sq = [aview(24576, [512], BF16), aview(25600, [512], BF16)]
            rs = aview(26624, [512], F32)
            tmp = [aview(28672, [512], F32), aview(30720, [512], F32)]
            aom = aview(32768, [8, 512], BF16)
            aod = aview(40960, [8, 512], BF16)
            xin = [aview(49152, [544], F32), aview(49152 + 2176, [544], F32)]
            gt3 = aview(53504, [3, 512], BF16)
            xr = [aview(56576, [512], F32), aview(58624, [512], F32)]
            cst = aview(32768, [D], F32)
            b_xT, b_uT, b_rs, b_aom, b_aod, b_gt3, b_yb, b_mrg, b_cst = [Buf() for _ in range(9)]
            b_sq, b_tmp, b_xin, b_xr = [Buf(), Buf()], [Buf(), Buf()], [Buf(), Buf()], [Buf(), Buf()]
            b_cv = [Buf() for _ in range(8)]

            def tile_C(tile):
                N, col0, prompt = tile["n"], tile["col0"], tile["prompt"]
                segs = tile["segs"]
                nseg = len(segs)
                sl = segs[0][2]
                last_prompt = prompt and (col0 + N == T)
                S.dma("sp", xT[:, :, :N], XTv[:, :, col0:col0 + N], reads=[b_XT], writes=[b_xT], war=[b_A])
                norm_mod(tile, l, 1, xT, uT, b_xT, b_uT, sq, b_sq, rs, b_rs, tmp, b_tmp, 0)
                for c in range(8):
                    xi, bxi = xin[c % 2], b_xin[c % 2]
                    x3 = xi[:, 0:nseg * (CST + sl)].rearrange("p (s n) -> p s n", s=nseg)
                    if prompt:
                        if col0 == 0:
                            S.op("dve", lambda h, x3=x3: h.memset(x3[:, 0, 0:CST], 0.0), writes=[bxi])
                            S.dma("sp", x3[:, 0, CST:CST + N], GLUT[c * 128:(c + 1) * 128, 0:N], writes=[bxi])
                        else:
                            S.dma("sp", x3[:, 0, 0:CST + N], GLUT[c * 128:(c + 1) * 128, col0 - CST:col0 + N], writes=[bxi])
                    else:
                        S.dma("sp", x3[:, :, CST:CST + sl], GLUT[c * 128:(c + 1) * 128, col0:col0 + N].rearrange("p (s n) -> p s n", s=nseg),
                              writes=[bxi])
                        for si in range(nseg):
                            S.dma("sp", cst[0:CST, c * 128:(c + 1) * 128], c_conv[l, si, :, c * 128:(c + 1) * 128], writes=[b_cst])
                            S.op("pe", lambda h, c=c: h.transpose(PS[1][:, 0:CST], cst[0:CST, c * 128:(c + 1) * 128], ident_f[0:CST, 0:CST]),
                                 reads=[b_cst, b_const], writes=[bPS[1]])
                            S.op("act", lambda h, x3=x3, si=si: h.copy(x3[:, si, 0:CST], PS[1][:, 0:CST]), reads=[bPS[1]], writes=[bxi])
                    cv = xT[:, c, :N].rearrange("p (s n) -> p s n", s=nseg)
                    ceng = "dve"
                    S.op(ceng, lambda h, c=c, x3=x3, cv=cv: h.tensor_scalar(cv, x3[:, :, 0:sl], wdw[:, c, 0:1], bdw[:, c:c + 1], ALU.mult, ALU.add),
                         reads=[bxi, b_par], writes=[b_cv[c]], war=[b_xT])
                    for k_ in range(1, CONVW):
                        S.op(ceng, lambda h, c=c, k_=k_, x3=x3, cv=cv: h.scalar_tensor_tensor(
                            cv, x3[:, :, k_:k_ + sl], wdw[:, c, k_:k_ + 1], cv, ALU.mult, ALU.add),
                            reads=[bxi, b_par, b_cv[c]], writes=[b_cv[c]])
                    if last_prompt or not prompt:
                        for si in range(nseg):
                            S.op("pe", lambda h, x3=x3, si=si: h.transpose(PS[2][0:32, 0:128], x3[:, si, CST + sl - 32:CST + sl], ident_f[:]),
                                 reads=[bxi, b_const], writes=[bPS[2]])
                            t_, bt_ = tmp[si % 2], b_tmp[si % 2]
                            S.op("act", lambda h, t_=t_: h.copy(t_[0:32, 0:128], PS[2][0:32, 0:128]), reads=[bPS[2]], writes=[bt_])
                            dst = o_conv_p[l, :, c * 128:(c + 1) * 128] if prompt else o_conv_s[l, si, :, c * 128:(c + 1) * 128]
                            S.dma("sp", dst, t_[2:32, 0:128], reads=[bt_])
                    S.op("act", lambda h, c=c: h.copy(sq[0][:, :N], xT[:, c, :N]), reads=[b_cv[c]], writes=[b_sq[0]])
                    S.op("act", lambda h, c=c: h.activation(sq[1][:, :N], xT[:, c, :N], AF.Square), reads=[b_cv[c]], writes=[b_sq[1]])
                    S.op("pe", lambda h, c=c: h.matmul(PS[3][:, :N], ones_bf[:], sq[0][:, :N], start=(c == 0), stop=(c == 7)),
                         reads=[b_sq[0], b_const], writes=[bPS[3]])
                    S.op("pe", lambda h, c=c: h.matmul(PS[4][:, :N], ones_bf[:], sq[1][:, :N], start=(c == 0), stop=(c == 7)),
                         reads=[b_sq[1], b_const], writes=[bPS[4]])
                nmu, var_ = tmp[0], tmp[1]
                S.op("act", lambda h: h.activation(nmu[:, :N], PS[3][:, :N], AF.Copy, scale=-1.0 / D), reads=[bPS[3]], writes=[b_tmp[0]])
                S.op("dve", lambda h: h.tensor_tensor(var_[:, :N], nmu[:, :N], nmu[:, :N], ALU.mult), reads=[b_tmp[0]], writes=[b_tmp[1]])
                S.op("dve", lambda h: h.scalar_tensor_tensor(var_[:, :N], PS[4][:, :N], 1.0 / D, var_[:, :N], ALU.mult, ALU.subtract),
                     reads=[bPS[4], b_tmp[1]], writes=[b_tmp[1]])
                S.op("dve", lambda h: h.tensor_scalar_max(var_[:, :N], var_[:, :N], 0.0), reads=[b_tmp[1]], writes=[b_tmp[1]])
                S.op("act", lambda h: h.activation(rs[:, :N], var_[:, :N], AF.Sqrt, bias=eps_t[:, 0:1], scale=1.0),
                     reads=[b_tmp[1], b_const], writes=[b_rs])
                S.op("dve", lambda h: h.reciprocal(rs[:, :N], rs[:, :N]), reads=[b_rs], writes=[b_rs])
                for c in range(8):
                    S.op("dve", lambda h, c=c: h.tensor_tensor(xT[:, c, :N], xT[:, c, :N], nmu[:, :N], ALU.add), reads=[b_cv[c], b_tmp[0]], writes=[b_cv[c]])
                    S.op("dve", lambda h, c=c: h.tensor_tensor(xT[:, c, :N], xT[:, c, :N], rs[:, :N], ALU.mult), reads=[b_cv[c], b_rs], writes=[b_cv[c]])
                    S.op("act", lambda h, c=c: h.activation(yb[:, c, :N], xT[:, c, :N], AF.Silu, bias=lnb[:, c:c + 1], scale=lng[:, c:c + 1]),
                         reads=[b_cv[c], b_par], writes=[b_yb])
                S.dma("sp", aom[:, :, :N], AOm.rearrange("(c p) n -> p c n", p=128)[:, :, col0:col0 + N], writes=[b_aom])
                S.dma("sp", aod[:, :, :N], AOd.rearrange("(c p) n -> p c n", p=128)[:, :, col0:col0 + N], writes=[b_aod])
                for oc in range(8):
                    for gi in range(3):
                        def mmgate(h, gi=gi, oc=oc):
                            ins = None
                            for k_ in range(8):
                                ins = h.matmul(PS[1 + gi][:, :N], Wg_[:, k_, (gi * 8 + oc) * 128:(gi * 8 + oc + 1) * 128], uT[:, k_, :N],
                                               start=(k_ == 0), stop=(k_ == 7))
                            return ins
                        S.op("pe", mmgate, reads=[b_wm[0], b_uT, b_W], writes=[bPS[1 + gi]])
                        S.op("act", lambda h, gi=gi, oc=oc: h.activation(gt3[:, gi, :N], PS[1 + gi][:, :N], AF.Sigmoid,
                                                                         bias=bgt[:, gi * 8 + oc:gi * 8 + oc + 1], scale=1.0),
                             reads=[bPS[1 + gi], b_par], writes=[b_gt3])
                    for bi_, (wv_, src, bsrc) in enumerate(((Wmo, aom, b_aom), (Wdo, aod, b_aod), (Wpw, yb, b_yb))):
                        def mmbr(h, bi_=bi_, wv_=wv_, src=src, oc=oc):
                            ins = None
                            for k_ in range(8):
                                ins = h.matmul(PS[4 + bi_][:, :N], wv_[:, k_, oc * 128:(oc + 1) * 128], src[:, k_, :N],
                                               start=(k_ == 0), stop=(k_ == 7))
                            return ins
                        S.op("pe", mmbr, reads=[b_wm[1 + bi_], bsrc, b_W], writes=[bPS[4 + bi_]])
                    t0, t1 = tmp[0], tmp[1]
                    S.op("dve", lambda h: h.tensor_tensor(t0[:, :N], gt3[:, 0, :N], PS[4][:, :N], ALU.mult), reads=[b_gt3, bPS[4]], writes=[b_tmp[0]])
                    S.op("dve", lambda h: h.tensor_tensor(t1[:, :N], gt3[:, 1, :N], PS[5][:, :N], ALU.mult), reads=[b_gt3, bPS[5]], writes=[b_tmp[1]])
                    S.op("dve", lambda h: h.tensor_tensor(t0[:, :N], t0[:, :N], t1[:, :N], ALU.add), reads=[b_tmp[0], b_tmp[1]], writes=[b_tmp[0]])
                    S.op("dve", lambda h, oc=oc: h.scalar_tensor_tensor(t1[:, :N], PS[6][:, :N], bpw[:, oc:oc + 1], gt3[:, 2, :N], ALU.add, ALU.mult),
                         reads=[bPS[6], b_par, b_gt3], writes=[b_tmp[1]])
                    S.op("dve", lambda h, oc=oc: h.tensor_tensor(mrg[:, oc, :N], t0[:, :N], t1[:, :N], ALU.add), reads=[b_tmp[0], b_tmp[1]], writes=[b_mrg])
                for oc in range(8):
                    py = 1 + (oc % 6)

                    def mmo(h, oc=oc, py=py):
                        ins = None
                        for k_ in range(8):
                            ins = h.matmul(PS[py][:, :N], Wou[:, k_, oc * 128:(oc + 1) * 128], mrg[:, k_, :N], start=(k_ == 0), stop=(k_ == 7))
                        return ins
                    S.op("pe", mmo, reads=[b_wm[4], b_mrg, b_W], writes=[bPS[py]])
                    S.op("act", lambda h, oc=oc, py=py: h.copy(xT[:, oc, :N], PS[py][:, :N]), reads=[bPS[py]], writes=[b_xT], war=b_cv)
                    S.op("act", lambda h, oc=oc, py=py: h.activation(sq[oc % 2][:, :N], PS[py][:, :N], AF.Square), reads=[bPS[py]], writes=[b_sq[oc % 2]])
                    S.op("pe", lambda h, oc=oc: h.matmul(PS[7][:, :N], ones_bf[:], sq[oc % 2][:, :N], start=(oc == 0), stop=(oc == 7)),
                         reads=[b_sq[oc % 2], b_const], writes=[bPS[7]])
                post_norm_residual(tile, l, 1, xT, b_xT, xr, b_xr, rs, b_rs, tmp, b_tmp, 7)

            for tile in tiles:
                tile_C(tile)
            S.barrier()

        def store_y_phase():
            xi = [aview(0, [8, 128], F32), aview(4096, [8, 128], F32)]
            b_xi = [Buf(), Buf()]
            xo = [aview(8192, [D], F32), aview(8192 + 4096, [D], F32)]
            b_xo = [Buf(), Buf()]
            blocks = [(y_p, i * 128, i * 128) for i in range(T // 128)] + \
                     [(y_s, i * 128, T + i * 128) for i in range(NSTOK // 128)]
            for bi, (dst, r0, col) in enumerate(blocks):
                xi_, bx = xi[bi % 2], b_xi[bi % 2]
                S.dma("sp", xi_, XTv[:, :, col:col + 128], reads=[b_XT], writes=[bx], war=[b_A])
                o, bo = xo[bi % 2], b_xo[bi % 2]
                for half in range(2):
                    pb = 2 * (bi % 2) + half
                    S.op("pe", lambda h, xi_=xi_, half=half, pb=pb: [
                        h.transpose(PS[pb][:, c * 128:(c + 1) * 128], xi_[:, half * 4 + c, :], ident_f[:])
                        for c in range(4)][-1], reads=[bx, b_const], writes=[bPS[pb]])
                    if half == 0:
                        S.op("act", lambda h, o=o, pb=pb: h.copy(o[:, 0:512], PS[pb][:, :]), reads=[bPS[pb]], writes=[bo])
                    else:
                        S.op("dve", lambda h, o=o, pb=pb: h.tensor_copy(o[:, 512:1024], PS[pb][:, :]), reads=[bPS[pb]], writes=[bo])
                S.dma("sp", dst[r0:r0 + 128, :], o, reads=[bo])

        ada_phase()
        S.barrier()
        load_x_phase()
        S.barrier()
        for l in range(L):
            if "ffn1" in cfg.phases:
                ffn_phase(l, 0)
            if "mixB" in cfg.phases or "mix" in cfg.phases:
                mixB_phase(l)
            if "attn" in cfg.phases or "mix" in cfg.phases:
                attn_phase(l)
            if "mixC" in cfg.phases or "mix" in cfg.phases:
                mixC_phase(l)
            if "ffn2" in cfg.phases:
                ffn_phase(l, 1)
        if cfg.dbg:
            S.dma("sp", dbg_out["dbg_xt"], XT, reads=[b_XT])
        S.barrier()
        store_y_phase()
        S.finish()
        S.emit()
        print("sched: ops=%d waits=%d" % (S.n_ops, S.n_wait))
    return nc


def make_in_maps(inputs, cfg, ncores):
    NS, T = cfg.NS, cfg.T
    maps = []
    for b in range(ncores):
        m = {}
        m["x_prompt"] = np.ascontiguousarray(inputs["x_prompt"][b])
        m["x_sample"] = np.ascontiguousarray(inputs["x_sample"][b * NS:(b + 1) * NS]).reshape(NS * TS, D)
        m["c_all"] = np.ascontiguousarray(
            np.concatenate([inputs["c_prompt"][b:b + 1], inputs["c_sample"][b * NS:(b + 1) * NS]], axis=0))
        m["cache_mla_ckv"] = np.ascontiguousarray(inputs["cache_mla_ckv"][:, b * NS:(b + 1) * NS])
        m["cache_mla_kpe"] = np.ascontiguousarray(inputs["cache_mla_kpe"][:, b * NS:(b + 1) * NS])
        m["cache_diff_k"] = np.ascontiguousarray(inputs["cache_diff_k"][:, b * NS:(b + 1) * NS]).reshape(
            DEPTH, NS, cfg.PAST, 1024)
        m["cache_diff_v"] = np.ascontiguousarray(inputs["cache_diff_v"][:, b * NS:(b + 1) * NS]).reshape(
            DEPTH, NS, cfg.PAST, 1024)
        m["state_conv"] = np.ascontiguousarray(inputs["state_conv"][:, b * NS:(b + 1) * NS])
        for name, _ in WEIGHT_SPECS:
            m[name] = np.ascontiguousarray(inputs[name])
        maps.append(m)
    return maps


def gather_outputs(results, cfg, ncores):
    NS, T, L = cfg.NS, cfg.T, DEPTH

    def stk(key, shape_fn=None, axis=0):
        return [r[key] for r in results]
    y_p = np.stack([r["y_prompt"] for r in results], 0)
    y_s = np.concatenate([r["y_sample"].reshape(NS, TS, D) for r in results], 0)
    ckv_p = np.stack([r["o_ckv_p"] for r in results], 1)
    kpe_p = np.stack([r["o_kpe_p"] for r in results], 1)
    dk_p = np.stack([r["o_dk_p"].reshape(L, T, DIFF_H, 2, DIFF_HD) for r in results], 1)
    dv_p = np.stack([r["o_dv_p"].reshape(L, T, DIFF_H, 128) for r in results], 1)
    conv_p = np.stack([r["o_conv_p"] for r in results], 1)
    ckv_s = np.concatenate([r["o_ckv_s"].reshape(L, NS, TS, MLA_KV) for r in results], 1)
    kpe_s = np.concatenate([r["o_kpe_s"].reshape(L, NS, TS, MLA_ROPE) for r in results], 1)
    dk_s = np.concatenate([r["o_dk_s"].reshape(L, NS, TS, DIFF_H, 2, DIFF_HD) for r in results], 1)
    dv_s = np.concatenate([r["o_dv_s"].reshape(L, NS, TS, DIFF_H, 128) for r in results], 1)
    conv_s = np.concatenate([r["o_conv_s"] for r in results], 1)
    return (y_p, y_s, ckv_p, kpe_p, dk_p, dv_p, conv_p, ckv_s, kpe_s, dk_s, dv_s, conv_s)


def kernel(**inputs):
    cfg = Cfg()
    ncores = 8
    nc = build_program(cfg)
    in_maps = make_in_maps(inputs, cfg, ncores)
    res = run_bass_kernel_spmd(nc, in_maps, core_ids=list(range(ncores)))
    outs = gather_outputs(res.results, cfg, ncores)
    return tuple(np.ascontiguousarray(o, dtype=np.float32) for o in outs)
```

```python
import math
from contextlib import ExitStack

import numpy as np
import concourse.bass as bass
import concourse.mybir as mybir
from concourse.bass_utils import run_bass_kernel_spmd

F32 = mybir.dt.float32
BF16 = mybir.dt.bfloat16
I32 = mybir.dt.int32
AF = mybir.ActivationFunctionType
ALU = mybir.AluOpType
AX = mybir.AxisListType

D = 1024
DFF = 2816
NADA = 9
CHUNK = 64
EPS = 1e-6
THETA = 10000.0
MLA_H, MLA_NOPE, MLA_ROPE, MLA_V, MLA_KV, MLA_Q = 16, 64, 32, 64, 256, 512
MLA_SCALE = (MLA_NOPE + MLA_ROPE) ** -0.5
DIFF_H, DIFF_HD = 8, 64
DIFF_SCALE = DIFF_HD ** -0.5
CONVW = 31
CST = 30
DIN = 5920
C_CQ, C_CKV, C_KPE, C_DQ, C_DK, C_DV, C_CV = 0, 512, 768, 800, 1824, 2848, 3872
TS = 64
DEPTH = 2


class Buf:
    __slots__ = ("name", "w", "r")

    def __init__(self, name=""):
        self.name = name
        self.w = None
        self.r = {}


class Sched:
    ENGS = ("pe", "act", "dve", "pool", "sp")

    def __init__(self, nc, stack, n_dma_sems=14):
        self.nc = nc
        self.sems = {}
        self.cnt = {}
        self.ops = {e: [] for e in self.ENGS}
        self.seen = {e: {} for e in self.ENGS}
        for e in ("pe", "act", "dve", "pool"):
            k = "E_" + e
            self.sems[k] = stack.enter_context(nc.semaphore(k))
            self.cnt[k] = 0
        self.dma_pool = {}
        self.dma_rr = {}
        for q in ("sp", "pool", "act"):
            keys = []
            for i in range(n_dma_sems):
                k = "D_%s_%d" % (q, i)
                self.sems[k] = stack.enter_context(nc.semaphore(k))
                self.cnt[k] = 0
                keys.append(k)
            self.dma_pool[q] = keys
            self.dma_rr[q] = 0
        self.n_wait = 0
        self.n_ops = 0

    def _need(self, eng, k, v, deng, deps):
        if eng == "pe" and deng == "pe":
            return
        if self.seen[eng].get(k, 0) >= v:
            return
        if deps.get(k, 0) < v:
            deps[k] = v

    def _collect(self, eng, reads, writes, war):
        deps = {}
        for b in reads:
            if b.w is not None:
                self._need(eng, b.w[0], b.w[1], b.w[2], deps)
        for b in writes:
            if b.w is not None:
                self._need(eng, b.w[0], b.w[1], b.w[2], deps)
            for k, (v, de) in b.r.items():
                self._need(eng, k, v, de, deps)
        for b in war:
            for k, (v, de) in b.r.items():
                self._need(eng, k, v, de, deps)
        return deps

    def _emit_waits(self, eng, deps):
        for k, v in deps.items():
            sem = self.sems[k]
            self.ops[eng].append(lambda h, sem=sem, v=v: h.wait_ge(sem, v))
            self.seen[eng][k] = v
            self.n_wait += 1

    def _commit(self, ident, reads, writes):
        k, v, e = ident
        for b in reads:
            b.r[k] = (v, e)
        for b in writes:
            b.w = ident
            b.r = {}

    def op(self, eng, fn, reads=(), writes=(), war=()):
        deps = self._collect(eng, reads, writes, war)
        self._emit_waits(eng, deps)
        k = "E_" + eng
        self.cnt[k] += 1
        v = self.cnt[k]
        sem = self.sems[k]
        self.ops[eng].append(lambda h, fn=fn, sem=sem: fn(h).then_inc(sem, 1))
        self._commit((k, v, eng), reads, writes)
        self.n_ops += 1

    def dma(self, q, out, in_, reads=(), writes=(), war=(), **kw):
        deps = self._collect(q, reads, writes, war)
        pool = self.dma_pool[q]
        k = pool[self.dma_rr[q] % len(pool)]
        self.dma_rr[q] += 1
        pv = self.cnt[k]
        if pv > 0 and self.seen[q].get(k, 0) < pv:
            deps[k] = max(deps.get(k, 0), pv)
        self._emit_waits(q, deps)
        self.cnt[k] += 16
        v = self.cnt[k]
        sem = self.sems[k]
        self.ops[q].append(
            lambda h, out=out, in_=in_, sem=sem, kw=kw: h.dma_start(out=out, in_=in_, **kw).then_inc(sem, 16))
        self._commit((k, v, q), reads, writes)
        self.n_ops += 1

    def barrier(self):
        for eng in self.ENGS:
            deps = {}
            for k, c in self.cnt.items():
                if c > 0 and self.seen[eng].get(k, 0) < c and not (eng == "pe" and k == "E_pe"):
                    deps[k] = c
            self._emit_waits(eng, deps)

    def finish(self):
        deps = {}
        for q, keys in self.dma_pool.items():
            for k in keys:
                if self.cnt[k] > 0 and self.seen["sp"].get(k, 0) < self.cnt[k]:
                    deps[k] = self.cnt[k]
        for e in ("pe", "act", "dve", "pool"):
            k = "E_" + e
            if self.cnt[k] > 0 and self.seen["sp"].get(k, 0) < self.cnt[k]:
                deps[k] = self.cnt[k]
        self._emit_waits("sp", deps)

    def emit(self):
        nc = self.nc
        with nc.Block() as block:
            @block.sync
            def _(h):
                for f in self.ops["sp"]:
                    f(h)

            @block.tensor
            def _(h):
                for f in self.ops["pe"]:
                    f(h)

            @block.scalar
            def _(h):
                for f in self.ops["act"]:
                    f(h)

            @block.vector
            def _(h):
                for f in self.ops["dve"]:
                    f(h)

            @block.gpsimd
            def _(h):
                for f in self.ops["pool"]:
                    f(h)


WEIGHT_SPECS = [
    ("ada_w", [D, NADA * D]), ("ada_b", [NADA * D]),
    ("ffn1_pre_g", [D]), ("ffn1_post_g", [D]),
    ("ffn1_w_gate", [D, DFF]), ("ffn1_w_up", [D, DFF]), ("ffn1_w_down", [DFF, D]),
    ("mix_pre_g", [D]), ("mix_post_g", [D]), ("w_in", [D, DIN]),
    ("mla_q_norm_g", [MLA_Q]), ("mla_w_uq", [MLA_Q, MLA_H * 96]), ("mla_kv_norm_g", [MLA_KV]),
    ("mla_w_uk", [MLA_KV, MLA_H * 64]), ("mla_w_uv", [MLA_KV, MLA_H * 64]), ("mla_w_o", [D, D]),
    ("diff_lq1", [64]), ("diff_lk1", [64]), ("diff_lq2", [64]), ("diff_lk2", [64]),
    ("diff_subln_g", [128]), ("diff_w_o", [D, D]),
    ("conv_w_dw", [CONVW, D]), ("conv_b_dw", [D]), ("conv_ln_g", [D]), ("conv_ln_b", [D]),
    ("conv_w_pw2", [D, D]), ("conv_b_pw2", [D]),
    ("w_branch_gate", [D, 3 * D]), ("b_branch_gate", [3 * D]), ("w_out", [D, D]),
    ("ffn2_pre_g", [D]), ("ffn2_post_g", [D]),
    ("ffn2_w_gate", [D, DFF]), ("ffn2_w_up", [D, DFF]), ("ffn2_w_down", [DFF, D]),
]


class Cfg:
    def __init__(self, T=4096, NS=4, PAST=4096, L=2, phases=("ffn1", "mix", "ffn2"), dbg=False):
        self.T, self.NS, self.PAST, self.L = T, NS, PAST, L
        self.phases = phases
        self.dbg = dbg
        self.NSTOK = NS * TS
        self.NTOK = T + self.NSTOK
        self.NR = 1 + NS
        self.KS = PAST + TS


def build_program(cfg):
    T, NS, PAST, L = cfg.T, cfg.NS, cfg.PAST, cfg.L
    NTOK, NSTOK, NR, KS = cfg.NTOK, cfg.NSTOK, cfg.NR, cfg.KS
    nc = bass.Bass("TRN2", target_bir_lowering=False)

    def din(name, shape, dt=F32):
        return nc.dram_tensor(name, list(shape), dt, kind="ExternalInput").ap()

    def dout(name, shape, dt=F32):
        return nc.dram_tensor(name, list(shape), dt, kind="ExternalOutput").ap()

    def dscr(name, shape, dt):
        return nc.dram_tensor(name, list(shape), dt).ap()

    x_p = din("x_prompt", [T, D])
    x_s = din("x_sample", [NSTOK, D])
    c_all = din("c_all", [NR, D])
    c_ckv = din("cache_mla_ckv", [DEPTH, NS, PAST, MLA_KV])
    c_kpe = din("cache_mla_kpe", [DEPTH, NS, PAST, MLA_ROPE])
    c_dk = din("cache_diff_k", [DEPTH, NS, PAST, 1024])
    c_dv = din("cache_diff_v", [DEPTH, NS, PAST, 1024])
    c_conv = din("state_conv", [DEPTH, NS, CST, D])
    Wd = {}
    for name, shp in WEIGHT_SPECS:
        Wd[name] = din(name, [2] + shp)

    y_p = dout("y_prompt", [T, D])
    y_s = dout("y_sample", [NSTOK, D])
    o_ckv_p = dout("o_ckv_p", [DEPTH, T, MLA_KV])
    o_kpe_p = dout("o_kpe_p", [DEPTH, T, MLA_ROPE])
    o_dk_p = dout("o_dk_p", [DEPTH, T, 1024])
    o_dv_p = dout("o_dv_p", [DEPTH, T, 1024])
    o_conv_p = dout("o_conv_p", [DEPTH, CST, D])
    o_ckv_s = dout("o_ckv_s", [DEPTH, NSTOK, MLA_KV])
    o_kpe_s = dout("o_kpe_s", [DEPTH, NSTOK, MLA_ROPE])
    o_dk_s = dout("o_dk_s", [DEPTH, NSTOK, 1024])
    o_dv_s = dout("o_dv_s", [DEPTH, NSTOK, 1024])
    o_conv_s = dout("o_conv_s", [DEPTH, NS, CST, D])

    XT = dscr("XT", [D, NTOK], F32)
    NSP = max(NS, 1)
    QTd = dscr("QTd", [8, 128, NTOK], BF16)
    KTd_p = dscr("KTd_p", [8, 128, T], BF16)
    KTd_s = dscr("KTd_s", [NSP, 8, 128, KS], BF16)
    Vd_p = dscr("Vd_p", [T, 1024], BF16)
    Vd_s = dscr("Vd_s", [NSP, KS, 1024], BF16)
    QTm = dscr("QTm", [16, 96, NTOK], BF16)
    KTm_p = dscr("KTm_p", [1024, T], BF16)
    KTm_s = dscr("KTm_s", [NSP, 1024, KS], BF16)
    KPT_p = dscr("KPT_p", [32, T], BF16)
    KPT_s = dscr("KPT_s", [NSP, 32, KS], BF16)
    Vm_p = dscr("Vm_p", [T, 1024], BF16)
    Vm_s = dscr("Vm_s", [NSP, KS, 1024], BF16)
    GLUT = dscr("GLUT", [D, NTOK], F32)
    AOm = dscr("AOm", [D, NTOK], BF16)
    AOd = dscr("AOd", [D, NTOK], BF16)
    b_XT = Buf("XT")
    b_XT2 = Buf("XT_st")
    dbg_out = {}
    if cfg.dbg:
        dbg_out["dbg_xt"] = dout("dbg_xt", [D, NTOK])

    with ExitStack() as st:
        S = Sched(nc, st)
        dumped = set()

        def dbg_dump(name, ap, shape, dt, reads):
            if not cfg.dbg or name in dumped:
                return
            dumped.add(name)
            o = dout(name, shape, dt)
            S.dma("sp", o, ap, reads=reads)

        def sb(name, shape, dt):
            return st.enter_context(nc.sbuf_tensor(name, list(shape), dt))

        PS = [st.enter_context(nc.psum_tensor("ps%d" % i, [128, 512], F32)) for i in range(8)]
        bPS = [Buf("ps%d" % i) for i in range(8)]

        ones_bf = sb("ones_bf", [128, 128], BF16)
        ones_f = sb("ones_f", [128, 128], F32)
        ident_f = sb("ident_f", [128, 128], F32)
        ident_bf = sb("ident_bf", [128, 128], BF16)
        eps_t = sb("eps_t", [128, 1], F32)
        b_const = Buf("const")
        S.op("dve", lambda h: h.memset(ones_f[:], 1.0), writes=[b_const])
        S.op("dve", lambda h: h.memset(ones_bf[:], 1.0), writes=[b_const])
        S.op("dve", lambda h: h.memset(eps_t[:], EPS), writes=[b_const])
        S.op("pool", lambda h: h.affine_select(ident_f[:], ones_f[:], [[-1, 128]], ALU.is_equal, 0.0,
                                               base=0, channel_multiplier=1),
             reads=[b_const], writes=[b_const])
        S.op("dve", lambda h: h.tensor_copy(ident_bf[:], ident_f[:]), reads=[b_const], writes=[b_const])

        modsT = sb("modsT", [128, L, 72, NR], F32)
        gvec = sb("gvec", [128, L, 6, 8], F32)
        gsc = sb("gsc", [128, L, 3, 8, NR], F32)
        gpo = sb("gpo", [128, L, 3, 8, NR], F32)
        b_mods = Buf("mods")
        b_gvec = Buf("gvec")
        WREG = sb("WREG", [128, 67584], BF16)
        b_W = Buf("Wregion")
        AREG = sb("AREG", [128, 30720], BF16)
        b_A = Buf("Aregion")

        def aview(off_bytes, shape, dt):
            esz = 2 if dt == BF16 else 4
            n = int(np.prod(shape))
            a = AREG[:, off_bytes // 2: off_bytes // 2 + n * esz // 2]
            if dt != BF16:
                a = a.bitcast(dt)
            if len(shape) == 2:
                a = a.rearrange("p (a b) -> p a b", a=shape[0])
            elif len(shape) == 3:
                a = a.rearrange("p (a b c) -> p a b c", a=shape[0], b=shape[1])
            return a

        def wview(off_el, shape):
            n = int(np.prod(shape))
            a = WREG[:, off_el: off_el + n]
            if len(shape) == 2:
                a = a.rearrange("p (a b) -> p a b", a=shape[0])
            return a

        vstage = [sb("vstage0", [128, 128], F32), sb("vstage1", [128, 128], F32)]
        b_vstage = [Buf(), Buf()]
        vcnt = [0]

        def load_fm(dst, src2d, n, wbuf):
            i = vcnt[0] % 2
            vcnt[0] += 1
            stg, bs = vstage[i], b_vstage[i]
            S.dma("sp", stg[0:n, :], src2d, writes=[bs])
            S.op("pe", lambda h, stg=stg, n=n, i=i: h.transpose(PS[6 + i][:, 0:n], stg[0:n, :], ident_f[0:n, 0:n]),
                 reads=[bs, b_const], writes=[bPS[6 + i]])
            S.op("dve", lambda h, dst=dst, n=n, i=i: h.tensor_copy(dst, PS[6 + i][:, 0:n]), reads=[bPS[6 + i]], writes=[wbuf])

        def ada_phase():
            c_sb = aview(0, [D], F32)
            cT = aview(4096, [8, NR], F32)
            cTb = aview(4096 + 8 * NR * 4 + 64, [8, NR], BF16)
            adab = aview(8192, [L, 72], F32)
            b_c, b_cT, b_adab = Buf(), Buf(), Buf()
            S.dma("sp", c_sb[0:NR, :], c_all, writes=[b_c], war=[b_A])
            for l in range(L):
                load_fm(adab[:, l, :], Wd["ada_b"][l].rearrange("(m p) -> m p", p=128), 72, b_adab)
                for gi, nm in enumerate(("ffn1_pre_g", "ffn1_post_g", "mix_pre_g", "mix_post_g",
                                         "ffn2_pre_g", "ffn2_post_g")):
                    load_fm(gvec[:, l, gi, :], Wd[nm][l].rearrange("(c p) -> c p", p=128), 8, b_gvec)
            for c in range(8):
                S.op("pe", lambda h, c=c: h.transpose(PS[0][:, c * 8: c * 8 + NR], c_sb[0:NR, c * 128:(c + 1) * 128],
                                                      ident_f[0:NR, 0:NR]),
                     reads=[b_c, b_const], writes=[bPS[0]])
            S.op("act", lambda h: h.activation(cT[:, :, :], PS[0][:, 0:64].rearrange("p (c r) -> p c r", r=8)[:, :, 0:NR],
                                               AF.Silu),
                 reads=[bPS[0]], writes=[b_cT])
            S.op("dve", lambda h: h.tensor_copy(cTb[:, :, :], cT[:, :, :]), reads=[b_cT], writes=[b_cT])
            wbuf = [wview(0, [8, 1024]), wview(8192, [8, 1024])]
            b_wb = [Buf(), Buf()]
            gi = 0
            for l in range(L):
                for g in range(9):
                    wb, bw = wbuf[gi % 2], b_wb[gi % 2]
                    S.dma("pool", wb, Wd["ada_w"][l][:, g * 1024:(g + 1) * 1024].rearrange("(c p) m -> p c m", p=128),
                          writes=[bw], war=[b_W])
                    pb = 1 + gi % 2

                    def mm(h, wb=wb, pb=pb):
                        ins = None
                        for m in range(8):
                            for c in range(8):
                                ins = h.matmul(PS[pb][:, m * 8: m * 8 + NR], wb[:, c, m * 128:(m + 1) * 128],
                                               cTb[:, c, :], start=(c == 0), stop=(c == 7))
                        return ins
                    S.op("pe", mm, reads=[bw, b_cT], writes=[bPS[pb]])
                    S.op("dve", lambda h, l=l, g=g, pb=pb: h.tensor_tensor(
                        modsT[:, l, g * 8:(g + 1) * 8, :],
                        PS[pb][:, 0:64].rearrange("p (c r) -> p c r", r=8)[:, :, 0:NR],
                        adab[:, l, g * 8:(g + 1) * 8].unsqueeze(2).to_broadcast([128, 8, NR]), ALU.add),
                        reads=[bPS[pb], b_adab], writes=[b_mods])
                    gi += 1
            for l in range(L):
                for k in range(3):
                    sc = modsT[:, l, (3 * k + 1) * 8:(3 * k + 2) * 8, :]
                    gt = modsT[:, l, (3 * k + 2) * 8:(3 * k + 3) * 8, :]
                    pre = gvec[:, l, 2 * k, :].unsqueeze(2).to_broadcast([128, 8, NR])
                    post = gvec[:, l, 2 * k + 1, :].unsqueeze(2).to_broadcast([128, 8, NR])
                    wgt = 1.0 if k == 1 else 0.5
                    S.op("dve", lambda h, l=l, k=k, sc=sc, pre=pre: h.scalar_tensor_tensor(
                        gsc[:, l, k, :, :], sc, 1.0, pre, ALU.add, ALU.mult),
                        reads=[b_mods, b_gvec], writes=[b_mods])
                    S.op("dve", lambda h, l=l, k=k, gt=gt, post=post, wgt=wgt: h.scalar_tensor_tensor(
                        gpo[:, l, k, :, :], gt, wgt, post, ALU.mult, ALU.mult),
                        reads=[b_mods, b_gvec], writes=[b_mods])

        tiles = []
        import os
        TN = int(os.environ.get("TILE_N", "512"))
        for i in range(T // TN):
            tiles.append(dict(n=TN, col0=i * TN, segs=[(0, 0, TN)], prompt=True, idx=i))
        tiles.append(dict(n=NSTOK, col0=T, segs=[(1 + s, s * TS, TS) for s in range(NS)], prompt=False, idx=0))

        XTv = XT.rearrange("(c p) n -> p c n", p=128)

        def load_x_phase():
            xin = [aview(0, [D], F32), aview(4096, [D], F32)]
            b_xin = [Buf(), Buf()]
            xo = [aview(8192, [8, 128], F32), aview(8192 + 4096, [8, 128], F32)]
            b_xo = [Buf(), Buf()]
            blocks = [(x_p, i * 128, i * 128) for i in range(T // 128)] + \
                     [(x_s, i * 128, T + i * 128) for i in range(NSTOK // 128)]
            for bi, (src, r0, col) in enumerate(blocks):
                xi, bx = xin[bi % 2], b_xin[bi % 2]
                S.dma("sp", xi, src[r0:r0 + 128, :], writes=[bx], war=[b_A])
                for half in range(2):
                    pb = 2 * (bi % 2) + half
                    S.op("pe", lambda h, xi=xi, half=half, pb=pb: [
                        h.transpose(PS[pb][:, c * 128:(c + 1) * 128],
                                    xi[:, (half * 4 + c) * 128:(half * 4 + c + 1) * 128], ident_f[:])
                        for c in range(4)][-1], reads=[bx, b_const], writes=[bPS[pb]])
                    o, bo = xo[bi % 2], b_xo[bi % 2]
                    eng = "act" if half == 0 else "dve"
                    if half == 0:
                        S.op("act", lambda h, o=o, pb=pb: h.copy(o[:, 0:4, :], PS[pb][:, :].rearrange("p (c n) -> p c n", c=4)),
                             reads=[bPS[pb]], writes=[bo])
                    else:
                        S.op("dve", lambda h, o=o, pb=pb: h.tensor_copy(o[:, 4:8, :], PS[pb][:, :].rearrange("p (c n) -> p c n", c=4)),
                             reads=[bPS[pb]], writes=[bo])
                S.dma("sp", XTv[:, :, col:col + 128], xo[bi % 2], reads=[b_xo[bi % 2]], writes=[b_XT])

        A_XT = 0
        A_HT = 16384
        A_SQ = 24576
        A_RS = 26624
        A_TMP = 28672
        A_FREE = 32768

        def norm_mod(tile, l, k, xT, hT, b_xT, b_hT, sq, b_sq, rs, b_rs, tmp, b_tmp, psb):
            N = tile["n"]
            for c in range(8):
                S.op("act", lambda h, c=c: h.activation(sq[c % 2][:, :N], xT[:, c, :N], AF.Square),
                     reads=[b_xT], writes=[b_sq[c % 2]])
                S.op("pe", lambda h, c=c: h.matmul(PS[psb][:, :N], ones_bf[:], sq[c % 2][:, :N], start=(c == 0), stop=(c == 7)),
                     reads=[b_sq[c % 2], b_const], writes=[bPS[psb]])
            S.op("act", lambda h: h.activation(rs[:, :N], PS[psb][:, :N], AF.Sqrt, bias=eps_t[:, 0:1], scale=1.0 / D),
                 reads=[bPS[psb], b_const], writes=[b_rs])
            S.op("dve", lambda h: h.reciprocal(rs[:, :N], rs[:, :N]), reads=[b_rs], writes=[b_rs])
            for c in range(8):
                t, bt = tmp[c % 2], b_tmp[c % 2]
                S.op("dve", lambda h, c=c, t=t: h.tensor_tensor(t[:, :N], xT[:, c, :N], rs[:, :N], ALU.mult),
                     reads=[b_xT, b_rs], writes=[bt])
                for (r, c0, n) in tile["segs"]:
                    S.op("act", lambda h, c=c, t=t, r=r, c0=c0, n=n: h.activation(
                        hT[:, c, c0:c0 + n], t[:, c0:c0 + n], AF.Identity,
                        bias=modsT[:, l, (3 * k) * 8 + c, r:r + 1], scale=gsc[:, l, k, c, r:r + 1]),
                        reads=[bt, b_mods], writes=[b_hT])

        def post_norm_residual(tile, l, k, yT, b_yT, xr, b_xr, rs, b_rs, tmp, b_tmp, psb):
            N, col0 = tile["n"], tile["col0"]
            S.op("act", lambda h: h.activation(rs[:, :N], PS[psb][:, :N], AF.Sqrt, bias=eps_t[:, 0:1], scale=1.0 / D),
                 reads=[bPS[psb], b_const], writes=[b_rs])
            S.op("dve", lambda h: h.reciprocal(rs[:, :N], rs[:, :N]), reads=[b_rs], writes=[b_rs])
            for c in range(8):
                t, bt = tmp[c % 2], b_tmp[c % 2]
                x_, bx = xr[c % 2], b_xr[c % 2]
                S.dma("sp", x_[:, :N], XT[c * 128:(c + 1) * 128, col0:col0 + N], reads=[b_XT], writes=[bx])
                S.op("dve", lambda h, c=c, t=t: h.tensor_tensor(t[:, :N], yT[:, c, :N], rs[:, :N], ALU.mult),
                     reads=[b_yT, b_rs], writes=[bt])
                for (r, c0, n) in tile["segs"]:
                    S.op("dve", lambda h, c=c, t=t, x_=x_, r=r, c0=c0, n=n: h.scalar_tensor_tensor(
                        x_[:, c0:c0 + n], t[:, c0:c0 + n], gpo[:, l, k, c, r:r + 1], x_[:, c0:c0 + n],
                        ALU.mult, ALU.add), reads=[bt, b_mods, bx], writes=[bx])
                S.dma("sp", XT[c * 128:(c + 1) * 128, col0:col0 + N], x_[:, :N], reads=[bx])

        def ffn_phase(l, k):
            pre = "ffn1" if k == 0 else "ffn2"
            Wg = wview(0, [8, DFF])
            Wu = wview(8 * DFF, [8, DFF])
            Wdn = wview(16 * DFF, [22, D])
            b_wg = [Buf() for _ in range(8)]
            b_wu = [Buf() for _ in range(8)]
            b_wd = [Buf() for _ in range(22)]
            for c in range(8):
                S.dma("pool", Wg[:, c, :], Wd[pre + "_w_gate"][l][c * 128:(c + 1) * 128, :], writes=[b_wg[c]], war=[b_W])
                S.dma("pool", Wu[:, c, :], Wd[pre + "_w_up"][l][c * 128:(c + 1) * 128, :], writes=[b_wu[c]], war=[b_W])
            for j in range(22):
                S.dma("pool", Wdn[:, j, :], Wd[pre + "_w_down"][l][j * 128:(j + 1) * 128, :], writes=[b_wd[j]], war=[b_W])
            xT = aview(A_XT, [8, 512], F32)
            hT = aview(A_HT, [8, 512], BF16)
            sq = [aview(A_SQ, [512], BF16), aview(A_SQ + 1024, [512], BF16)]
            rs = aview(A_RS, [512], F32)
            tmp = [aview(A_TMP, [512], F32), aview(A_TMP + 2048, [512], F32)]
            aT = aview(A_FREE, [22, 512], BF16)
            sg = [aview(A_FREE + 22528, [512], BF16), aview(A_FREE + 22528 + 1024, [512], BF16)]
            yT = xT
            xr = [aview(A_FREE + 22528 + 2048, [512], F32), aview(A_FREE + 22528 + 4096, [512], F32)]
            b_xT, b_hT, b_rs, b_aT = Buf(), Buf(), Buf(), Buf()
            b_yT = b_xT
            b_sq, b_tmp, b_sg, b_xr = [Buf(), Buf()], [Buf(), Buf()], [Buf(), Buf()], [Buf(), Buf()]
            kk = 0 if k == 0 else 2
            def ffn_tile(tile):
                N, col0 = tile["n"], tile["col0"]
                S.dma("sp", xT[:, :, :N], XTv[:, :, col0:col0 + N], reads=[b_XT], writes=[b_xT], war=[b_A])
                norm_mod(tile, l, kk, xT, hT, b_xT, b_hT, sq, b_sq, rs, b_rs, tmp, b_tmp, 0)
                if os.environ.get("DEBUG_BARRIER2"):
                    S.barrier()
                dbg_dump("dbg_rs", rs[:, :N], [128, N], F32, [b_rs])
                dbg_dump("dbg_hT", hT[:, :, :N], [128, 8, N], BF16, [b_hT])
                for j in range(22):
                    pg, pu = 1 + 2 * (j % 3), 2 + 2 * (j % 3)

                    def mmg(h, j=j, pg=pg):
                        ins = None
                        for c in range(8):
                            ins = h.matmul(PS[pg][:, :N], Wg[:, c, j * 128:(j + 1) * 128], hT[:, c, :N],
                                           start=(c == 0), stop=(c == 7))
                        return ins

                    def mmu(h, j=j, pu=pu):
                        ins = None
                        for c in range(8):
                            ins = h.matmul(PS[pu][:, :N], Wu[:, c, j * 128:(j + 1) * 128], hT[:, c, :N],
                                           start=(c == 0), stop=(c == 7))
                        return ins
                    S.op("pe", mmg, reads=b_wg + [b_hT, b_W], writes=[bPS[pg]])
                    S.op("pe", mmu, reads=b_wu + [b_hT, b_W], writes=[bPS[pu]])
                    if os.environ.get("DEBUG_BARRIER"):
                        S.barrier()
                    S.op("act", lambda h, j=j, pg=pg: h.activation(sg[j % 2][:, :N], PS[pg][:, :N], AF.Silu),
                         reads=[bPS[pg]], writes=[b_sg[j % 2]])
                    S.op("dve", lambda h, j=j, pu=pu: h.tensor_tensor(aT[:, j, :N], sg[j % 2][:, :N], PS[pu][:, :N], ALU.mult),
                         reads=[b_sg[j % 2], bPS[pu]], writes=[b_aT])
                for c in range(8):
                    py = 1 + (c % 6)

                    def mmd(h, c=c, py=py):
                        ins = None
                        for j in range(22):
                            ins = h.matmul(PS[py][:, :N], Wdn[:, j, c * 128:(c + 1) * 128], aT[:, j, :N],
                                           start=(j == 0), stop=(j == 21))
                        return ins
                    S.op("pe", mmd, reads=b_wd + [b_aT, b_W], writes=[bPS[py]])
                    S.op("act", lambda h, c=c, py=py: h.copy(yT[:, c, :N], PS[py][:, :N]), reads=[bPS[py]], writes=[b_yT])
                    S.op("act", lambda h, c=c, py=py: h.activation(sq[c % 2][:, :N], PS[py][:, :N], AF.Square),
                         reads=[bPS[py]], writes=[b_sq[c % 2]])
                    S.op("pe", lambda h, c=c: h.matmul(PS[7][:, :N], ones_bf[:], sq[c % 2][:, :N], start=(c == 0), stop=(c == 7)),
                         reads=[b_sq[c % 2], b_const], writes=[bPS[7]])
                dbg_dump("dbg_aT", aT[:, :, :N], [128, 22, N], BF16, [b_aT])
                dbg_dump("dbg_sg", sg[1][:, :N], [128, N], BF16, [b_sg[1]])
                post_norm_residual(tile, l, kk, yT, b_yT, xr, b_xr, rs, b_rs, tmp, b_tmp, 7)

            for tile in tiles:
                ffn_tile(tile)
            S.barrier()


        NBP = T // 128
        NBLK = NBP + 1
        TWO_PI = 2.0 * math.pi

        def mixB_phase(l):
            Win = wview(0, [8, DIN])
            Wuq = wview(8 * DIN, [4, 1536])
            Wuk = wview(8 * DIN + 6144, [2, 1024])
            Wuv = wview(8 * DIN + 8192, [2, 1024])
            TB0 = (8 * DIN + 10240) * 2

            def wtab(off, shape, dt):
                esz = 2 if dt == BF16 else 4
                n = int(np.prod(shape))
                a = WREG[:, (TB0 + off) // 2:(TB0 + off) // 2 + n * esz // 2]
                if dt != BF16:
                    a = a.bitcast(dt)
                if len(shape) == 2:
                    a = a.rearrange("p (a b) -> p a b", a=shape[0])
                return a
            b_win = [Buf() for _ in range(8)]
            b_wu = Buf()
            for c in range(8):
                S.dma("pool", Win[:, c, :], Wd["w_in"][l][c * 128:(c + 1) * 128, :], writes=[b_win[c]], war=[b_W])
            for c in range(4):
                S.dma("pool", Wuq[:, c, :], Wd["mla_w_uq"][l][c * 128:(c + 1) * 128, :], writes=[b_wu], war=[b_W])
            for c in range(2):
                S.dma("pool", Wuk[:, c, :], Wd["mla_w_uk"][l][c * 128:(c + 1) * 128, :], writes=[b_wu], war=[b_W])
                S.dma("pool", Wuv[:, c, :], Wd["mla_w_uv"][l][c * 128:(c + 1) * 128, :], writes=[b_wu], war=[b_W])
            cosD = wtab(0, [NBLK, 32], F32)
            sinD = wtab(NBLK * 128, [NBLK, 32], F32)
            cosM = wtab(NBLK * 256, [NBLK, 16], F32)
            sinM = wtab(NBLK * 320, [NBLK, 16], F32)
            gkv = wtab(NBLK * 384, [256], F32)
            gq = wtab(NBLK * 384 + 1024, [4], F32)
            b_tab = Buf()
            posi = aview(0, [NBLK], I32)
            posf = aview(1024, [NBLK], F32)
            ji = aview(2048, [32], I32)
            invf = aview(2304, [32], F32)
            ang = aview(4096, [NBLK, 32], F32)
            kf = aview(4096 + NBLK * 128, [NBLK, 32], F32)
            ki = aview(4096 + NBLK * 256, [NBLK, 32], I32)
            b_t = Buf()
            S.op("pool", lambda h: h.iota(posi[:, 0:NBP], [[128, NBP]], base=0, channel_multiplier=1), writes=[b_t], war=[b_A])
            S.op("pool", lambda h: h.iota(posi[0:64, NBP:NBLK], [[0, 1]], base=PAST, channel_multiplier=1), writes=[b_t])
            S.op("pool", lambda h: h.iota(posi[64:128, NBP:NBLK], [[0, 1]], base=PAST, channel_multiplier=1), writes=[b_t])
            S.op("dve", lambda h: h.tensor_copy(posf[:, :], posi[:, :]), reads=[b_t], writes=[b_t])
            S.op("pool", lambda h: h.iota(ji[:, :], [[1, 32]], base=0, channel_multiplier=0), writes=[b_t])
            S.dma("sp", gkv, Wd["mla_kv_norm_g"][l].partition_broadcast(128), writes=[b_tab], war=[b_W])
            load_fm(gq, Wd["mla_q_norm_g"][l].rearrange("(c p) -> c p", p=128), 4, b_tab)

            def build_tab(dst, half, is_cos):
                hw = half
                S.op("dve", lambda h: h.tensor_copy(invf[:, 0:hw], ji[:, 0:hw]), reads=[b_t], writes=[b_t])
                S.op("act", lambda h: h.activation(invf[:, 0:hw], invf[:, 0:hw], AF.Exp, scale=-math.log(THETA) / hw),
                     reads=[b_t], writes=[b_t])
                a3, k3, i3 = ang[:, :, 0:hw], kf[:, :, 0:hw], ki[:, :, 0:hw]
                S.op("dve", lambda h: h.tensor_tensor(a3, posf.unsqueeze(2).to_broadcast([128, NBLK, hw]),
                                                      invf[:, 0:hw].unsqueeze(1).to_broadcast([128, NBLK, hw]), ALU.mult),
                     reads=[b_t], writes=[b_t])
                if is_cos:
                    S.op("dve", lambda h: h.tensor_scalar_add(a3, a3, math.pi / 2), reads=[b_t], writes=[b_t])
                S.op("dve", lambda h: h.tensor_scalar_mul(k3, a3, 1.0 / TWO_PI), reads=[b_t], writes=[b_t])
                S.op("dve", lambda h: h.tensor_copy(i3, k3), reads=[b_t], writes=[b_t])
                S.op("dve", lambda h: h.tensor_copy(k3, i3), reads=[b_t], writes=[b_t])
                S.op("dve", lambda h: h.scalar_tensor_tensor(a3, k3, -TWO_PI, a3, ALU.mult, ALU.add), reads=[b_t], writes=[b_t])
                S.op("dve", lambda h: h.tensor_scalar(k3, a3, math.pi, -TWO_PI, ALU.is_gt, ALU.mult), reads=[b_t], writes=[b_t])
                S.op("dve", lambda h: h.tensor_tensor(a3, a3, k3, ALU.add), reads=[b_t], writes=[b_t])
                S.op("dve", lambda h: h.tensor_scalar(k3, a3, -math.pi, TWO_PI, ALU.is_lt, ALU.mult), reads=[b_t], writes=[b_t])
                S.op("dve", lambda h: h.tensor_tensor(a3, a3, k3, ALU.add), reads=[b_t], writes=[b_t])
                S.op("dve", lambda h: h.tensor_scalar(a3, a3, -math.pi, math.pi, ALU.max, ALU.min), reads=[b_t], writes=[b_t])
                S.op("act", lambda h: h.activation(dst, a3, AF.Sin), reads=[b_t], writes=[b_tab])
            build_tab(cosD, 32, True)
            build_tab(sinD, 32, False)
            build_tab(cosM, 16, True)
            build_tab(sinM, 16, False)
            S.barrier()

            xT = aview(0, [8, 512], F32)
            uT = aview(16384, [8, 512], BF16)
            sq = [aview(24576, [512], BF16), aview(25600, [512], BF16)]
            rs = aview(26624, [512], F32)
            tmp = [aview(28672, [512], F32), aview(30720, [512], F32)]
            tmA = aview(32768, [1536], F32)
            tmB = aview(38912, [1024], F32)
            tmC = aview(43008, [1024], F32)
            tbf = aview(47104, [1536], BF16)
            stg = [aview(50176, [8, 128], BF16), aview(52224, [8, 128], BF16)]
            cqn = aview(54272, [4, 512], BF16)
            ckvT = aview(58368, [2, 512], BF16)
            small = aview(60416, [64], F32)
            b_xT, b_uT, b_rs, b_tmA, b_tmB, b_tmC, b_tbf, b_cqn, b_ckvT, b_small = [Buf() for _ in range(10)]
            b_sq, b_tmp, b_stg = [Buf(), Buf()], [Buf(), Buf()], [Buf(), Buf()]
            stgc = [0]

            def evac_T(pb, nrows, ncol_groups, dsts, bf_src_buf):
                i = stgc[0] % 2
                stgc[0] += 1
                sg_, bs_ = stg[i], b_stg[i]
                pv = PS[pb].bitcast(BF16)[0:nrows, 0:ncol_groups * 128].rearrange("p (g n) -> p g n", g=ncol_groups)
                S.op("act", lambda h: h.copy(sg_[0:nrows, 0:ncol_groups, :], pv), reads=[bPS[pb]], writes=[bs_])
                for (dst, lo, hi) in dsts:
                    S.dma("sp", dst, sg_[0:nrows, 0:ncol_groups, lo:hi], reads=[bs_])

            def rope_tm(src, dst, G, half, cosb, sinb, blk, rd, wr, scale=None):
                cb = cosb[:, blk, :].unsqueeze(1).unsqueeze(1).to_broadcast([128, G, 2, half])
                sb_ = sinb[:, blk, :].unsqueeze(1).unsqueeze(1).to_broadcast([128, G, 2, half])
                s4 = src.rearrange("p g (t j) -> p g t j", t=2)
                d4 = dst.rearrange("p g (t j) -> p g t j", t=2)
                xc = tmC[:, 0:G * 2 * half].rearrange("p (g t j) -> p g t j", g=G, t=2)
                xs = tmB[:, 0:G * 2 * half].rearrange("p (g t j) -> p g t j", g=G, t=2)
                S.op("dve", lambda h: h.tensor_tensor(xc, s4, cb, ALU.mult), reads=rd + [b_tab], writes=[b_tmC])
                S.op("dve", lambda h: h.tensor_tensor(xs, s4, sb_, ALU.mult), reads=rd + [b_tab], writes=[b_tmB])
                if scale is None:
                    S.op("dve", lambda h: h.tensor_tensor(d4[:, :, 0, :], xc[:, :, 0, :], xs[:, :, 1, :], ALU.subtract),
                         reads=[b_tmC, b_tmB], writes=wr)
                    S.op("dve", lambda h: h.tensor_tensor(d4[:, :, 1, :], xc[:, :, 1, :], xs[:, :, 0, :], ALU.add),
                         reads=[b_tmC, b_tmB], writes=wr)
                else:
                    S.op("dve", lambda h: h.tensor_tensor(xc[:, :, 0, :], xc[:, :, 0, :], xs[:, :, 1, :], ALU.subtract),
                         reads=[b_tmC, b_tmB], writes=[b_tmC])
                    S.op("dve", lambda h: h.tensor_tensor(xc[:, :, 1, :], xc[:, :, 1, :], xs[:, :, 0, :], ALU.add),
                         reads=[b_tmC, b_tmB], writes=[b_tmC])
                    S.op("act", lambda h: h.activation(d4, xc, AF.Copy, scale=scale), reads=[b_tmC], writes=wr)

            def tm_matmul(pb, tb, col_lo, ncols, lhs, nk, W, wbufs):
                def mm(h):
                    ins = None
                    for c in range(nk):
                        ins = h.matmul(PS[pb][:, 0:ncols], lhs[:, c, tb * 128:(tb + 1) * 128], W[:, c, col_lo:col_lo + ncols],
                                       start=(c == 0), stop=(c == nk - 1))
                    return ins
                return mm

            def tile_B(tile):
                N, col0 = tile["n"], tile["col0"]
                NTB = N // 128
                prompt = tile["prompt"]
                S.dma("sp", xT[:, :, :N], XTv[:, :, col0:col0 + N], reads=[b_XT], writes=[b_xT], war=[b_A])
                norm_mod(tile, l, 1, xT, uT, b_xT, b_uT, sq, b_sq, rs, b_rs, tmp, b_tmp, 0)

                def out_rows(o_p, o_s, tb, width):
                    r0 = (col0 if prompt else 0) + tb * 128
                    return (o_p if prompt else o_s)[l, r0:r0 + 128, :]

                for tb in range(NTB):
                    blk = (col0 // 128 + tb) if prompt else NBP
                    tokc = col0 + tb * 128
                    S.op("pe", tm_matmul(1, tb, C_CKV, 288, uT, 8, Win, b_win), reads=b_win + [b_uT, b_W], writes=[bPS[1]])
                    S.op("act", lambda h: h.activation(tmB[:, 0:256], PS[1][:, 0:256], AF.Square, accum_out=small[:, 0:1]),
                         reads=[bPS[1]], writes=[b_tmB, b_small])
                    S.op("act", lambda h: h.activation(small[:, 1:2], small[:, 0:1], AF.Sqrt, bias=eps_t[:, 0:1], scale=1.0 / MLA_KV),
                         reads=[b_small, b_const], writes=[b_small])
                    S.op("dve", lambda h: h.reciprocal(small[:, 2:3], small[:, 1:2]), reads=[b_small], writes=[b_small])
                    S.op("dve", lambda h: h.scalar_tensor_tensor(tmA[:, 0:256], PS[1][:, 0:256], small[:, 2:3], gkv[:, :],
                                                                 ALU.mult, ALU.mult),
                         reads=[bPS[1], b_small, b_tab], writes=[b_tmA])
                    S.dma("sp", out_rows(o_ckv_p, o_ckv_s, tb, 256), tmA[:, 0:256], reads=[b_tmA])
                    S.op("act", lambda h: h.copy(tbf[:, 0:256], tmA[:, 0:256]), reads=[b_tmA], writes=[b_tbf])
                    rope_tm(PS[1][:, 256:288].rearrange("p (g d) -> p g d", g=1), tmA[:, 256:288].rearrange("p (g d) -> p g d", g=1),
                            1, 16, cosM, sinM, blk, [bPS[1]], [b_tmA])
                    S.dma("sp", out_rows(o_kpe_p, o_kpe_s, tb, 32), tmA[:, 256:288], reads=[b_tmA])
                    S.op("act", lambda h: h.copy(tbf[:, 256:288], tmA[:, 256:288]), reads=[b_tmA], writes=[b_tbf])
                    S.op("pe", lambda h, tb=tb: [h.transpose(PS[2].bitcast(BF16)[:, c * 128:(c + 1) * 128], tbf[:, c * 128:(c + 1) * 128], ident_bf[:])
                                                 for c in range(2)][-1], reads=[b_tbf, b_const], writes=[bPS[2]])
                    S.op("dve", lambda h, tb=tb: h.tensor_copy(ckvT[:, :, tb * 128:(tb + 1) * 128],
                                                                PS[2].bitcast(BF16)[:, 0:256].rearrange("p (c n) -> p c n", c=2)),
                         reads=[bPS[2]], writes=[b_ckvT])
                    S.op("pe", lambda h: h.transpose(PS[3].bitcast(BF16)[0:32, 0:128], tbf[:, 256:288], ident_bf[:]),
                         reads=[b_tbf, b_const], writes=[bPS[3]])
                    if prompt:
                        kp_d = [(KPT_p[:, tokc:tokc + 128].rearrange("r (g n) -> r g n", g=1), 0, 128)]
                    else:
                        kp_d = [(KPT_s[2 * tb + e, :, PAST:PAST + 64].rearrange("r (g n) -> r g n", g=1), e * 64, e * 64 + 64) for e in range(2)]
                    evac_T(3, 32, 1, kp_d, None)
                    for which, cbase, scale in (("q", C_DQ, DIFF_SCALE), ("k", C_DK, None)):
                        for half_ in range(2):
                            pb = 4 + half_
                            S.op("pe", tm_matmul(pb, tb, cbase + half_ * 512, 512, uT, 8, Win, b_win),
                                 reads=b_win + [b_uT, b_W], writes=[bPS[pb]])
                            rope_tm(PS[pb][:, :].rearrange("p (g d) -> p g d", g=8),
                                    tmA[:, half_ * 512:(half_ + 1) * 512].rearrange("p (g d) -> p g d", g=8),
                                    8, 32, cosD, sinD, blk, [bPS[pb]], [b_tmA], scale=scale)
                        if which == "k":
                            S.dma("sp", out_rows(o_dk_p, o_dk_s, tb, 1024), tmA[:, 0:1024], reads=[b_tmA])
                        S.op("act", lambda h: h.copy(tbf[:, 0:1024], tmA[:, 0:1024]), reads=[b_tmA], writes=[b_tbf])
                        S.op("pe", lambda h: [h.transpose(PS[6].bitcast(BF16)[:, c * 128:(c + 1) * 128], tbf[:, c * 128:(c + 1) * 128], ident_bf[:])
                                              for c in range(8)][-1], reads=[b_tbf, b_const], writes=[bPS[6]])
                        if which == "q":
                            dd = [(QTd[:, :, tokc:tokc + 128].rearrange("h p n -> p h n"), 0, 128)]
                        elif prompt:
                            dd = [(KTd_p[:, :, tokc:tokc + 128].rearrange("h p n -> p h n"), 0, 128)]
                        else:
                            dd = [(KTd_s[2 * tb + e, :, :, PAST:PAST + 64].rearrange("h p n -> p h n"), e * 64, e * 64 + 64) for e in range(2)]
                        evac_T(6, 128, 8, dd, None)
                    for half_ in range(2):
                        pb = 4 + half_
                        S.op("pe", tm_matmul(pb, tb, C_DV + half_ * 512, 512, uT, 8, Win, b_win),
                             reads=b_win + [b_uT, b_W], writes=[bPS[pb]])
                        S.op("act", lambda h, half_=half_, pb=pb: h.copy(tmA[:, half_ * 512:(half_ + 1) * 512], PS[pb][:, :]),
                             reads=[bPS[pb]], writes=[b_tmA])
                    S.dma("sp", out_rows(o_dv_p, o_dv_s, tb, 1024), tmA[:, 0:1024], reads=[b_tmA])
                    S.op("dve", lambda h: h.tensor_copy(tbf[:, 0:1024], tmA[:, 0:1024]), reads=[b_tmA], writes=[b_tbf])
                    if prompt:
                        S.dma("sp", Vd_p[tokc:tokc + 128, :], tbf[:, 0:1024], reads=[b_tbf])
                    else:
                        for e in range(2):
                            S.dma("sp", Vd_s[2 * tb + e, PAST:PAST + 64, :], tbf[e * 64:(e + 1) * 64, 0:1024], reads=[b_tbf])
                    for half_ in range(2):
                        pb = 4 + half_
                        S.op("pe", tm_matmul(pb, tb, half_ * 512, 512, ckvT, 2, Wuv, None),
                             reads=[b_wu, b_ckvT, b_W], writes=[bPS[pb]])
                        S.op("act", lambda h, half_=half_, pb=pb: h.copy(tbf[:, half_ * 512:(half_ + 1) * 512], PS[pb][:, :]),
                             reads=[bPS[pb]], writes=[b_tbf])
                    if prompt:
                        S.dma("sp", Vm_p[tokc:tokc + 128, :], tbf[:, 0:1024], reads=[b_tbf])
                    else:
                        for e in range(2):
                            S.dma("sp", Vm_s[2 * tb + e, PAST:PAST + 64, :], tbf[e * 64:(e + 1) * 64, 0:1024], reads=[b_tbf])

                for oc in range(8):
                    pb = 1 + oc % 2

                    def mmk(h, oc=oc, pb=pb):
                        ins = None
                        for c in range(2):
                            ins = h.matmul(PS[pb][:, :N], Wuk[:, c, oc * 128:(oc + 1) * 128], ckvT[:, c, :N], start=(c == 0), stop=(c == 1))
                        return ins
                    S.op("pe", mmk, reads=[b_wu, b_ckvT, b_W], writes=[bPS[pb]])
                    t_ = tmp[oc % 2].bitcast(BF16)
                    S.op("act", lambda h, pb=pb, t_=t_: h.copy(t_[:, :N], PS[pb][:, :N]), reads=[bPS[pb]], writes=[b_tmp[oc % 2]])
                    if prompt:
                        S.dma("sp", KTm_p[oc * 128:(oc + 1) * 128, col0:col0 + N], t_[:, :N], reads=[b_tmp[oc % 2]])
                    else:
                        for s_ in range(NS):
                            S.dma("sp", KTm_s[s_, oc * 128:(oc + 1) * 128, PAST:PAST + 64], t_[:, s_ * 64:(s_ + 1) * 64], reads=[b_tmp[oc % 2]])
                cqf = xT
                for c in range(4):
                    pb = 1 + c % 2

                    def mmq(h, c=c, pb=pb):
                        ins = None
                        for k_ in range(8):
                            ins = h.matmul(PS[pb][:, :N], Win[:, k_, C_CQ + c * 128:C_CQ + (c + 1) * 128], uT[:, k_, :N],
                                           start=(k_ == 0), stop=(k_ == 7))
                        return ins
                    S.op("pe", mmq, reads=b_win + [b_uT, b_W], writes=[bPS[pb]])
                    S.op("act", lambda h, c=c, pb=pb: h.copy(cqf[:, c, :N], PS[pb][:, :N]), reads=[bPS[pb]], writes=[b_xT])
                    S.op("act", lambda h, c=c, pb=pb: h.activation(sq[c % 2][:, :N], PS[pb][:, :N], AF.Square),
                         reads=[bPS[pb]], writes=[b_sq[c % 2]])
                    S.op("pe", lambda h, c=c: h.matmul(PS[0][:, :N], ones_bf[:], sq[c % 2][:, :N], start=(c == 0), stop=(c == 3)),
                         reads=[b_sq[c % 2], b_const], writes=[bPS[0]])
                S.op("act", lambda h: h.activation(rs[:, :N], PS[0][:, :N], AF.Sqrt, bias=eps_t[:, 0:1], scale=1.0 / MLA_Q),
                     reads=[bPS[0], b_const], writes=[b_rs])
                S.op("dve", lambda h: h.reciprocal(rs[:, :N], rs[:, :N]), reads=[b_rs], writes=[b_rs])
                for c in range(4):
                    S.op("dve", lambda h, c=c: h.scalar_tensor_tensor(cqn[:, c, :N], cqf[:, c, :N], gq[:, c:c + 1], rs[:, :N],
                                                                      ALU.mult, ALU.mult),
                         reads=[b_xT, b_rs, b_tab], writes=[b_cqn])
                for tb in range(NTB):
                    blk = (col0 // 128 + tb) if prompt else NBP
                    tokc = col0 + tb * 128
                    for j in range(3):
                        pb = 4 + j
                        S.op("pe", tm_matmul(pb, tb, j * 512, 512, cqn, 4, Wuq, None), reads=[b_wu, b_cqn, b_W], writes=[bPS[pb]])
                        S.op("act", lambda h, j=j, pb=pb: h.activation(tmA[:, j * 512:(j + 1) * 512], PS[pb][:, :], AF.Copy, scale=MLA_SCALE),
                             reads=[bPS[pb]], writes=[b_tmA])
                    q3 = tmA[:, 0:1536].rearrange("p (h d) -> p h d", h=16)
                    qb3 = tbf[:, 0:1536].rearrange("p (h d) -> p h d", h=16)
                    S.op("act", lambda h: h.copy(qb3[:, :, 0:64], q3[:, :, 0:64]), reads=[b_tmA], writes=[b_tbf])
                    rope_tm(q3[:, :, 64:96], qb3[:, :, 64:96], 16, 16, cosM, sinM, blk, [b_tmA], [b_tbf])
                    for g in range(2):
                        S.op("pe", lambda h, g=g: [h.transpose(PS[6 + g].bitcast(BF16)[0:96, hh * 128:(hh + 1) * 128],
                                                               tbf[:, (g * 8 + hh) * 96:(g * 8 + hh + 1) * 96], ident_bf[:])
                                                   for hh in range(8)][-1], reads=[b_tbf, b_const], writes=[bPS[6 + g]])
                        evac_T(6 + g, 96, 8, [(QTm[g * 8:(g + 1) * 8, :, tokc:tokc + 128].rearrange("h p n -> p h n"), 0, 128)], None)
                for c in range(8):
                    def mmg(h, c=c):
                        ins = None
                        for k_ in range(8):
                            ins = h.matmul(PS[1][:, :N], Win[:, k_, C_CV + c * 128:C_CV + (c + 1) * 128], uT[:, k_, :N],
                                           start=(k_ == 0), stop=(k_ == 7))
                        return ins

                    def mmb(h, c=c):
                        ins = None
                        for k_ in range(8):
                            ins = h.matmul(PS[2][:, :N], Win[:, k_, C_CV + 1024 + c * 128:C_CV + 1024 + (c + 1) * 128], uT[:, k_, :N],
                                           start=(k_ == 0), stop=(k_ == 7))
                        return ins
                    S.op("pe", mmg, reads=b_win + [b_uT, b_W], writes=[bPS[1]])
                    S.op("pe", mmb, reads=b_win + [b_uT, b_W], writes=[bPS[2]])
                    t_, bt_ = tmp[c % 2], b_tmp[c % 2]
                    S.op("act", lambda h, t_=t_: h.activation(t_[:, :N], PS[2][:, :N], AF.Sigmoid), reads=[bPS[2]], writes=[bt_])
                    S.op("dve", lambda h, t_=t_: h.tensor_tensor(t_[:, :N], t_[:, :N], PS[1][:, :N], ALU.mult), reads=[bt_, bPS[1]], writes=[bt_])
                    S.dma("sp", GLUT[c * 128:(c + 1) * 128, col0:col0 + N], t_[:, :N], reads=[bt_])


            KG = min(512, PAST)
            NKB = KG // 128
            cdk = [aview(0, [1024], BF16), aview(2048, [1024], BF16)]
            cdv = [aview(4096, [1024], BF16), aview(6144, [1024], BF16)]
            cck = [aview(8192, [288], BF16), aview(8192 + 576, [288], BF16)]
            ckvTc = aview(16384, [2, 512], BF16)
            b_cdk, b_cdv, b_cck = [Buf(), Buf()], [Buf(), Buf()], [Buf(), Buf()]
            b_ckc = Buf()
            cnt = [0]

            def prep_group(s_, g0):
                for kb in range(NKB):
                    k0 = g0 + kb * 128
                    i = cnt[0] % 2
                    cnt[0] += 1
                    S.dma("pool", cdk[i], c_dk[l, s_, k0:k0 + 128, :], writes=[b_cdk[i]])
                    S.dma("pool", cdv[i], c_dv[l, s_, k0:k0 + 128, :], writes=[b_cdv[i]])
                    S.dma("pool", cck[i][:, 0:256], c_ckv[l, s_, k0:k0 + 128, :], writes=[b_cck[i]])
                    S.dma("pool", cck[i][:, 256:288], c_kpe[l, s_, k0:k0 + 128, :], writes=[b_cck[i]])
                    S.dma("sp", Vd_s[s_, k0:k0 + 128, :], cdv[i], reads=[b_cdv[i]])
                    S.op("pe", lambda h, i=i: [h.transpose(PS[6].bitcast(BF16)[:, c * 128:(c + 1) * 128], cdk[i][:, c * 128:(c + 1) * 128], ident_bf[:])
                                               for c in range(8)][-1], reads=[b_cdk[i], b_const], writes=[bPS[6]])
                    evac_T(6, 128, 8, [(KTd_s[s_, :, :, k0:k0 + 128].rearrange("h p n -> p h n"), 0, 128)], None)
                    S.op("pe", lambda h, i=i: [h.transpose(PS[2].bitcast(BF16)[:, c * 128:(c + 1) * 128], cck[i][:, c * 128:(c + 1) * 128], ident_bf[:])
                                               for c in range(2)][-1], reads=[b_cck[i], b_const], writes=[bPS[2]])
                    S.op("dve", lambda h, kb=kb: h.tensor_copy(ckvTc[:, :, kb * 128:(kb + 1) * 128],
                                                                PS[2].bitcast(BF16)[:, 0:256].rearrange("p (c n) -> p c n", c=2)),
                         reads=[bPS[2]], writes=[b_ckc])
                    S.op("pe", lambda h, i=i: h.transpose(PS[3].bitcast(BF16)[0:32, 0:128], cck[i][:, 256:288], ident_bf[:]),
                         reads=[b_cck[i], b_const], writes=[bPS[3]])
                    evac_T(3, 32, 1, [(KPT_s[s_, :, k0:k0 + 128].rearrange("r (g n) -> r g n", g=1), 0, 128)], None)
                    for half_ in range(2):
                        pb = 4 + half_
                        S.op("pe", tm_matmul(pb, kb, half_ * 512, 512, ckvTc, 2, Wuv, None), reads=[b_wu, b_ckc, b_W], writes=[bPS[pb]])
                        S.op("act", lambda h, half_=half_, pb=pb: h.copy(tbf[:, half_ * 512:(half_ + 1) * 512], PS[pb][:, :]),
                             reads=[bPS[pb]], writes=[b_tbf])
                    S.dma("sp", Vm_s[s_, k0:k0 + 128, :], tbf[:, 0:1024], reads=[b_tbf])
                for oc in range(8):
                    pb = 1 if oc % 2 == 0 else 7

                    def mmk(h, oc=oc, pb=pb):
                        ins = None
                        for c in range(2):
                            ins = h.matmul(PS[pb][:, :KG], Wuk[:, c, oc * 128:(oc + 1) * 128], ckvTc[:, c, :KG], start=(c == 0), stop=(c == 1))
                        return ins
                    S.op("pe", mmk, reads=[b_wu, b_ckc, b_W], writes=[bPS[pb]])
                    t_ = tmp[oc % 2].bitcast(BF16)
                    S.op("act", lambda h, pb=pb, t_=t_: h.copy(t_[:, :KG], PS[pb][:, :KG]), reads=[bPS[pb]], writes=[b_tmp[oc % 2]])
                    S.dma("sp", KTm_s[s_, oc * 128:(oc + 1) * 128, g0:g0 + KG], t_[:, :KG], reads=[b_tmp[oc % 2]])

            for s_ in range(NS):
                for g0 in range(0, PAST, KG):
                    prep_group(s_, g0)
            S.barrier()

            for tile in tiles:
                tile_B(tile)
            S.barrier()


        def attn_phase(l):
            lam_init = 0.8 - 0.6 * math.exp(-0.3 * l)
            prm = aview(0, [4, 64], F32)
            pr2 = aview(1024, [16], F32)
            gsub = aview(1100, [1], F32)
            b_prm = Buf()
            for i, nm in enumerate(("diff_lq1", "diff_lk1", "diff_lq2", "diff_lk2")):
                S.dma("sp", prm[:, i, :], Wd[nm][l].partition_broadcast(128), writes=[b_prm], war=[b_A])
            load_fm(gsub[:, 0:1], Wd["diff_subln_g"][l].rearrange("(o p) -> o p", o=1), 1, b_prm)
            S.op("dve", lambda h: h.tensor_tensor(prm[:, 0, :], prm[:, 0, :], prm[:, 1, :], ALU.mult), reads=[b_prm], writes=[b_prm])
            S.op("dve", lambda h: h.tensor_tensor(prm[:, 2, :], prm[:, 2, :], prm[:, 3, :], ALU.mult), reads=[b_prm], writes=[b_prm])
            S.op("dve", lambda h: h.reduce_sum(pr2[:, 0:1], prm[:, 0, :], AX.X), reads=[b_prm], writes=[b_prm])
            S.op("dve", lambda h: h.reduce_sum(pr2[:, 1:2], prm[:, 2, :], AX.X), reads=[b_prm], writes=[b_prm])
            S.op("act", lambda h: h.activation(pr2[:, 2:4], pr2[:, 0:2], AF.Exp), reads=[b_prm], writes=[b_prm])
            S.op("dve", lambda h: h.scalar_tensor_tensor(pr2[:, 4:5], pr2[:, 3:4], -lam_init, pr2[:, 2:3], ALU.add, ALU.subtract),
                 reads=[b_prm], writes=[b_prm])
            S.op("dve", lambda h: h.tensor_scalar_mul(gsub[:, 0:1], gsub[:, 0:1], 1.0 - lam_init), reads=[b_prm], writes=[b_prm])
            neg_lam = pr2[:, 4:5]

            KMAX = max(T, KS)
            NKT = (KMAX + 127) // 128
            def hset(i):
                base = i * 16640
                kt_ = WREG[:, base: base + 4224]
                qt_ = WREG[:, base + 4224: base + 4224 + 4096]
                v_ = WREG[:, base + 8320: base + 8320 + 4224]
                qb_ = WREG[:, base + 12544: base + 12544 + 4096]
                return kt_, qt_, v_, qb_
            hs = [hset(0), hset(1)]
            b_hs = [[Buf() for _ in range(6)], [Buf() for _ in range(6)]]
            PT = [aview(2048 + i * 1024, [512], BF16) for i in range(4)] + [aview(25600 + i * 1024, [512], BF16) for i in range(4)]
            b_PT = [Buf() for _ in range(8)]
            acc = [aview(6144, [512], F32), aview(8192, [512], F32)]
            b_acc = [Buf(), Buf()]
            accB = [aview(21504, [512], F32), aview(23552, [512], F32)]
            b_accB = [Buf(), Buf()]
            tt = [aview(10240 + i * 2048, [512], F32) for i in range(4)]
            b_tt = [Buf() for _ in range(4)]
            sqd = aview(18432, [512], BF16)
            b_sqd = Buf()
            ost = [aview(19456, [512], BF16), aview(20480, [512], BF16)]
            b_ost = [Buf(), Buf()]
            ctr = dict(pt=0, sb=0, os=0, hs=0)

            seqs = [dict(q0=0, nq=T, QB=min(512, T), K=T, causal=True,
                         KTd=KTd_p, Vd=Vd_p, KTm=KTm_p, KPT=KPT_p, Vm=Vm_p)]
            for s_ in range(NS):
                seqs.append(dict(q0=T + s_ * TS, nq=TS, QB=TS, K=KS, causal=False,
                                 KTd=KTd_s[s_], Vd=Vd_s[s_], KTm=KTm_s[s_], KPT=KPT_s[s_], Vm=Vm_s[s_]))

            def load_head(kind, sq_, hd, i):
                kt_, qt_, v_, qb_ = hs[i]
                K, nq, q0 = sq_["K"], sq_["nq"], sq_["q0"]
                nfull = K // 128
                rem = K - nfull * 128
                if kind == "mla":
                    S.dma("sp", kt_[0:64, 0:K], sq_["KTm"][hd * 64:(hd + 1) * 64, :], writes=[b_hs[i][0]], war=[b_W])
                    S.dma("sp", kt_[64:96, 0:K], sq_["KPT"][:, :], writes=[b_hs[i][1]], war=[b_W])
                    S.dma("sp", qt_[0:96, 0:nq], QTm[hd, :, q0:q0 + nq], writes=[b_hs[i][2]], war=[b_W])
                    vc0 = (hd // 2) * 128
                    Vsrc = sq_["Vm"]
                else:
                    S.dma("sp", kt_[:, 0:K], sq_["KTd"][hd, :, :], writes=[b_hs[i][0], b_hs[i][1]], war=[b_W])
                    S.op("pool", lambda h: h.memset(qt_[64:128, 0:nq], 0.0), writes=[b_hs[i][2]], war=[b_W])
                    S.op("pool", lambda h: h.memset(qb_[0:64, 0:nq], 0.0), writes=[b_hs[i][5]], war=[b_W])
                    S.dma("sp", qt_[0:64, 0:nq], QTd[hd, 0:64, q0:q0 + nq], writes=[b_hs[i][2]], war=[b_W])
                    S.dma("sp", qb_[64:128, 0:nq], QTd[hd, 64:128, q0:q0 + nq], writes=[b_hs[i][5]], war=[b_W])
                    vc0 = hd * 128
                    Vsrc = sq_["Vd"]
                v3 = v_[:, 0:NKT * 128].rearrange("p (t d) -> p t d", d=128)
                S.dma("sp", v3[:, 0:nfull, :], Vsrc[0:nfull * 128, vc0:vc0 + 128].rearrange("(t p) d -> p t d", p=128),
                      writes=[b_hs[i][3]], war=[b_W])
                if rem:
                    S.dma("sp", v3[0:rem, nfull, :], Vsrc[nfull * 128:K, vc0:vc0 + 128], writes=[b_hs[i][4]], war=[b_W])

            def attend_block(kind, sq_, hd, i, qb, maps):
                kt_, qt_, v_, qb_ = hs[i]
                K, QB = sq_["K"], sq_["QB"]
                dv = 128
                po = (hd % 2) * 64 if kind == "mla" else 0
                pn = 64 if kind == "mla" else 128
                v3 = v_[:, 0:NKT * 128].rearrange("p (t d) -> p t d", d=128)
                qc0 = qb * QB
                nq = QB
                if sq_["causal"]:
                    nkt = (qc0 + QB) // 128
                    diag0 = qc0 // 128
                else:
                    nkt = (K + 127) // 128
                    diag0 = nkt + 1
                for m in maps:
                    if kind == "mla":
                        r0, r1 = 0, 96
                        qsrc = qt_
                    else:
                        r0, r1 = 0, 128
                        qsrc = qt_ if m == 0 else qb_
                    ob, sb_, ac, bac = 2 + m, 4 + m, acc[m], b_acc[m]
                    ac2, bac2 = accB[m], b_accB[m]
                    SK = 2
                    stageA, stageB, stageC = [], [], []
                    for kt in range(nkt):
                        nk = min(128, K - kt * 128)
                        c_lo = (kt - diag0) * 128 if kt >= diag0 else 0
                        spb = (0, 1, 7)[ctr["sb"] % 3]
                        ctr["sb"] += 1
                        pi = ctr["pt"] % 8
                        ctr["pt"] += 1
                        pt_, bpt = PT[pi], b_PT[pi]

                        def fA(kt=kt, nk=nk, c_lo=c_lo, spb=spb, r0=r0, r1=r1, qsrc=qsrc):
                            S.op("pe", lambda h: h.matmul(
                                PS[spb][0:nk, c_lo:nq], kt_[r0:r1, kt * 128:kt * 128 + nk], qsrc[r0:r1, qc0 + c_lo:qc0 + nq],
                                start=True, stop=True), reads=b_hs[i], writes=[bPS[spb]])

                        def fB(kt=kt, nk=nk, c_lo=c_lo, spb=spb, pt_=pt_, bpt=bpt, ac=ac, bac=bac, ac2=ac2, bac2=bac2):
                            S.op("act", lambda h: h.activation(pt_[0:nk, c_lo:nq], PS[spb][0:nk, c_lo:nq], AF.Exp),
                                 reads=[bPS[spb]], writes=[bpt])
                            if kt >= diag0 and nk == 128:
                                S.op("dve", lambda h: h.memset(pt_[64:128, c_lo:c_lo + 64], 0.0), reads=[bpt], writes=[bpt])
                            if kt == 0:
                                S.op("dve", lambda h: h.tensor_copy(ac[0:nk, 0:nq], pt_[0:nk, 0:nq]), reads=[bpt], writes=[bac])
                            elif kt == 1:
                                if c_lo > 0:
                                    S.op("pool", lambda h: h.memset(ac2[:, 0:c_lo], 0.0), writes=[bac2])
                                if nk < 128:
                                    S.op("pool", lambda h: h.memset(ac2[:, 0:nq], 0.0), writes=[bac2])
                                S.op("pool", lambda h: h.tensor_copy(ac2[0:nk, c_lo:nq], pt_[0:nk, c_lo:nq]), reads=[bpt], writes=[bac2])
                            elif kt % 2 == 0:
                                S.op("dve", lambda h: h.tensor_tensor(ac[0:nk, c_lo:nq], ac[0:nk, c_lo:nq], pt_[0:nk, c_lo:nq], ALU.add),
                                     reads=[bpt, bac], writes=[bac])
                            else:
                                S.op("pool", lambda h: h.tensor_tensor(ac2[0:nk, c_lo:nq], ac2[0:nk, c_lo:nq], pt_[0:nk, c_lo:nq], ALU.add),
                                     reads=[bpt, bac2], writes=[bac2])

                        def fC(kt=kt, nk=nk, c_lo=c_lo, pt_=pt_, bpt=bpt, ob=ob):
                            S.op("pe", lambda h: h.matmul(
                                PS[ob][0:dv, c_lo:nq], v3[0:nk, kt, :], pt_[0:nk, c_lo:nq], start=(kt == 0), stop=(kt == nkt - 1)),
                                reads=b_hs[i] + [bpt], writes=[bPS[ob]])
                        stageA.append(fA)
                        stageB.append(fB)
                        stageC.append(fC)
                    for step in range(nkt + SK):
                        if step < nkt:
                            stageA[step]()
                            stageB[step]()
                        if step - SK >= 0:
                            stageC[step - SK]()
                    if nkt > 1:
                        S.op("pe", lambda h, ac=ac, ac2=ac2, sb_=sb_: [
                            h.matmul(PS[sb_][0:dv, 0:nq], ones_f[:, 0:dv], ac[:, 0:nq], start=True, stop=False),
                            h.matmul(PS[sb_][0:dv, 0:nq], ones_f[:, 0:dv], ac2[:, 0:nq], start=False, stop=True)][-1],
                            reads=[bac, bac2, b_const], writes=[bPS[sb_]])
                    else:
                        S.op("pe", lambda h, ac=ac, sb_=sb_: h.matmul(PS[sb_][0:dv, 0:nq], ones_f[:, 0:dv], ac[:, 0:nq], start=True, stop=True),
                             reads=[bac, b_const], writes=[bPS[sb_]])
                    S.op("act", lambda h, m=m, sb_=sb_: h.activation(tt[m][po:po + pn, 0:nq], PS[sb_][po:po + pn, 0:nq], AF.Ln),
                         reads=[bPS[sb_]], writes=[b_tt[m]])
                    S.op("act", lambda h, m=m: h.activation(tt[m][po:po + pn, 0:nq], tt[m][po:po + pn, 0:nq], AF.Exp, scale=-1.0),
                         reads=[b_tt[m]], writes=[b_tt[m]])
                    S.op("dve", lambda h, m=m, ob=ob: h.tensor_tensor(tt[m][po:po + pn, 0:nq], tt[m][po:po + pn, 0:nq], PS[ob][po:po + pn, 0:nq], ALU.mult),
                         reads=[bPS[ob], b_tt[m]], writes=[b_tt[m]])
                oi = ctr["os"] % 2
                ctr["os"] += 1
                o_, bo_ = ost[oi], b_ost[oi]
                gq0 = sq_["q0"] + qc0
                if kind == "mla":
                    S.op("act", lambda h, o_=o_: h.copy(o_[po:po + 64, 0:nq], tt[0][po:po + 64, 0:nq]), reads=[b_tt[0]], writes=[bo_])
                    S.dma("sp", AOm[hd * 64:(hd + 1) * 64, gq0:gq0 + nq], o_[po:po + 64, 0:nq], reads=[bo_])
                else:
                    S.op("dve", lambda h: h.scalar_tensor_tensor(tt[2][:, 0:nq], tt[1][:, 0:nq], neg_lam, tt[0][:, 0:nq], ALU.mult, ALU.add),
                         reads=[b_tt[0], b_tt[1], b_prm], writes=[b_tt[2]])
                    S.op("act", lambda h: h.activation(sqd[:, 0:nq], tt[2][:, 0:nq], AF.Square), reads=[b_tt[2]], writes=[b_sqd])
                    S.op("pe", lambda h: h.matmul(PS[6][:, 0:nq], ones_bf[:], sqd[:, 0:nq], start=True, stop=True),
                         reads=[b_sqd, b_const], writes=[bPS[6]])
                    S.op("act", lambda h: h.activation(tt[3][:, 0:nq], PS[6][:, 0:nq], AF.Sqrt, bias=eps_t[:, 0:1], scale=1.0 / 128),
                         reads=[bPS[6], b_const], writes=[b_tt[3]])
                    S.op("dve", lambda h: h.reciprocal(tt[3][:, 0:nq], tt[3][:, 0:nq]), reads=[b_tt[3]], writes=[b_tt[3]])
                    S.op("dve", lambda h, o_=o_: h.scalar_tensor_tensor(o_[:, 0:nq], tt[2][:, 0:nq], gsub[:, 0:1], tt[3][:, 0:nq], ALU.mult, ALU.mult),
                         reads=[b_tt[2], b_tt[3], b_prm], writes=[bo_])
                    S.dma("sp", AOd[hd * 128:(hd + 1) * 128, gq0:gq0 + nq], o_[:, 0:nq], reads=[bo_])

            work = []
            for sq_ in seqs:
                for hd in range(16):
                    work.append(("mla", sq_, hd))
                for hd in range(8):
                    work.append(("diff", sq_, hd))
            for wi, (kind, sq_, hd) in enumerate(work):
                if wi == 0:
                    load_head(kind, sq_, hd, 0)
                if wi + 1 < len(work):
                    k2, s2, h2 = work[wi + 1]
                    load_head(k2, s2, h2, (wi + 1) % 2)
                for qb in range(sq_["nq"] // sq_["QB"]):
                    attend_block(kind, sq_, hd, wi % 2, qb, [0] if kind == "mla" else [0, 1])
            S.barrier()


        def mixC_phase(l):
            Wg_ = wview(0, [8, 3072])
            Wmo = wview(24576, [8, 1024])
            Wdo = wview(32768, [8, 1024])
            Wpw = wview(40960, [8, 1024])
            Wou = wview(49152, [8, 1024])
            yb = wview(57344, [8, 512])
            mrg = wview(61440, [8, 512])
            PB0 = 65536 * 2

            def wpar(off, shape):
                n = int(np.prod(shape))
                a = WREG[:, (PB0 + off) // 2:(PB0 + off) // 2 + n * 2].bitcast(F32)
                if len(shape) == 2:
                    a = a.rearrange("p (a b) -> p a b", a=shape[0])
                return a
            wdw = wpar(0, [8, 32])
            bdw = wpar(1024, [8])
            lng = wpar(1056, [8])
            lnb = wpar(1088, [8])
            bpw = wpar(1120, [8])
            bgt = wpar(1152, [24])
            b_wm = [Buf() for _ in range(5)]
            b_par = Buf()
            for c in range(8):
                S.dma("pool", Wg_[:, c, :], Wd["w_branch_gate"][l][c * 128:(c + 1) * 128, :], writes=[b_wm[0]], war=[b_W])
            for wi_, (wv_, nm) in enumerate(((Wmo, "mla_w_o"), (Wdo, "diff_w_o"), (Wpw, "conv_w_pw2"), (Wou, "w_out"))):
                S.dma("pool", wv_, Wd[nm][l].rearrange("(c p) m -> p c m", p=128), writes=[b_wm[1 + wi_]], war=[b_W])
            for c in range(8):
                load_fm(wdw[:, c, 0:31], Wd["conv_w_dw"][l][:, c * 128:(c + 1) * 128], 31, b_par)
            load_fm(bdw, Wd["conv_b_dw"][l].rearrange("(c p) -> c p", p=128), 8, b_par)
            load_fm(lng, Wd["conv_ln_g"][l].rearrange("(c p) -> c p", p=128), 8, b_par)
            load_fm(lnb, Wd["conv_ln_b"][l].rearrange("(c p) -> c p", p=128), 8, b_par)
            load_fm(bpw, Wd["conv_b_pw2"][l].rearrange("(c p) -> c p", p=128), 8, b_par)
            load_fm(bgt, Wd["b_branch_gate"][l].rearrange("(c p) -> c p", p=128), 24, b_par)
            S.barrier()

            xT = aview(0, [8, 512], F32)
            uT = aview(16384, [8, 512], BF16)
            sq = [aview(24576, [512], BF16), aview(25600, [512], BF16)]
            rs = aview(26624, [512], F32)
            tmp = [aview(28672, [512], F32), aview(30720, [512], F32)]
            aom = aview(32768, [8, 512], BF16)
            aod = aview(40960, [8, 512], BF16)
            xin = [aview(49152, [544], F32), aview(49152 + 2176, [544], F32)]
            gt3 = aview(53504, [3, 512], BF16)
            xr = [aview(56576, [512], F32), aview(58624, [512], F32)]
            cst = aview(32768, [D], F32)
            b_xT, b_uT, b_rs, b_aom, b_aod, b_gt3, b_yb, b_mrg, b_cst = [Buf() for _ in range(9)]
            b_sq, b_tmp, b_xin, b_xr = [Buf(), Buf()], [Buf(), Buf()], [Buf(), Buf()], [Buf(), Buf()]
            b_cv = [Buf() for _ in range(8)]

            def tile_C(tile):
                N, col0, prompt = tile["n"], tile["col0"], tile["prompt"]
                segs = tile["segs"]
                nseg = len(segs)
                sl = segs[0][2]
                last_prompt = prompt and (col0 + N == T)
                S.dma("sp", xT[:, :, :N], XTv[:, :, col0:col0 + N], reads=[b_XT], writes=[b_xT], war=[b_A])
                norm_mod(tile, l, 1, xT, uT, b_xT, b_uT, sq, b_sq, rs, b_rs, tmp, b_tmp, 0)
                for c in range(8):
                    xi, bxi = xin[c % 2], b_xin[c % 2]
                    x3 = xi[:, 0:nseg * (CST + sl)].rearrange("p (s n) -> p s n", s=nseg)
                    if prompt:
                        if col0 == 0:
                            S.op("dve", lambda h, x3=x3: h.memset(x3[:, 0, 0:CST], 0.0), writes=[bxi])
                            S.dma("sp", x3[:, 0, CST:CST + N], GLUT[c * 128:(c + 1) * 128, 0:N], writes=[bxi])
                        else:
                            S.dma("sp", x3[:, 0, 0:CST + N], GLUT[c * 128:(c + 1) * 128, col0 - CST:col0 + N], writes=[bxi])
                    else:
                        S.dma("sp", x3[:, :, CST:CST + sl], GLUT[c * 128:(c + 1) * 128, col0:col0 + N].rearrange("p (s n) -> p s n", s=nseg),
                              writes=[bxi])
                        for si in range(nseg):
                            S.dma("sp", cst[0:CST, c * 128:(c + 1) * 128], c_conv[l, si, :, c * 128:(c + 1) * 128], writes=[b_cst])
                            S.op("pe", lambda h, c=c: h.transpose(PS[1][:, 0:CST], cst[0:CST, c * 128:(c + 1) * 128], ident_f[0:CST, 0:CST]),
                                 reads=[b_cst, b_const], writes=[bPS[1]])
                            S.op("act", lambda h, x3=x3, si=si: h.copy(x3[:, si, 0:CST], PS[1][:, 0:CST]), reads=[bPS[1]], writes=[bxi])
                    cv = xT[:, c, :N].rearrange("p (s n) -> p s n", s=nseg)
                    ceng = "dve"
                    S.op(ceng, lambda h, c=c, x3=x3, cv=cv: h.tensor_scalar(cv, x3[:, :, 0:sl], wdw[:, c, 0:1], bdw[:, c:c + 1], ALU.mult, ALU.add),
                         reads=[bxi, b_par], writes=[b_cv[c]], war=[b_xT])
                    for k_ in range(1, CONVW):
                        S.op(ceng, lambda h, c=c, k_=k_, x3=x3, cv=cv: h.scalar_tensor_tensor(
                            cv, x3[:, :, k_:k_ + sl], wdw[:, c, k_:k_ + 1], cv, ALU.mult, ALU.add),
                            reads=[bxi, b_par, b_cv[c]], writes=[b_cv[c]])
                    if last_prompt or not prompt:
                        for si in range(nseg):
                            S.op("pe", lambda h, x3=x3, si=si: h.transpose(PS[2][0:32, 0:128], x3[:, si, CST + sl - 32:CST + sl], ident_f[:]),
                                 reads=[bxi, b_const], writes=[bPS[2]])
                            t_, bt_ = tmp[si % 2], b_tmp[si % 2]
                            S.op("act", lambda h, t_=t_: h.copy(t_[0:32, 0:128], PS[2][0:32, 0:128]), reads=[bPS[2]], writes=[bt_])
                            dst = o_conv_p[l, :, c * 128:(c + 1) * 128] if prompt else o_conv_s[l, si, :, c * 128:(c + 1) * 128]
                            S.dma("sp", dst, t_[2:32, 0:128], reads=[bt_])
                    S.op("act", lambda h, c=c: h.copy(sq[0][:, :N], xT[:, c, :N]), reads=[b_cv[c]], writes=[b_sq[0]])
                    S.op("act", lambda h, c=c: h.activation(sq[1][:, :N], xT[:, c, :N], AF.Square), reads=[b_cv[c]], writes=[b_sq[1]])
                    S.op("pe", lambda h, c=c: h.matmul(PS[3][:, :N], ones_bf[:], sq[0][:, :N], start=(c == 0), stop=(c == 7)),
                         reads=[b_sq[0], b_const], writes=[bPS[3]])
                    S.op("pe", lambda h, c=c: h.matmul(PS[4][:, :N], ones_bf[:], sq[1][:, :N], start=(c == 0), stop=(c == 7)),
                         reads=[b_sq[1], b_const], writes=[bPS[4]])
                nmu, var_ = tmp[0], tmp[1]
                S.op("act", lambda h: h.activation(nmu[:, :N], PS[3][:, :N], AF.Copy, scale=-1.0 / D), reads=[bPS[3]], writes=[b_tmp[0]])
                S.op("dve", lambda h: h.tensor_tensor(var_[:, :N], nmu[:, :N], nmu[:, :N], ALU.mult), reads=[b_tmp[0]], writes=[b_tmp[1]])
                S.op("dve", lambda h: h.scalar_tensor_tensor(var_[:, :N], PS[4][:, :N], 1.0 / D, var_[:, :N], ALU.mult, ALU.subtract),
                     reads=[bPS[4], b_tmp[1]], writes=[b_tmp[1]])
                S.op("dve", lambda h: h.tensor_scalar_max(var_[:, :N], var_[:, :N], 0.0), reads=[b_tmp[1]], writes=[b_tmp[1]])
                S.op("act", lambda h: h.activation(rs[:, :N], var_[:, :N], AF.Sqrt, bias=eps_t[:, 0:1], scale=1.0),
                     reads=[b_tmp[1], b_const], writes=[b_rs])
                S.op("dve", lambda h: h.reciprocal(rs[:, :N], rs[:, :N]), reads=[b_rs], writes=[b_rs])
                for c in range(8):
                    S.op("dve", lambda h, c=c: h.tensor_tensor(xT[:, c, :N], xT[:, c, :N], nmu[:, :N], ALU.add), reads=[b_cv[c], b_tmp[0]], writes=[b_cv[c]])
                    S.op("dve", lambda h, c=c: h.tensor_tensor(xT[:, c, :N], xT[:, c, :N], rs[:, :N], ALU.mult), reads=[b_cv[c], b_rs], writes=[b_cv[c]])
                    S.op("act", lambda h, c=c: h.activation(yb[:, c, :N], xT[:, c, :N], AF.Silu, bias=lnb[:, c:c + 1], scale=lng[:, c:c + 1]),
                         reads=[b_cv[c], b_par], writes=[b_yb])
                S.dma("sp", aom[:, :, :N], AOm.rearrange("(c p) n -> p c n", p=128)[:, :, col0:col0 + N], writes=[b_aom])
                S.dma("sp", aod[:, :, :N], AOd.rearrange("(c p) n -> p c n", p=128)[:, :, col0:col0 + N], writes=[b_aod])
                for oc in range(8):
                    for gi in range(3):
                        def mmgate(h, gi=gi, oc=oc):
                            ins = None
                            for k_ in range(8):
                                ins = h.matmul(PS[1 + gi][:, :N], Wg_[:, k_, (gi * 8 + oc) * 128:(gi * 8 + oc + 1) * 128], uT[:, k_, :N],
                                               start=(k_ == 0), stop=(k_ == 7))
                            return ins
                        S.op("pe", mmgate, reads=[b_wm[0], b_uT, b_W], writes=[bPS[1 + gi]])
                        S.op("act", lambda h, gi=gi, oc=oc: h.activation(gt3[:, gi, :N], PS[1 + gi][:, :N], AF.Sigmoid,
                                                                         bias=bgt[:, gi * 8 + oc:gi * 8 + oc + 1], scale=1.0),
                             reads=[bPS[1 + gi], b_par], writes=[b_gt3])
                    for bi_, (wv_, src, bsrc) in enumerate(((Wmo, aom, b_aom), (Wdo, aod, b_aod), (Wpw, yb, b_yb))):
                        def mmbr(h, bi_=bi_, wv_=wv_, src=src, oc=oc):
                            ins = None
                            for k_ in range(8):
                                ins = h.matmul(PS[4 + bi_][:, :N], wv_[:, k_, oc * 128:(oc + 1) * 128], src[:, k_, :N],
                                               start=(k_ == 0), stop=(k_ == 7))
                            return ins
                        S.op("pe", mmbr, reads=[b_wm[1 + bi_], bsrc, b_W], writes=[bPS[4 + bi_]])
                    t0, t1 = tmp[0], tmp[1]
                    S.op("dve", lambda h: h.tensor_tensor(t0[:, :N], gt3[:, 0, :N], PS[4][:, :N], ALU.mult), reads=[b_gt3, bPS[4]], writes=[b_tmp[0]])
                    S.op("dve", lambda h: h.tensor_tensor(t1[:, :N], gt3[:, 1, :N], PS[5][:, :N], ALU.mult), reads=[b_gt3, bPS[5]], writes=[b_tmp[1]])
                    S.op("dve", lambda h: h.tensor_tensor(t0[:, :N], t0[:, :N], t1[:, :N], ALU.add), reads=[b_tmp[0], b_tmp[1]], writes=[b_tmp[0]])
                    S.op("dve", lambda h, oc=oc: h.scalar_tensor_tensor(t1[:, :N], PS[6][:, :N], bpw[:, oc:oc + 1], gt3[:, 2, :N], ALU.add, ALU.mult),
                         reads=[bPS[6], b_par, b_gt3], writes=[b_tmp[1]])
                    S.op("dve", lambda h, oc=oc: h.tensor_tensor(mrg[:, oc, :N], t0[:, :N], t1[:, :N], ALU.add), reads=[b_tmp[0], b_tmp[1]], writes=[b_mrg])
                for oc in range(8):
                    py = 1 + (oc % 6)

                    def mmo(h, oc=oc, py=py):
                        ins = None
                        for k_ in range(8):
                            ins = h.matmul(PS[py][:, :N], Wou[:, k_, oc * 128:(oc + 1) * 128], mrg[:, k_, :N], start=(k_ == 0), stop=(k_ == 7))
                        return ins
                    S.op("pe", mmo, reads=[b_wm[4], b_mrg, b_W], writes=[bPS[py]])
                    S.op("act", lambda h, oc=oc, py=py: h.copy(xT[:, oc, :N], PS[py][:, :N]), reads=[bPS[py]], writes=[b_xT], war=b_cv)
                    S.op("act", lambda h, oc=oc, py=py: h.activation(sq[oc % 2][:, :N], PS[py][:, :N], AF.Square), reads=[bPS[py]], writes=[b_sq[oc % 2]])
                    S.op("pe", lambda h, oc=oc: h.matmul(PS[7][:, :N], ones_bf[:], sq[oc % 2][:, :N], start=(oc == 0), stop=(oc == 7)),
                         reads=[b_sq[oc % 2], b_const], writes=[bPS[7]])
                post_norm_residual(tile, l, 1, xT, b_xT, xr, b_xr, rs, b_rs, tmp, b_tmp, 7)

            for tile in tiles:
                tile_C(tile)
            S.barrier()

        def store_y_phase():
            xi = [aview(0, [8, 128], F32), aview(4096, [8, 128], F32)]
            b_xi = [Buf(), Buf()]
            xo = [aview(8192, [D], F32), aview(8192 + 4096, [D], F32)]
            b_xo = [Buf(), Buf()]
            blocks = [(y_p, i * 128, i * 128) for i in range(T // 128)] + \
                     [(y_s, i * 128, T + i * 128) for i in range(NSTOK // 128)]
            for bi, (dst, r0, col) in enumerate(blocks):
                xi_, bx = xi[bi % 2], b_xi[bi % 2]
                S.dma("sp", xi_, XTv[:, :, col:col + 128], reads=[b_XT], writes=[bx], war=[b_A])
                o, bo = xo[bi % 2], b_xo[bi % 2]
                for half in range(2):
                    pb = 2 * (bi % 2) + half
                    S.op("pe", lambda h, xi_=xi_, half=half, pb=pb: [
                        h.transpose(PS[pb][:, c * 128:(c + 1) * 128], xi_[:, half * 4 + c, :], ident_f[:])
                        for c in range(4)][-1], reads=[bx, b_const], writes=[bPS[pb]])
                    if half == 0:
                        S.op("act", lambda h, o=o, pb=pb: h.copy(o[:, 0:512], PS[pb][:, :]), reads=[bPS[pb]], writes=[bo])
                    else:
                        S.op("dve", lambda h, o=o, pb=pb: h.tensor_copy(o[:, 512:1024], PS[pb][:, :]), reads=[bPS[pb]], writes=[bo])
                S.dma("sp", dst[r0:r0 + 128, :], o, reads=[bo])

        ada_phase()
        S.barrier()
        load_x_phase()
        S.barrier()
        for l in range(L):
            if "ffn1" in cfg.phases:
                ffn_phase(l, 0)
            if "mixB" in cfg.phases or "mix" in cfg.phases:
                mixB_phase(l)
            if "attn" in cfg.phases or "mix" in cfg.phases:
                attn_phase(l)
            if "mixC" in cfg.phases or "mix" in cfg.phases:
                mixC_phase(l)
            if "ffn2" in cfg.phases:
                ffn_phase(l, 1)
        if cfg.dbg:
            S.dma("sp", dbg_out["dbg_xt"], XT, reads=[b_XT])
        S.barrier()
        store_y_phase()
        S.finish()
        S.emit()
        print("sched: ops=%d waits=%d" % (S.n_ops, S.n_wait))
    return nc


def make_in_maps(inputs, cfg, ncores):
    NS, T = cfg.NS, cfg.T
    maps = []
    for b in range(ncores):
        m = {}
        m["x_prompt"] = np.ascontiguousarray(inputs["x_prompt"][b])
        m["x_sample"] = np.ascontiguousarray(inputs["x_sample"][b * NS:(b + 1) * NS]).reshape(NS * TS, D)
        m["c_all"] = np.ascontiguousarray(
            np.concatenate([inputs["c_prompt"][b:b + 1], inputs["c_sample"][b * NS:(b + 1) * NS]], axis=0))
        m["cache_mla_ckv"] = np.ascontiguousarray(inputs["cache_mla_ckv"][:, b * NS:(b + 1) * NS])
        m["cache_mla_kpe"] = np.ascontiguousarray(inputs["cache_mla_kpe"][:, b * NS:(b + 1) * NS])
        m["cache_diff_k"] = np.ascontiguousarray(inputs["cache_diff_k"][:, b * NS:(b + 1) * NS]).reshape(
            DEPTH, NS, cfg.PAST, 1024)
        m["cache_diff_v"] = np.ascontiguousarray(inputs["cache_diff_v"][:, b * NS:(b + 1) * NS]).reshape(
            DEPTH, NS, cfg.PAST, 1024)
        m["state_conv"] = np.ascontiguousarray(inputs["state_conv"][:, b * NS:(b + 1) * NS])
        for name, _ in WEIGHT_SPECS:
            m[name] = np.ascontiguousarray(inputs[name])
        maps.append(m)
    return maps


def gather_outputs(results, cfg, ncores):
    NS, T, L = cfg.NS, cfg.T, DEPTH

    def stk(key, shape_fn=None, axis=0):
        return [r[key] for r in results]
    y_p = np.stack([r["y_prompt"] for r in results], 0)
    y_s = np.concatenate([r["y_sample"].reshape(NS, TS, D) for r in results], 0)
    ckv_p = np.stack([r["o_ckv_p"] for r in results], 1)
    kpe_p = np.stack([r["o_kpe_p"] for r in results], 1)
    dk_p = np.stack([r["o_dk_p"].reshape(L, T, DIFF_H, 2, DIFF_HD) for r in results], 1)
    dv_p = np.stack([r["o_dv_p"].reshape(L, T, DIFF_H, 128) for r in results], 1)
    conv_p = np.stack([r["o_conv_p"] for r in results], 1)
    ckv_s = np.concatenate([r["o_ckv_s"].reshape(L, NS, TS, MLA_KV) for r in results], 1)
    kpe_s = np.concatenate([r["o_kpe_s"].reshape(L, NS, TS, MLA_ROPE) for r in results], 1)
    dk_s = np.concatenate([r["o_dk_s"].reshape(L, NS, TS, DIFF_H, 2, DIFF_HD) for r in results], 1)
    dv_s = np.concatenate([r["o_dv_s"].reshape(L, NS, TS, DIFF_H, 128) for r in results], 1)
    conv_s = np.concatenate([r["o_conv_s"] for r in results], 1)
    return (y_p, y_s, ckv_p, kpe_p, dk_p, dv_p, conv_p, ckv_s, kpe_s, dk_s, dv_s, conv_s)


def kernel(**inputs):
    cfg = Cfg()
    ncores = 8
    nc = build_program(cfg)
    in_maps = make_in_maps(inputs, cfg, ncores)
    res = run_bass_kernel_spmd(nc, in_maps, core_ids=list(range(ncores)))
    outs = gather_outputs(res.results, cfg, ncores)
    return tuple(np.ascontiguousarray(o, dtype=np.float32) for o in outs)
```
